# Optimizing a Trainium2 kernel written in Bass

```python
import jax, jax.numpy as jnp
from jax import lax
import numpy as np

D_MODEL = 1024
BATCH = 8
SEQ = 2048
DEPTH = 4

GRID_W = 64
HEAD_DIM = 64
EPS = 1e-6
A_HEADS = D_MODEL // 256
A_PATTERNS = ((128, 1), (512, 4), (2048, 16))
A_ROT_DIM = HEAD_DIM // 4
ROPE_THETA = 500000.0
B_HEADS = D_MODEL // 256
B_DK = 128
B_DV = 128
B_CONV = 5
B_CHUNK = 64
C_Q_HEADS = D_MODEL // 256
C_KV_HEADS = C_Q_HEADS // 2
C_THETA = 10000.0
Q_BLOCK = 128
D_FF = (-(-(8 * D_MODEL) // (3 * 256))) * 256

SPLITS = (
    A_HEADS * HEAD_DIM, A_HEADS * HEAD_DIM, A_HEADS * HEAD_DIM,
    B_HEADS * B_DK, B_HEADS * B_DK, B_HEADS * B_DV, B_HEADS * B_DV,
    2 * B_HEADS, 2 * B_HEADS,
    C_Q_HEADS * HEAD_DIM, C_KV_HEADS * HEAD_DIM, C_KV_HEADS * HEAD_DIM,
)
IN_DIM = sum(SPLITS)
D_MIX = A_HEADS * HEAD_DIM + B_HEADS * B_DV + C_Q_HEADS * HEAD_DIM
B_QKV = 2 * B_HEADS * B_DK + B_HEADS * B_DV

kernel_name = 'hybrid_parallel_dilated_gdn_axialgqa_block'

F32 = jnp.float32


def rmsnorm(x, w):
    xf = x.astype(F32)
    y = xf * lax.rsqrt(jnp.mean(xf * xf, axis=-1, keepdims=True) + EPS)
    return (y * w.astype(F32)).astype(x.dtype)


def l2norm(x):
    return x * lax.rsqrt(jnp.sum(x * x, axis=-1, keepdims=True) + EPS)


def rope(x, pos, theta):
    half = x.shape[-1] // 2
    inv = jnp.float32(theta) ** (-jnp.arange(half, dtype=F32) / half)
    ang = pos.astype(F32)[:, None] * inv[None, :]
    cos = jnp.cos(ang)[None, :, None, :]
    sin = jnp.sin(ang)[None, :, None, :]
    xf = x.astype(F32)
    x1, x2 = xf[..., :half], xf[..., half:]
    return jnp.concatenate([x1 * cos - x2 * sin, x2 * cos + x1 * sin], axis=-1).astype(x.dtype)


def to_strided(t, d):
    b, s = t.shape[:2]
    rest = t.shape[2:]
    return t.reshape(b, s // d, d, *rest).swapaxes(1, 2).reshape(b * d, s // d, *rest)


def from_strided(t, d, b):
    n, L = t.shape[:2]
    rest = t.shape[2:]
    return t.reshape(b, d, L, *rest).swapaxes(1, 2).reshape(b, L * d, *rest)


def banded_attention(q, k, v, radius):
    n, L, h, dh = q.shape
    blk = radius
    nb = -(-L // blk)
    pad = nb * blk - L
    qb = jnp.pad(q, ((0, 0), (0, pad), (0, 0), (0, 0))).reshape(n, nb, blk, h, dh)
    padk = ((0, 0), (blk, pad + blk), (0, 0), (0, 0))
    kp = jnp.pad(k, padk).reshape(n, nb + 2, blk, h, dh)
    vp = jnp.pad(v, padk).reshape(n, nb + 2, blk, h, dh)
    kw = jnp.concatenate([kp[:, :-2], kp[:, 1:-1], kp[:, 2:]], axis=2)
    vw = jnp.concatenate([vp[:, :-2], vp[:, 1:-1], vp[:, 2:]], axis=2)
    qpos = jnp.arange(nb)[:, None] * blk + jnp.arange(blk)[None, :]
    kpos = (jnp.arange(nb)[:, None] - 1) * blk + jnp.arange(3 * blk)[None, :]
    dist = qpos[:, :, None] - kpos[:, None, :]
    kp_b = kpos[:, None, :]
    valid = ((jnp.abs(dist) <= radius) & (kp_b >= 0) & (kp_b < L)) | (dist == 0)
    sc = jnp.einsum('nbqhd,nbkhd->nbhqk', qb.astype(F32), kw.astype(F32)) * (dh ** -0.5)
    sc = jnp.where(valid[None, :, None], sc, -jnp.inf)
    m = jnp.max(sc, axis=-1, keepdims=True)
    p = jnp.exp(sc - m)
    l = jnp.sum(p, axis=-1, keepdims=True)
    o = jnp.einsum('nbhqk,nbkhd->nbqhd', p / l, vw.astype(F32)).reshape(n, nb * blk, h, dh)[:, :L]
    lse = (m + jnp.log(l))[..., 0].transpose(0, 1, 3, 2).reshape(n, nb * blk, h)[:, :L]
    return o, lse


def mixer_dilated(q, k, v, qn, kn, pos):
    b, s = q.shape[:2]
    q = rmsnorm(q.reshape(b, s, A_HEADS, HEAD_DIM), qn)
    k = rmsnorm(k.reshape(b, s, A_HEADS, HEAD_DIM), kn)
    v = v.reshape(b, s, A_HEADS, HEAD_DIM)
    q = jnp.concatenate([rope(q[..., :A_ROT_DIM], pos, ROPE_THETA), q[..., A_ROT_DIM:]], axis=-1)
    k = jnp.concatenate([rope(k[..., :A_ROT_DIM], pos, ROPE_THETA), k[..., A_ROT_DIM:]], axis=-1)
    outs, lses = [], []
    for window, dil in A_PATTERNS:
        radius = window // (2 * dil)
        o, lse = banded_attention(to_strided(q, dil), to_strided(k, dil), to_strided(v, dil), radius)
        outs.append(from_strided(o, dil, b))
        lses.append(from_strided(lse, dil, b))
    wts = jax.nn.softmax(jnp.stack(lses), axis=0)
    o = jnp.einsum('pbsh,pbshd->bshd', wts, jnp.stack(outs))
    return o.reshape(b, s, -1).astype(v.dtype)


def short_conv(x, w):
    kk, c = w.shape
    return lax.conv_general_dilated(
        x, w[:, None, :].astype(x.dtype), window_strides=(1,),
        padding=[(kk // 2, kk // 2)], dimension_numbers=('NWC', 'WIO', 'NWC'),
        feature_group_count=c)


def gated_delta_chunked(q, k, v, g, beta):
    b, s, h, dk = q.shape
    dv = v.shape[-1]
    c = B_CHUNK
    n = s // c
    to_c = lambda t: t.reshape(b, n, c, h, t.shape[-1]).transpose(1, 0, 3, 2, 4)
    q, k, v = to_c(q), to_c(k), to_c(v)
    g = g.reshape(b, n, c, h).transpose(1, 0, 3, 2)
    beta = beta.reshape(b, n, c, h).transpose(1, 0, 3, 2)
    gc = jnp.cumsum(g, axis=-1)
    tril = jnp.tril(jnp.ones((c, c), dtype=bool))
    decay = jnp.exp(jnp.where(tril, gc[..., :, None] - gc[..., None, :], -jnp.inf))
    kbeta = k * beta[..., None]
    eye = jnp.eye(c, dtype=F32)
    a_mat = jnp.einsum('nbhid,nbhjd->nbhij', kbeta, k) * decay * (1.0 - eye)
    rhs = jnp.concatenate([v * beta[..., None], kbeta * jnp.exp(gc)[..., None]], axis=-1)
    sol = lax.linalg.triangular_solve(a_mat + eye, rhs, left_side=True, lower=True)
    u, w = sol[..., :dv], sol[..., dv:]
    qk = jnp.einsum('nbhid,nbhjd->nbhij', q, k) * decay

    def step(state, xs):
        q_i, k_i, u_i, w_i, gc_i, qk_i = xs
        v_new = u_i - jnp.einsum('bhck,bhkv->bhcv', w_i, state)
        o = (jnp.einsum('bhck,bhkv->bhcv', q_i * jnp.exp(gc_i)[..., None], state)
             + jnp.einsum('bhij,bhjv->bhiv', qk_i, v_new))
        g_last = gc_i[..., -1:]
        state = (state * jnp.exp(g_last)[..., None]
                 + jnp.einsum('bhck,bhcv->bhkv', k_i * jnp.exp(g_last - gc_i)[..., None], v_new))
        return state, o

    state0 = jnp.zeros((b, h, dk, dv), F32)
    _, o = lax.scan(step, state0, (q, k, u, w, gc, qk))
    return o.transpose(1, 0, 3, 2, 4).reshape(b, s, h, dv)


def mixer_gdn(q, k, v, z, a, beta_in, conv_w, a_log, dt_bias, onorm):
    b, s = q.shape[:2]
    qkv = jax.nn.silu(short_conv(jnp.concatenate([q, k, v], axis=-1), conv_w)).astype(F32)
    q, k, v = jnp.split(qkv, [B_HEADS * B_DK, 2 * B_HEADS * B_DK], axis=-1)
    q = l2norm(q.reshape(b, s, B_HEADS, B_DK)) * (B_DK ** -0.5)
    k = l2norm(k.reshape(b, s, B_HEADS, B_DK))
    v = v.reshape(b, s, B_HEADS, B_DV)
    a = a.astype(F32).reshape(b, s, 2, B_HEADS)
    beta = jax.nn.sigmoid(beta_in.astype(F32).reshape(b, s, 2, B_HEADS))
    g = -jnp.exp(a_log.astype(F32)) * jax.nn.softplus(a + dt_bias.astype(F32))
    o_f = gated_delta_chunked(q, k, v, g[:, :, 0], beta[:, :, 0])
    flip = lambda t: jnp.flip(t, axis=1)
    o_b = flip(gated_delta_chunked(flip(q), flip(k), flip(v), flip(g[:, :, 1]), flip(beta[:, :, 1])))
    o = rmsnorm(o_f + o_b, onorm) * jax.nn.silu(z.astype(F32).reshape(b, s, B_HEADS, B_DV))
    return o.reshape(b, s, -1).astype(z.dtype)


def mixer_gqa(q, k, v, qn, kn, row_pos, col_pos):
    b, s = q.shape[:2]
    q = rmsnorm(q.reshape(b, s, C_Q_HEADS, HEAD_DIM), qn)
    k = rmsnorm(k.reshape(b, s, C_KV_HEADS, HEAD_DIM), kn)
    v = v.reshape(b, s, C_KV_HEADS, HEAD_DIM)
    half = HEAD_DIM // 2
    axial = lambda t: jnp.concatenate(
        [rope(t[..., :half], row_pos, C_THETA), rope(t[..., half:], col_pos, C_THETA)], axis=-1)
    q, k = axial(q), axial(k)
    grp = C_Q_HEADS // C_KV_HEADS
    nblk = s // Q_BLOCK
    qb = q.reshape(b, nblk, Q_BLOCK, C_KV_HEADS, grp, HEAD_DIM).transpose(1, 0, 2, 3, 4, 5).astype(F32)
    kf, vf = k.astype(F32), v.astype(F32)

    def block(qi):
        sc = jnp.einsum('bqhgd,bshd->bhgqs', qi, kf) * (HEAD_DIM ** -0.5)
        p = jax.nn.softmax(sc, axis=-1)
        return jnp.einsum('bhgqs,bshd->bqhgd', p, vf)

    o = lax.map(block, qb)
    return o.transpose(1, 0, 2, 3, 4, 5).reshape(b, s, -1).astype(v.dtype)


def setup_inputs(seed: int = 0) -> dict:
    key = jax.random.key(seed)
    ks = jax.random.split(key, 16)
    nrm = lambda k, shape, fan: jax.random.normal(k, shape, F32) * (fan ** -0.5)
    gain = lambda k, shape: 1.0 + 0.01 * jax.random.normal(k, shape, F32)
    dt = jnp.exp(jax.random.uniform(ks[7], (DEPTH, 2, B_HEADS), F32, np.log(1e-3), np.log(1e-1)))
    return {
        'x': jax.random.normal(ks[0], (BATCH, SEQ, D_MODEL), F32),
        'norm1': gain(ks[1], (DEPTH, D_MODEL)),
        'w_in': nrm(ks[2], (DEPTH, D_MODEL, IN_DIM), D_MODEL),
        'qn_a': gain(ks[3], (DEPTH, HEAD_DIM)),
        'kn_a': gain(ks[4], (DEPTH, HEAD_DIM)),
        'conv_b': nrm(ks[5], (DEPTH, B_CONV, B_QKV), B_CONV),
        'a_log_b': jnp.log(jax.random.uniform(ks[6], (DEPTH, 2, B_HEADS), F32, 1.0, 16.0)),
        'dt_bias_b': jnp.log(jnp.expm1(dt)),
        'onorm_b': gain(ks[8], (DEPTH, B_DV)),
        'qn_c': gain(ks[9], (DEPTH, HEAD_DIM)),
        'kn_c': gain(ks[10], (DEPTH, HEAD_DIM)),
        'w_out': nrm(ks[11], (DEPTH, D_MIX, D_MODEL), D_MIX),
        'norm2': gain(ks[12], (DEPTH, D_MODEL)),
        'w_gate_up': nrm(ks[13], (DEPTH, D_MODEL, 2 * D_FF), D_MODEL),
        'w_down': nrm(ks[14], (DEPTH, D_FF, D_MODEL), D_FF),
    }


def reference(x, norm1, w_in, qn_a, kn_a, conv_b, a_log_b, dt_bias_b, onorm_b, qn_c, kn_c,
              w_out, norm2, w_gate_up, w_down):
    s = x.shape[1]
    rows = s // GRID_W
    pos = jnp.arange(s)
    row_pos = jnp.repeat(jnp.arange(rows), GRID_W)
    col_pos = jnp.tile(jnp.arange(GRID_W), rows)
    cuts = np.cumsum(SPLITS)[:-1].tolist()
    for i in range(DEPTH):
        h = rmsnorm(x, norm1[i])
        (qa, ka, va, qb, kb, vb, zb, ab, bb, qc, kc, vc) = jnp.split(h @ w_in[i], cuts, axis=-1)
        o_a = mixer_dilated(qa, ka, va, qn_a[i], kn_a[i], pos)
        o_b = mixer_gdn(qb, kb, vb, zb, ab, bb, conv_b[i], a_log_b[i], dt_bias_b[i], onorm_b[i])
        o_c = mixer_gqa(qc, kc, vc, qn_c[i], kn_c[i], row_pos, col_pos)
        x = x + jnp.concatenate([o_a, o_b, o_c], axis=-1) @ w_out[i]
        gate, up = jnp.split(rmsnorm(x, norm2[i]) @ w_gate_up[i], 2, axis=-1)
        x = x + (jax.nn.silu(gate) * up) @ w_down[i]
    return x
```

```python
import contextlib
import numpy as np
import ml_dtypes
import concourse.bass as bass
import concourse.mybir as mybir
from concourse.bass_utils import run_bass_kernel_spmd

F32 = mybir.dt.float32
BF16 = mybir.dt.bfloat16
AF = mybir.ActivationFunctionType
ALU = mybir.AluOpType

L_FULL = 4
S = 2048
D = 1024
DFF = 2816
NFF = 22
IN_DIM = 3344
EPS = 1e-6
NEG = -30000.0


class Res:
    __slots__ = ("name", "w", "r")

    def __init__(self, name):
        self.name = name
        self.w = None
        self.r = {}


class FW:
    ENG = ("pe", "act", "dve", "pool", "sp")

    def __init__(self, nc, stack):
        self.nc = nc
        self.stack = stack
        self.e = {"pe": nc.tensor, "act": nc.scalar, "dve": nc.vector, "pool": nc.gpsimd, "sp": nc.sync}
        self.sems = {}
        self.cnt = {}
        self.waited = {k: {} for k in self.ENG}
        for k in self.ENG:
            self.sems[k] = stack.enter_context(nc.semaphore("s_" + k))
            self.cnt[k] = 0
        self.nres = 0
        self.ninst = {k: 0 for k in self.ENG}

    def sb(self, name, shape, dt, stack=None):
        self.nsb = getattr(self, "nsb", 0) + 1
        return (stack or self.stack).enter_context(self.nc.sbuf_tensor(f"{name}_{self.nsb}", list(shape), dt))

    def ps(self, name, shape, dt=F32):
        return self.stack.enter_context(self.nc.psum_tensor(name, list(shape), dt))

    def res(self, name=None):
        self.nres += 1
        return Res(name or f"r{self.nres}")

    def new_sem(self, name):
        s = self.stack.enter_context(self.nc.semaphore(name))
        self.sems[name] = s
        self.cnt[name] = 0
        return name

    def _deps(self, eng, reads, writes, waiter=None):
        waiter = waiter or eng
        deps = {}

        def need(sv):
            if sv is None:
                return
            s, v = sv
            if deps.get(s, 0) < v:
                deps[s] = v

        for r in reads:
            need(r.w)
        for w in writes:
            need(w.w)
            for s, v in w.r.items():
                need((s, v))
        out = {}
        for s, v in deps.items():
            if s == eng:
                if eng == "pe":
                    continue
                raw_max = max([r.w[1] for r in reads if r.w is not None and r.w[0] == eng], default=0)
                if raw_max == 0:
                    continue
                v = raw_max
            if self.waited[waiter].get(s, 0) >= v:
                continue
            out[s] = v
        return out

    def _emit_waits(self, eng, deps):
        for s, v in deps.items():
            self.e[eng].wait_ge(self.sems[s], v)
            self.waited[eng][s] = v
            self.ninst[eng] += 1

    def op(self, eng, fn, reads=(), writes=()):
        deps = self._deps(eng, reads, writes)
        self._emit_waits(eng, deps)
        inst = fn(self.e[eng])
        self.cnt[eng] += 1
        self.ninst[eng] += 1
        v = self.cnt[eng]
        inst.then_inc(self.sems[eng], 1)
        for r in reads:
            if r.r.get(eng, 0) < v:
                r.r[eng] = v
        for w in writes:
            w.w = (eng, v)
            w.r = {}
        return inst

    def dma(self, q, out, in_, reads=(), writes=(), sem=None, **kw):
        d2 = self._deps(None, reads, writes, waiter=q)
        self._emit_waits(q, d2)
        inst = self.e[q].dma_start(out=out, in_=in_, **kw)
        self.cnt[sem] += 16
        v = self.cnt[sem]
        inst.then_inc(self.sems[sem], 16)
        self.ninst[q] += 1
        for r in reads:
            if r.r.get(sem, 0) < v:
                r.r[sem] = v
        for w in writes:
            w.w = (sem, v)
            w.r = {}
        return inst

    def wait_all(self, eng, resources):
        d2 = self._deps(None, resources, (), waiter=eng)
        self._emit_waits(eng, d2)

    def barrier(self):
        snap = dict(self.cnt)
        for eng in self.ENG:
            d = {}
            for s, v in snap.items():
                if v > 0 and s != eng and self.waited[eng].get(s, 0) < v:
                    d[s] = v
            self._emit_waits(eng, d)

    def mm(self, out, lhsT, rhs, start=True, stop=True, reads=(), writes=()):
        return self.op("pe", lambda e: e.matmul(out, lhsT, rhs, start=start, stop=stop), reads, writes)

    def tr(self, out, in_, ident, reads=(), writes=()):
        return self.op("pe", lambda e: e.transpose(out, in_, ident), reads, writes)

    def act(self, out, in_, func, reads=(), writes=(), **kw):
        return self.op("act", lambda e: e.activation(out=out, in_=in_, func=func, **kw), reads, writes)

    def tt(self, eng, out, in0, in1, op, reads=(), writes=()):
        return self.op(eng, lambda e: e.tensor_tensor(out=out, in0=in0, in1=in1, op=op), reads, writes)

    def stt(self, out, in0, scalar, in1, op0, op1, reads=(), writes=()):
        return self.op("dve", lambda e: e.scalar_tensor_tensor(out=out, in0=in0, scalar=scalar, in1=in1,
                                                                op0=op0, op1=op1), reads, writes)

    def ts(self, eng, out, in0, s1, s2, op0, op1=None, reads=(), writes=()):
        if op1 is None:
            return self.op(eng, lambda e: e.tensor_scalar(out=out, in0=in0, scalar1=s1, scalar2=None, op0=op0),
                           reads, writes)
        return self.op(eng, lambda e: e.tensor_scalar(out=out, in0=in0, scalar1=s1, scalar2=s2, op0=op0, op1=op1),
                       reads, writes)

    def cp(self, eng, out, in_, reads=(), writes=()):
        if eng == "act":
            return self.act(out, in_, AF.Copy, reads, writes)
        return self.op(eng, lambda e: e.tensor_copy(out=out, in_=in_), reads, writes)


O_QA, O_KA, O_VA = 0, 256, 512
O_QB, O_KB, O_VB, O_ZB = 768, 1280, 1792, 2304
O_AB, O_BB = 2816, 2824
O_QC, O_KC, O_VC = 2832, 3088, 3216

P_QA, P_KA, P_VA, P_AB = 0, 256, 512, 768
P_QC, P_KC, P_VC = 784, 1040, 1168
P_B = 1296


def _win_perm():
    idx = []
    idx += list(range(O_QA, O_QA + 256)) + list(range(O_KA, O_KA + 256)) + list(range(O_VA, O_VA + 256))
    idx += list(range(O_AB, O_AB + 16))
    for h in (0, 2, 1, 3):
        idx += list(range(O_QC + 64 * h, O_QC + 64 * h + 64))
    idx += list(range(O_KC, O_KC + 128)) + list(range(O_VC, O_VC + 128))
    for pp in range(2):
        for base in (O_QB, O_KB, O_VB, O_ZB):
            idx += list(range(base + 256 * pp, base + 256 * pp + 256))
    assert len(idx) == IN_DIM and len(set(idx)) == IN_DIM
    return np.array(idx)


def _wout_perm():
    idx = list(range(0, 768))
    for h in (0, 2, 1, 3):
        idx += list(range(768 + 64 * h, 768 + 64 * h + 64))
    return np.array(idx)


def _wgu_perm():
    idx = []
    for f in range(NFF):
        idx += list(range(f * 128, f * 128 + 128)) + list(range(DFF + f * 128, DFF + f * 128 + 128))
    return np.array(idx)


def _cf_layout(nl):
    off = {}
    c = 0

    def add(name, n):
        nonlocal c
        off[name] = c
        c += n

    for l in range(nl):
        add(f"n1_{l}", 8)
        add(f"n2_{l}", 8)
        add(f"qna_{l}", 1)
        add(f"kna_{l}", 1)
        add(f"qnc_{l}", 1)
        add(f"knc_{l}", 1)
        add(f"onb_{l}", 1)
        add(f"cw_{l}", 60)
        add(f"alog_{l}", 8)
        add(f"dtb_{l}", 8)
    add("ident", 128)
    add("ones", 128)
    add("uuf", 128)
    add("uub", 128)
    add("slf", 128)
    add("slb", 128)
    for d in range(2):
        for cc in range(2):
            add(f"sla_{d}{cc}", 128)
    off["_n"] = c
    return off


CB = {"ident": 0, "ones": 128, "blk64": 256, "ra": 384, "rc": 512, "strict": 640,
      "maskA": 768, "mask16": 1280, "gmask": 3328, "_n": 3840}


def _build_consts(inputs, nl):
    off = _cf_layout(nl)
    cf = np.zeros((128, off["_n"]), np.float32)
    p = np.arange(128)
    for l in range(nl):
        cf[:, off[f"n1_{l}"]:off[f"n1_{l}"] + 8] = inputs["norm1"][l].reshape(8, 128).T
        cf[:, off[f"n2_{l}"]:off[f"n2_{l}"] + 8] = inputs["norm2"][l].reshape(8, 128).T
        cf[:, off[f"qna_{l}"]] = inputs["qn_a"][l][p % 64]
        cf[:, off[f"kna_{l}"]] = inputs["kn_a"][l][p % 64]
        cf[:, off[f"qnc_{l}"]] = inputs["qn_c"][l][p % 64]
        cf[:, off[f"knc_{l}"]] = inputs["kn_c"][l][p % 64]
        cf[:, off[f"onb_{l}"]] = inputs["onorm_b"][l]
        cw = inputs["conv_b"][l]
        cf[:, off[f"cw_{l}"]:off[f"cw_{l}"] + 60] = cw.reshape(5, 12, 128).transpose(2, 1, 0).reshape(128, 60)
        cf[:, off[f"alog_{l}"]:off[f"alog_{l}"] + 8] = inputs["a_log_b"][l].reshape(1, 8)
        cf[:, off[f"dtb_{l}"]:off[f"dtb_{l}"] + 8] = inputs["dt_bias_b"][l].reshape(1, 8)
    j = p[:, None]
    i = p[None, :]
    same = (j // 64) == (i // 64)
    cf[:, off["ident"]:off["ident"] + 128] = (j == i)
    cf[:, off["ones"]:off["ones"] + 128] = 1.0
    cf[:, off["uuf"]:off["uuf"] + 128] = same & (j <= i)
    cf[:, off["uub"]:off["uub"] + 128] = same & (j >= i)
    cf[:, off["slf"]:off["slf"] + 128] = (j == 64 * (i // 64) + 63)
    cf[:, off["slb"]:off["slb"] + 128] = (j == 64 * (i // 64))
    for d in range(2):
        for cc in range(2):
            last = 64 * cc + (63 if d == 0 else 0)
            cf[:, off[f"sla_{d}{cc}"]:off[f"sla_{d}{cc}"] + 128] = (j == last) * np.ones((1, 128))

    cb = np.zeros((128, CB["_n"]), np.float32)
    cb[:, CB["ident"]:CB["ident"] + 128] = (j == i)
    cb[:, CB["ones"]:CB["ones"] + 128] = 1.0
    cb[:, CB["blk64"]:CB["blk64"] + 128] = same
    def rot_mat(pairs_fn):
        R = np.zeros((128, 128), np.float32)
        for m in range(128):
            hd = m % 64
            base = m - hd
            k, sgn = pairs_fn(hd)
            if k is not None:
                R[base + k, m] = sgn
        return R

    def pa(hd):
        if hd < 8:
            return hd + 8, -1.0
        if hd < 16:
            return hd - 8, 1.0
        return None, 0.0

    def pc(hd):
        blk = hd // 32
        r = hd % 32
        if r < 16:
            return blk * 32 + r + 16, -1.0
        return blk * 32 + r - 16, 1.0

    cb[:, CB["ra"]:CB["ra"] + 128] = rot_mat(pa)
    cb[:, CB["rc"]:CB["rc"] + 128] = rot_mat(pc)
    cb[:, CB["strict"]:CB["strict"] + 128] = (j != i)
    pk = p[:, None]
    f128 = np.arange(128)[None, :]
    f64 = np.arange(64)[None, :]
    mfull = np.where(np.abs(f128 - pk) <= 64, 0.0, NEG)
    mprev = np.where(pk >= f64 + 64, 0.0, NEG)
    mnext = np.where(pk <= f64, 0.0, NEG)
    one = np.concatenate([mfull, mprev, mnext], axis=1)
    cb[:, CB["maskA"]:CB["maskA"] + 512] = np.concatenate([one, one], axis=1)
    for B in range(4):
        blk = mfull[:, 32 * B:32 * B + 32]
        cb[:, CB["mask16"] + 512 * B:CB["mask16"] + 512 * B + 512] = np.tile(blk, (1, 16))
    gf = np.where(same & (i >= j), 0.0, NEG)
    gb = np.where(same & (i <= j), 0.0, NEG)
    cb[:, CB["gmask"]:CB["gmask"] + 512] = np.concatenate([gf, gf, gb, gb], axis=1)
    cb16 = cb.astype(ml_dtypes.bfloat16)

    t = np.arange(S, dtype=np.float32)
    ropeA = np.zeros((128, 2, S), np.float32)
    ropeC = np.zeros((128, 2, S), np.float32)
    invA = np.float32(500000.0) ** (-np.arange(8, dtype=np.float32) / 8)
    invC = np.float32(10000.0) ** (-np.arange(16, dtype=np.float32) / 16)
    rowp = (np.arange(S) // 64).astype(np.float32)
    colp = (np.arange(S) % 64).astype(np.float32)
    for m in range(128):
        hd = m % 64
        if hd < 16:
            ang = t * invA[hd % 8]
            ropeA[m, 0] = np.cos(ang)
            ropeA[m, 1] = np.sin(ang)
        else:
            ropeA[m, 0] = 1.0
        pos = rowp if hd < 32 else colp
        ang = pos * invC[(hd % 32) % 16]
        ropeC[m, 0] = np.cos(ang)
        ropeC[m, 1] = np.sin(ang)
    return cf, cb16, ropeA.astype(ml_dtypes.bfloat16), ropeC.astype(ml_dtypes.bfloat16), off


def build_program(nl, taps=()):
    taps = set(taps)
    nc = bass.Bass("TRN2", target_bir_lowering=False)
    off = _cf_layout(nl)
    xT_d = nc.dram_tensor("xT", [D, S], F32, kind="ExternalInput").ap()
    win_d = nc.dram_tensor("w_in", [nl, D, IN_DIM], F32, kind="ExternalInput").ap()
    wout_d = nc.dram_tensor("w_out", [nl, D, D], F32, kind="ExternalInput").ap()
    wgu_d = nc.dram_tensor("w_gu", [nl, D, 2 * DFF], F32, kind="ExternalInput").ap()
    wdn_d = nc.dram_tensor("w_dn", [nl, DFF, D], F32, kind="ExternalInput").ap()
    cf_d = nc.dram_tensor("cf", [128, off["_n"]], F32, kind="ExternalInput").ap()
    cb_d = nc.dram_tensor("cb", [128, CB["_n"]], BF16, kind="ExternalInput").ap()
    ropeA_d = nc.dram_tensor("ropeA", [128, 2, S], BF16, kind="ExternalInput").ap()
    ropeC_d = nc.dram_tensor("ropeC", [128, 2, S], BF16, kind="ExternalInput").ap()
    yT_d = nc.dram_tensor("yT", [D, S], F32, kind="ExternalOutput").ap()
    tap_d = {}
    for name, (shape, tdt) in TAP_SHAPES.items():
        if name in taps:
            tap_d[name] = nc.dram_tensor("tap_" + name, list(shape), tdt, kind="ExternalOutput").ap()

    with contextlib.ExitStack() as st:
        fw = FW(nc, st)
        xT = fw.sb("xT_s", [128, 8, S], F32)
        hT = fw.sb("hT_s", [128, 8, S], BF16)
        cf = fw.sb("cf_s", [128, off["_n"]], F32)
        cb = fw.sb("cb_s", [128, CB["_n"]], BF16)
        wbuf = [fw.sb(f"wbuf{i}", [128, 4096], BF16) for i in range(2)]
        R_x = [[fw.res(f"x{c}_{tb}") for tb in range(4)] for c in range(8)]
        R_h = [fw.res(f"h{tb}") for tb in range(4)]
        R_cf = fw.res("cf")
        R_cb = fw.res("cb")
        R_w = [fw.res("w0"), fw.res("w1")]
        R_out = fw.res("out")
        R_tap = fw.res("tap")
        PB = [fw.ps(f"pb{i}", [128, 512], F32) for i in range(8)]
        R_pb = [fw.res(f"pb{i}") for i in range(8)]
        s_ld = fw.new_sem("ld")
        s_w = [fw.new_sem("w0"), fw.new_sem("w1")]
        s_st = fw.new_sem("st")
        s_tap = fw.new_sem("tap")
        wstate = {"i": 0}

        def cfc(name, n=1, o=0):
            return cf[:, off[name] + o: off[name] + o + n]

        def cbm(name, n=128, o=0):
            return cb[:, CB[name] + o: CB[name] + o + n]

        ident_b = cbm("ident")
        ones_b = cbm("ones")
        blk64_b = cbm("blk64")
        ident_f = cfc("ident", 128)
        ones_f = cfc("ones", 128)

        fw.dma("sp", cf[:, :], cf_d[:, :], writes=[R_cf], sem=s_ld)
        fw.dma("sp", cb[:, :], cb_d[:, :], writes=[R_cb], sem=s_ld)
        xv = xT_d.rearrange("(c p) t -> p c t", p=128)
        for c in range(8):
            fw.dma("sp", xT[:, c, :], xv[:, c, :], writes=R_x[c], sem=s_ld)

        def load_w(dram_ap, ncols, kchunks=8):
            i = wstate["i"]
            wstate["i"] = 1 - i
            view = wbuf[i][:, 0:kchunks * ncols].rearrange("p (k n) -> p k n", k=kchunks)
            fw.dma("pool", view, dram_ap.rearrange("(k p) n -> p k n", p=128), writes=[R_w[i]], sem=s_w[i])
            return view, R_w[i]

        def tap(name, src_ap, reads, dst=None):
            if name not in tap_d:
                return
            fw.dma("sp", dst if dst is not None else tap_d[name], src_ap, reads=reads, writes=[R_tap], sem=s_tap)

        def rmsnorm_fm(l, which, scr):
            sq = [fw.sb(f"nsq{i}", [128, 512], BF16, scr) for i in range(2)]
            R_sq = [fw.res() for _ in range(2)]
            lnv = [fw.sb(f"nln{i}", [128, 512], F32, scr) for i in range(2)]
            rstd = [fw.sb(f"nrs{i}", [128, 512], F32, scr) for i in range(2)]
            R_ln = [fw.res() for _ in range(2)]
            R_rs = [fw.res() for _ in range(2)]
            for tb in range(4):
                ts_ = slice(tb * 512, tb * 512 + 512)
                pb = tb % 2
                for c in range(8):
                    i = c % 2
                    fw.act(sq[i][:, :], xT[:, c, ts_], AF.Square, reads=[R_x[c][tb]], writes=[R_sq[i]])
                    fw.mm(PB[pb][:, :], ones_b, sq[i][:, :], start=(c == 0), stop=(c == 7),
                          reads=[R_sq[i], R_cb], writes=[R_pb[pb]])
                fw.act(lnv[pb][:, :], PB[pb][:, :], AF.Ln, reads=[R_pb[pb]], writes=[R_ln[pb]],
                       scale=1.0 / D, bias=eps_col)
                fw.act(rstd[pb][:, :], lnv[pb][:, :], AF.Exp, reads=[R_ln[pb]], writes=[R_rs[pb]], scale=-0.5)
                for c in range(8):
                    fw.stt(hT[:, c, ts_], xT[:, c, ts_], cfc(f"{which}_{l}", 1, c), rstd[pb][:, :],
                           ALU.mult, ALU.mult, reads=[R_x[c][tb], R_rs[pb], R_cf], writes=[R_h[tb]])

        def qk_post(pbank, R_pbank, wcol, rot_b, cos_ap, sin_ap, R_tab, out_ap, R_out_, tmp, scale_q=None):
            (qw, R_qw, sq, R_sq2, lnv, R_ln2, rs, R_rs2, t1, R_t1, t2, R_t2, pss, R_pss, psr, R_psr) = tmp
            fw.act(qw[:, :], pbank[:, :], AF.Copy, reads=[R_pbank, R_cf], writes=[R_qw], scale=wcol)
            fw.act(sq[:, :], pbank[:, :], AF.Square, reads=[R_pbank, R_cf], writes=[R_sq2], scale=wcol)
            fw.mm(pss[:, :], blk64_b, sq[:, :], reads=[R_sq2, R_cb], writes=[R_pss])
            fw.mm(psr[:, :], rot_b, qw[:, :], reads=[R_qw, R_cb], writes=[R_psr])
            fw.act(lnv[:, :], pss[:, :], AF.Ln, reads=[R_pss], writes=[R_ln2], scale=1.0 / 64, bias=eps_col)
            fw.act(rs[:, :], lnv[:, :], AF.Exp, reads=[R_ln2], writes=[R_rs2], scale=-0.5)
            fw.tt("dve", t1[:, :], qw[:, :], cos_ap, ALU.mult, reads=[R_qw, R_tab], writes=[R_t1])
            fw.tt("dve", t2[:, :], psr[:, :], sin_ap, ALU.mult, reads=[R_psr, R_tab], writes=[R_t2])
            fw.tt("dve", t1[:, :], t1[:, :], t2[:, :], ALU.add, reads=[R_t1, R_t2], writes=[R_t1])
            fw.tt("dve", out_ap, t1[:, :], rs[:, :], ALU.mult, reads=[R_t1, R_rs2], writes=[R_out_])

        def qk_tmp(scr, tag, pss_i, psr_i):
            return (fw.sb(f"qw{tag}", [128, 512], BF16, scr), fw.res(),
                    fw.sb(f"qsq{tag}", [128, 512], BF16, scr), fw.res(),
                    fw.sb(f"qln{tag}", [128, 512], F32, scr), fw.res(),
                    fw.sb(f"qrs{tag}", [128, 512], F32, scr), fw.res(),
                    fw.sb(f"qt1{tag}", [128, 512], F32, scr), fw.res(),
                    fw.sb(f"qt2{tag}", [128, 512], F32, scr), fw.res(),
                    PB[pss_i], R_pb[pss_i], PB[psr_i], R_pb[psr_i])

        def proj_fm(wv, R_wv, col0, m, tb, pbank, R_pbank):
            ts_ = slice(tb * 512, tb * 512 + 512)
            for k in range(8):
                fw.mm(pbank[0:m, :], wv[:, k, col0:col0 + m], hT[:, k, ts_], start=(k == 0), stop=(k == 7),
                      reads=[R_wv, R_h[tb]], writes=[R_pbank])

        def wout_update(l, o_tiles, scr):
            for half in range(2):
                views = []
                for (row0, apf, R_o) in o_tiles:
                    wv, R_wv = load_w(wout_d[l, row0:row0 + 128, half * 512:half * 512 + 512], 512, kchunks=1)
                    views.append((wv, R_wv, apf, R_o))
                for oc4 in range(4):
                    oc = half * 4 + oc4
                    for tb in range(4):
                        pb = (oc4 * 4 + tb) % 4
                        n = len(views)
                        for i, (wv, R_wv, apf, R_o) in enumerate(views):
                            fw.mm(PB[pb][:, :], wv[:, 0, oc4 * 128:oc4 * 128 + 128], apf(tb), start=(i == 0),
                                  stop=(i == n - 1), reads=[R_wv, R_o], writes=[R_pb[pb]])
                        ts_ = slice(tb * 512, tb * 512 + 512)
                        fw.tt("dve", xT[:, oc, ts_], xT[:, oc, ts_], PB[pb][:, :], ALU.add,
                              reads=[R_x[oc][tb], R_pb[pb]], writes=[R_x[oc][tb]])

        def attn_finalize(acc, R_acc, hh, out_ap, R_o, tmp):
            lnd, R_lnd, rd, R_rd = tmp
            nr = slice(hh * 64, hh * 64 + 64)
            dr = slice((1 - hh) * 64, (1 - hh) * 64 + 64)
            fw.act(lnd[nr, :], acc[dr, :], AF.Ln, reads=[R_acc], writes=[R_lnd])
            fw.act(rd[nr, :], lnd[nr, :], AF.Exp, reads=[R_lnd], writes=[R_rd], scale=-1.0)
            fw.tt("dve", out_ap, acc[nr, :], rd[nr, :], ALU.mult, reads=[R_acc, R_rd], writes=[R_o])

        def mixer_A(l):
            with contextlib.ExitStack() as scr:
                qT = fw.sb("a_qT", [128, 2, S], BF16, scr)
                kT = fw.sb("a_kT", [128, 2, S], BF16, scr)
                R_q = [fw.res() for _ in range(2)]
                R_k = [fw.res() for _ in range(2)]
                tab = fw.sb("a_tab", [128, 2, S], BF16, scr)
                R_tab = fw.res()
                fw.dma("sp", tab[:, :, :], ropeA_d[:, :, :], writes=[R_tab], sem=s_ld)
                vx = fw.sb("a_vx", [128, 3, 16, 2, 128], BF16, scr)
                R_vx = fw.res()
                oT = fw.sb("a_oT", [128, 2, S], BF16, scr)
                R_o = [fw.res() for _ in range(2)]
                PT = [fw.sb(f"a_pt{i}", [128, 512], BF16, scr) for i in range(3)]
                R_pt = [fw.res() for _ in range(3)]
                lnd = fw.sb("a_lnd", [128, 512], F32, scr)
                rd = fw.sb("a_rd", [128, 512], F32, scr)
                ftmp = (lnd, fw.res(), rd, fw.res())
                tmps = [qk_tmp(scr, "a0", 2, 4)] * 2
                wv, R_wv = load_w(win_d[l, :, P_QA:P_QA + 512], 512)
                wv2, R_wv2 = load_w(win_d[l, :, P_VA:P_VA + 272], 272)
                n = 0
                for ci in range(4):
                    isq = ci < 2
                    dst, Rd = (qT, R_q) if isq else (kT, R_k)
                    c = ci % 2
                    for tb in range(4):
                        ts_ = slice(tb * 512, tb * 512 + 512)
                        pb = n % 2
                        proj_fm(wv, R_wv, ci * 128, 128, tb, PB[pb], R_pb[pb])
                        qk_post(PB[pb], R_pb[pb], cfc(f"qna_{l}" if isq else f"kna_{l}"), cbm("ra"),
                                tab[:, 0, ts_], tab[:, 1, ts_], R_tab, dst[:, c, ts_], Rd[c], tmps[n % 2])
                        n += 1
                tap("a_qT", qT[:, :, :], R_q)
                for hp in range(2):
                    fw.op("pool", lambda e: e.memset(vx[:, :, :, 0, 64:128], 1.0), reads=[], writes=[R_vx])
                    fw.op("pool", lambda e: e.memset(vx[:, :, :, 1, 0:64], 1.0), reads=[], writes=[R_vx])
                    n = 0
                    for pat, dil in enumerate((1, 4, 16)):
                        for tile in range(16):
                            if dil == 1:
                                t0, st_ = tile * 128, 1
                            elif dil == 4:
                                r, m = tile // 4, tile % 4
                                t0, st_ = r + 512 * m, 4
                            else:
                                t0, st_ = tile, 16
                            pb = 6 + (n // 4) % 2
                            sub = n % 4
                            for k in range(8):
                                fw.mm(PB[pb][:, sub * 128:sub * 128 + 128], hT[:, k, t0:t0 + 127 * st_ + 1:st_],
                                      wv2[:, k, hp * 128:hp * 128 + 128], start=(k == 0), stop=(k == 7),
                                      reads=[R_wv2] + R_h, writes=[R_pb[pb]])
                            if sub == 3:
                                tl = tile - 3
                                src = PB[pb][:, :].rearrange("p (t h d) -> p t h d", t=4, h=2)
                                fw.cp("dve", vx[:, pat, tl:tl + 4, 0, 0:64], src[:, :, 0, :],
                                      reads=[R_pb[pb]], writes=[R_vx])
                                fw.cp("dve", vx[:, pat, tl:tl + 4, 1, 64:128], src[:, :, 1, :],
                                      reads=[R_pb[pb]], writes=[R_vx])
                            n += 1
                    sidx = 0
                    for hh in range(2):
                        pr = slice(hh * 64, hh * 64 + 64)
                        c = hp
                        for B in range(4):
                            acc_i = (hh * 4 + B) % 2
                            acc, R_acc = PB[acc_i], R_pb[acc_i]
                            first = [True]

                            def run_bank(blocks, mask_ap):
                                nonlocal sidx
                                sb_i = 2 + sidx % 4
                                pt_i = sidx % 3
                                sidx += 1
                                Sb, R_S = PB[sb_i], R_pb[sb_i]
                                for bi, (scol, nn, q_ap, k_ap, pat, tile, acc_ap) in enumerate(blocks):
                                    fw.mm(Sb[:, scol:scol + nn], k_ap, q_ap, start=(bi == 0), stop=False,
                                          reads=[R_q[c], R_k[c]], writes=[R_S])
                                fw.mm(Sb[:, :], ident_b, mask_ap, start=False, stop=True, reads=[R_cb], writes=[R_S])
                                fw.act(PT[pt_i][:, :], Sb[:, :], AF.Exp, reads=[R_S], writes=[R_pt[pt_i]], scale=0.125)
                                for (scol, nn, q_ap, k_ap, pat, tile, acc_ap) in blocks:
                                    fw.mm(acc_ap, vx[:, pat, tile, hh, :], PT[pt_i][:, scol:scol + nn],
                                          start=first[0], stop=False, reads=[R_pt[pt_i], R_vx], writes=[R_acc])
                                    first[0] = False

                            for half in range(2):
                                blocks = []
                                for qi in range(2):
                                    ml = half * 2 + qi
                                    m = 4 * B + ml
                                    so = qi * 256
                                    blocks.append((so, 128, qT[pr, c, m * 128:m * 128 + 128],
                                                   kT[pr, c, m * 128:m * 128 + 128], 0, m, acc[:, ml * 128:ml * 128 + 128]))
                                    if m > 0:
                                        blocks.append((so + 128, 64, qT[pr, c, m * 128:m * 128 + 64],
                                                       kT[pr, c, (m - 1) * 128:m * 128], 0, m - 1,
                                                       acc[:, ml * 128:ml * 128 + 64]))
                                    if m < 15:
                                        blocks.append((so + 192, 64, qT[pr, c, m * 128 + 64:m * 128 + 128],
                                                       kT[pr, c, (m + 1) * 128:(m + 2) * 128], 0, m + 1,
                                                       acc[:, ml * 128 + 64:ml * 128 + 128]))
                                run_bank(blocks, cbm("maskA", 512))
                            for half in range(2):
                                blocks = []
                                for qi in range(2):
                                    r = half * 2 + qi
                                    so = qi * 256
                                    b0 = 512 * B + r

                                    def kt(mm_):
                                        return kT[pr, c, 512 * mm_ + r:512 * mm_ + 512:4]
                                    blocks.append((so, 128, qT[pr, c, b0:512 * B + 512:4], kt(B), 1, r * 4 + B,
                                                   acc[:, r:512:4]))
                                    if B > 0:
                                        blocks.append((so + 128, 64, qT[pr, c, b0:512 * B + 256:4], kt(B - 1), 1,
                                                       r * 4 + B - 1, acc[:, r:256:4]))
                                    if B < 3:
                                        blocks.append((so + 192, 64, qT[pr, c, b0 + 256:512 * B + 512:4], kt(B + 1), 1,
                                                       r * 4 + B + 1, acc[:, 256 + r:512:4]))
                                run_bank(blocks, cbm("maskA", 512))
                            blocks = []
                            for b in range(16):
                                blocks.append((b * 32, 32, qT[pr, c, 512 * B + b:512 * B + 512:16],
                                               kT[pr, c, b:S:16], 2, b, acc[:, b:512:16]))
                            run_bank(blocks, cbm("mask16", 512, 512 * B))
                            attn_finalize(acc, R_acc, hh, oT[pr, c, B * 512:B * 512 + 512], R_o[c], ftmp)
                tap("a_oT", oT[:, :, :], R_o)
                wout_update(l, [(c * 128, (lambda tb, c=c: oT[:, c, tb * 512:tb * 512 + 512]), R_o[c]) for c in range(2)], scr)
            fw.barrier()

        def mixer_C(l):
            with contextlib.ExitStack() as scr:
                qT = fw.sb("c_qT", [128, 2, S], BF16, scr)
                kT = fw.sb("c_kT", [128, S], BF16, scr)
                R_q = [fw.res() for _ in range(2)]
                R_k = fw.res()
                tab = fw.sb("c_tab", [128, 2, S], BF16, scr)
                R_tab = fw.res()
                fw.dma("sp", tab[:, :, :], ropeC_d[:, :, :], writes=[R_tab], sem=s_ld)
                vx = fw.sb("c_vx", [128, 16, 2, 128], BF16, scr)
                R_vx = fw.res()
                oT = fw.sb("c_oT", [128, 2, S], BF16, scr)
                R_o = [fw.res() for _ in range(2)]
                PT = [fw.sb(f"c_pt{i}", [128, 512], BF16, scr) for i in range(3)]
                R_pt = [fw.res() for _ in range(3)]
                lnd = fw.sb("c_lnd", [128, 512], F32, scr)
                rd = fw.sb("c_rd", [128, 512], F32, scr)
                ftmp = (lnd, fw.res(), rd, fw.res())
                tmps = [qk_tmp(scr, "c0", 2, 4)] * 2
                wv, R_wv = load_w(win_d[l, :, P_QC:P_QC + 512], 512)
                n = 0
                for ci in range(3):
                    for tb in range(4):
                        ts_ = slice(tb * 512, tb * 512 + 512)
                        pb = n % 2
                        proj_fm(wv, R_wv, ci * 128, 128, tb, PB[pb], R_pb[pb])
                        if ci < 2:
                            dst, Rd, wn = qT[:, ci, ts_], R_q[ci], f"qnc_{l}"
                        else:
                            dst, Rd, wn = kT[:, ts_], R_k, f"knc_{l}"
                        qk_post(PB[pb], R_pb[pb], cfc(wn), cbm("rc"), tab[:, 0, ts_], tab[:, 1, ts_], R_tab,
                                dst, Rd, tmps[n % 2])
                        n += 1
                tap("c_qT", qT[:, :, :], R_q)
                fw.op("pool", lambda e: e.memset(vx[:, :, 0, 64:128], 1.0), reads=[], writes=[R_vx])
                fw.op("pool", lambda e: e.memset(vx[:, :, 1, 0:64], 1.0), reads=[], writes=[R_vx])
                for tile in range(16):
                    pb = 6 + (tile // 4) % 2
                    sub = tile % 4
                    for k in range(8):
                        fw.mm(PB[pb][:, sub * 128:sub * 128 + 128], hT[:, k, tile * 128:tile * 128 + 128],
                              wv[:, k, 384:512], start=(k == 0), stop=(k == 7), reads=[R_wv] + R_h, writes=[R_pb[pb]])
                    if sub == 3:
                        tl = tile - 3
                        src = PB[pb][:, :].rearrange("p (t h d) -> p t h d", t=4, h=2)
                        fw.cp("dve", vx[:, tl:tl + 4, 0, 0:64], src[:, :, 0, :], reads=[R_pb[pb]], writes=[R_vx])
                        fw.cp("dve", vx[:, tl:tl + 4, 1, 64:128], src[:, :, 1, :], reads=[R_pb[pb]], writes=[R_vx])
                sidx = 0
                for c in range(2):
                    for hh in range(2):
                        pr = slice(hh * 64, hh * 64 + 64)
                        for B in range(4):
                            acc_i = (c * 8 + hh * 4 + B) % 2
                            acc, R_acc = PB[acc_i], R_pb[acc_i]
                            for kt in range(16):
                                sb_i = 2 + sidx % 4
                                pt_i = sidx % 3
                                sidx += 1
                                fw.mm(PB[sb_i][:, :], kT[pr, kt * 128:kt * 128 + 128], qT[pr, c, B * 512:B * 512 + 512],
                                      reads=[R_q[c], R_k], writes=[R_pb[sb_i]])
                                fw.act(PT[pt_i][:, :], PB[sb_i][:, :], AF.Exp, reads=[R_pb[sb_i]], writes=[R_pt[pt_i]],
                                       scale=0.125)
                                fw.mm(acc[:, :], vx[:, kt, hh, :], PT[pt_i][:, :], start=(kt == 0), stop=(kt == 15),
                                      reads=[R_pt[pt_i], R_vx], writes=[R_acc])
                            attn_finalize(acc, R_acc, hh, oT[pr, c, B * 512:B * 512 + 512], R_o[c], ftmp)
                tap("c_oT", oT[:, :, :], R_o)
                wout_update(l, [(768 + c * 128, (lambda tb, c=c: oT[:, c, tb * 512:tb * 512 + 512]), R_o[c])
                                for c in range(2)], scr)
            fw.barrier()


        def mixer_B(l):
            with contextlib.ExitStack() as scr:
                def tk(name, n=8):
                    return fw.sb("b_" + name, [128, 16, n], F32, scr)
                ab_tok = tk("ab", 16)
                beta = tk("beta"); nbeta = tk("nbeta"); g_tok = tk("g"); gc = tk("gc"); ngc = tk("ngc")
                egc = tk("egc"); kdec = tk("kdec")
                gam = fw.sb("b_gam", [128, 16, 2, 8], F32, scr)
                ea = fw.sb("b_ea", [128, 8], F32, scr)
                R_ts = fw.res()
                wv, R_wv = load_w(win_d[l, :, P_AB:P_AB + 16], 16)
                for tile in range(16):
                    for k in range(8):
                        fw.mm(PB[0][:, tile * 16:tile * 16 + 16], hT[:, k, tile * 128:tile * 128 + 128], wv[:, k, 0:16],
                              start=(k == 0), stop=(k == 7), reads=[R_wv] + R_h, writes=[R_pb[0]])
                fw.cp("dve", ab_tok[:, :, :], PB[0][:, 0:256].rearrange("p (t n) -> p t n", t=16),
                      reads=[R_pb[0]], writes=[R_ts])
                fw.act(beta[:, :, :], ab_tok[:, :, 8:16], AF.Tanh, reads=[R_ts], writes=[R_ts], scale=0.5)
                fw.ts("dve", beta[:, :, :], beta[:, :, :], 0.5, 0.5, ALU.mult, ALU.add, reads=[R_ts], writes=[R_ts])
                fw.ts("dve", nbeta[:, :, :], beta[:, :, :], -1.0, None, ALU.mult, reads=[R_ts], writes=[R_ts])
                dtb_b = cfc(f"dtb_{l}", 8).unsqueeze(1).broadcast_to([128, 16, 8])
                fw.tt("dve", g_tok[:, :, :], ab_tok[:, :, 0:8], dtb_b, ALU.add, reads=[R_ts, R_cf], writes=[R_ts])
                fw.act(g_tok[:, :, :], g_tok[:, :, :], AF.Exp, reads=[R_ts], writes=[R_ts])
                fw.act(g_tok[:, :, :], g_tok[:, :, :], AF.Ln, reads=[R_ts], writes=[R_ts], bias=cfc("ones", 1))
                fw.act(ea[:, :], cfc(f"alog_{l}", 8), AF.Exp, reads=[R_cf], writes=[R_ts])
                fw.stt(g_tok[:, :, :], g_tok[:, :, :], -1.0, ea[:, :].unsqueeze(1).broadcast_to([128, 16, 8]),
                       ALU.mult, ALU.mult, reads=[R_ts], writes=[R_ts])
                tap("b_g", g_tok[:, :, :], [R_ts])
                tap("b_beta", beta[:, :, :], [R_ts])
                for tile in range(16):
                    for d in range(2):
                        fw.mm(PB[1][:, tile * 8 + d * 4:tile * 8 + d * 4 + 4], cfc("uuf" if d == 0 else "uub", 128),
                              g_tok[:, tile, d * 4:d * 4 + 4], reads=[R_ts, R_cf], writes=[R_pb[1]])
                fw.cp("dve", gc[:, :, :], PB[1][:, 0:128].rearrange("p (t n) -> p t n", t=16), reads=[R_pb[1]], writes=[R_ts])
                fw.ts("dve", ngc[:, :, :], gc[:, :, :], -1.0, None, ALU.mult, reads=[R_ts], writes=[R_ts])
                fw.act(egc[:, :, :], gc[:, :, :], AF.Exp, reads=[R_ts], writes=[R_ts])
                for tile in range(16):
                    for d in range(2):
                        fw.mm(PB[2][:, tile * 8 + d * 4:tile * 8 + d * 4 + 4], cfc("slf" if d == 0 else "slb", 128),
                              gc[:, tile, d * 4:d * 4 + 4], reads=[R_ts, R_cf], writes=[R_pb[2]])
                        for cc in range(2):
                            o_ = (tile * 2 + cc) * 8 + d * 4
                            fw.mm(PB[3][:, o_:o_ + 4], cfc(f"sla_{d}{cc}", 128), gc[:, tile, d * 4:d * 4 + 4],
                                  reads=[R_ts, R_cf], writes=[R_pb[3]])
                fw.tt("dve", kdec[:, :, :], PB[2][:, 0:128].rearrange("p (t n) -> p t n", t=16), gc[:, :, :], ALU.subtract,
                      reads=[R_pb[2], R_ts], writes=[R_ts])
                fw.act(kdec[:, :, :], kdec[:, :, :], AF.Exp, reads=[R_ts], writes=[R_ts])
                fw.act(gam[:, :, :, :], PB[3][:, 0:256].rearrange("p (t c n) -> p t c n", t=16, c=2), AF.Exp,
                       reads=[R_pb[3]], writes=[R_ts])
                tap("b_gc", gc[:, :, :], [R_ts])

                for pp in range(2):
                    with contextlib.ExitStack() as sp_:
                        bq = fw.sb("b_q", [128, 2, S], BF16, sp_)
                        bk = fw.sb("b_k", [128, 2, S], BF16, sp_)
                        K_tok = fw.sb("b_Kt", [128, 16, 2, 128], BF16, sp_)
                        V_tok = fw.sb("b_Vt", [128, 16, 2, 128], BF16, sp_)
                        R_bq = fw.res(); R_bk = fw.res(); R_Kt = fw.res(); R_Vt = fw.res()
                        with contextlib.ExitStack() as s1:
                            bv = fw.sb("b_v", [128, 2, S], BF16, s1)
                            raw = fw.sb("b_raw", [128, S + 4], BF16, s1)
                            dg = fw.sb("b_dg", [128, 5, 128], BF16, s1)
                            sq = fw.sb("b_sq", [128, 512], BF16, s1)
                            lnv = fw.sb("b_ln", [128, 512], F32, s1)
                            rs = fw.sb("b_rs", [128, 512], F32, s1)
                            R_bv = fw.res(); R_raw = fw.res(); R_dg = fw.res(); R_sq = fw.res(); R_ln = fw.res(); R_rs = fw.res()
                            fw.op("pool", lambda e: e.memset(raw[:, 0:2], 0.0), writes=[R_raw])
                            fw.op("pool", lambda e: e.memset(raw[:, S + 2:S + 4], 0.0), writes=[R_raw])
                            wq, R_wq = load_w(win_d[l, :, P_B + pp * 1024:P_B + pp * 1024 + 512], 512)
                            wz, R_wz = load_w(win_d[l, :, P_B + pp * 1024 + 512:P_B + pp * 1024 + 1024], 512)
                            n = 0
                            for kind in range(3):
                                for hh in range(2):
                                    wsrc, R_ws, col0 = (wq, R_wq, kind * 256 + hh * 128) if kind < 2 else (wz, R_wz, hh * 128)
                                    dst, R_dst = ((bq, R_bq), (bk, R_bk), (bv, R_bv))[kind]
                                    cch = kind * 4 + 2 * pp + hh
                                    for k in range(5):
                                        fw.ts("dve", dg[:, k, :], ident_b, cfc(f"cw_{l}", 1, cch * 5 + k), None, ALU.mult,
                                              reads=[R_cb, R_cf], writes=[R_dg])
                                    for tb in range(4):
                                        pb = n % 2
                                        n += 1
                                        proj_fm(wsrc, R_ws, col0, 128, tb, PB[pb], R_pb[pb])
                                        fw.cp("act", raw[:, 2 + tb * 512:2 + tb * 512 + 512], PB[pb][:, :],
                                              reads=[R_pb[pb]], writes=[R_raw])
                                    for tb in range(4):
                                        pb = 2 + n % 2
                                        n += 1
                                        ts_ = slice(tb * 512, tb * 512 + 512)
                                        for k in range(5):
                                            fw.mm(PB[pb][:, :], dg[:, k, :], raw[:, tb * 512 + k:tb * 512 + k + 512],
                                                  start=(k == 0), stop=(k == 4), reads=[R_dg, R_raw], writes=[R_pb[pb]])
                                        fw.act(dst[:, hh, ts_], PB[pb][:, :], AF.Silu, reads=[R_pb[pb]], writes=[R_dst])
                            for kind in range(2):
                                dst, R_dst = ((bq, R_bq), (bk, R_bk))[kind]
                                for hh in range(2):
                                    for tb in range(4):
                                        pb = 4 + n % 2
                                        n += 1
                                        ts_ = slice(tb * 512, tb * 512 + 512)
                                        fw.act(sq[:, :], dst[:, hh, ts_], AF.Square, reads=[R_dst], writes=[R_sq])
                                        fw.mm(PB[pb][:, :], ones_b, sq[:, :], reads=[R_sq, R_cb], writes=[R_pb[pb]])
                                        fw.act(lnv[:, :], PB[pb][:, :], AF.Ln, reads=[R_pb[pb]], writes=[R_ln], bias=eps_col)
                                        fw.act(rs[:, :], lnv[:, :], AF.Exp, reads=[R_ln], writes=[R_rs], scale=-0.5)
                                        fw.stt(dst[:, hh, ts_], dst[:, hh, ts_], (128.0 ** -0.5) if kind == 0 else 1.0, rs[:, :],
                                               ALU.mult, ALU.mult, reads=[R_dst, R_rs], writes=[R_dst])
                            tap("b_q", bq[:, :, :], [R_bq], dst=(tap_d["b_q"][:, 2 * pp:2 * pp + 2, :] if "b_q" in tap_d else None))
                            tap("b_k", bk[:, :, :], [R_bk], dst=(tap_d["b_k"][:, 2 * pp:2 * pp + 2, :] if "b_k" in tap_d else None))
                            tap("b_v", bv[:, :, :], [R_bv], dst=(tap_d["b_v"][:, 2 * pp:2 * pp + 2, :] if "b_v" in tap_d else None))
                            for src, R_src, dstt, R_dt in ((bk, R_bk, K_tok, R_Kt), (bv, R_bv, V_tok, R_Vt)):
                                for hh in range(2):
                                    for t4 in range(4):
                                        pb = 6 + n % 2
                                        n += 1
                                        pbv = PB[pb][:, :].bitcast(BF16)
                                        for i in range(4):
                                            tile = t4 * 4 + i
                                            fw.tr(pbv[:, i * 128:i * 128 + 128], src[:, hh, tile * 128:tile * 128 + 128], ident_b,
                                                  reads=[R_src, R_cb], writes=[R_pb[pb]])
                                        fw.cp("act", dstt[:, t4 * 4:t4 * 4 + 4, hh, :],
                                              pbv[:, 0:512].rearrange("p (t n) -> p t n", t=4), reads=[R_pb[pb]], writes=[R_dt])
                        fw.barrier()
                        obuf = fw.sb("b_ob", [128, 2, S], F32, sp_)
                        R_ob = fw.res()
                        fw.op("pool", lambda e: e.memset(obuf[:, :, :], 0.0), writes=[R_ob])
                        with contextlib.ExitStack() as s2:
                            def t4(name, dt):
                                return fw.sb("b_" + name, [128, 4, 128], dt, s2), fw.res()
                            KKs, R_KKs = t4("KKs", F32)
                            GU, R_GU = t4("GU", F32)
                            EG, R_EG = t4("EG", F32)
                            Et, R_Et = t4("Et", F32)
                            X, R_X = t4("X", BF16)
                            qkT, R_qkT = t4("qkT", BF16)
                            Yb = [t4(f"Y{i}", BF16) for i in range(2)]
                            Zb = [t4(f"Z{i}", BF16) for i in range(2)]
                            P32, R_P32 = t4("P32", F32)
                            Pb, R_Pb = t4("Pb", BF16)
                            Keg, R_Keg = t4("Keg", BF16)
                            Ktil, R_Ktil = t4("Ktil", BF16)
                            upp, R_upp = t4("upp", F32)
                            wT, R_wT = t4("wT", BF16)
                            qtT, R_qtT = t4("qtT", BF16)
                            S32 = fw.sb("b_S32", [128, 4, 128], F32, s2)
                            Sb = fw.sb("b_Sb", [128, 4, 128], BF16, s2)
                            vnew = fw.sb("b_vnew", [128, 4, 128], BF16, s2)
                            R_S32 = [fw.res(), fw.res()]; R_Sb = [fw.res(), fw.res()]; R_vn = [fw.res(), fw.res()]
                            R_psv = [fw.res(), fw.res()]; R_pss = [fw.res(), fw.res()]; R_po = [fw.res(), fw.res()]
                            fw.op("pool", lambda e: e.memset(S32[:, :, :], 0.0), writes=R_S32)
                            fw.op("pool", lambda e: e.memset(Sb[:, :, :], 0.0), writes=R_Sb)
                            v4 = lambda i: PB[i][:, :].rearrange("p (u n) -> p u n", u=4)
                            strict_b4 = cbm("strict").unsqueeze(1).broadcast_to([128, 4, 128])
                            ident_f4 = ident_f.unsqueeze(1).broadcast_to([128, 4, 128])
                            for n in range(16):
                                Td = (n, 15 - n)
                                ucol = lambda d, hh: d * 4 + 2 * pp + hh
                                tsl = lambda d: slice(Td[d] * 128, Td[d] * 128 + 128)
                                for d in range(2):
                                    for hh in range(2):
                                        u = d * 2 + hh
                                        fw.mm(PB[0][:, u * 128:u * 128 + 128], bk[:, hh, tsl(d)], bk[:, hh, tsl(d)],
                                              reads=[R_bk], writes=[R_pb[0]])
                                        fw.mm(PB[1][:, u * 128:u * 128 + 128], bk[:, hh, tsl(d)], bq[:, hh, tsl(d)],
                                              reads=[R_bk, R_bq], writes=[R_pb[1]])
                                fw.tt("dve", KKs[:, :, :], v4(0), strict_b4, ALU.mult, reads=[R_pb[0], R_cb], writes=[R_KKs])
                                for d in range(2):
                                    uu = cfc("uuf" if d == 0 else "uub", 128).unsqueeze(1).broadcast_to([128, 2, 128])
                                    gb_ = g_tok[:, Td[d], ucol(d, 0):ucol(d, 0) + 2].unsqueeze(2).broadcast_to([128, 2, 128])
                                    fw.tt("dve", GU[:, 2 * d:2 * d + 2, :], uu, gb_, ALU.mult, reads=[R_cf, R_ts], writes=[R_GU])
                                fw.mm(PB[2][:, :], ones_f, GU[:, :, :].rearrange("p u n -> p (u n)"), start=True, stop=False,
                                      reads=[R_GU, R_cf], writes=[R_pb[2]])
                                fw.act(EG[:, :, :], v4(2), AF.Exp, reads=[R_pb[2]], writes=[R_EG])
                                fw.mm(PB[2][:, :], ident_b, cbm("gmask", 512), start=False, stop=True, reads=[R_cb], writes=[R_pb[2]])
                                for d in range(2):
                                    for hh in range(2):
                                        u = d * 2 + hh
                                        c_ = ucol(d, hh)
                                        fw.act(Et[:, u, :], PB[2][:, u * 128:u * 128 + 128], AF.Exp, reads=[R_pb[2], R_ts],
                                               writes=[R_Et], bias=ngc[:, Td[d], c_:c_ + 1])
                                for d in range(2):
                                    for hh in range(2):
                                        u = d * 2 + hh
                                        c_ = ucol(d, hh)
                                        fw.stt(X[:, u, :], Et[:, u, :], beta[:, Td[d], c_:c_ + 1], KKs[:, u, :], ALU.mult, ALU.mult,
                                               reads=[R_Et, R_ts, R_KKs], writes=[R_X])
                                fw.tt("dve", qkT[:, :, :], v4(1), Et[:, :, :], ALU.mult, reads=[R_pb[1], R_Et], writes=[R_qkT])
                                pbv3 = PB[3][:, :].bitcast(BF16)
                                for u in range(4):
                                    fw.tr(pbv3[:, u * 128:u * 128 + 128], X[:, u, :], ident_b, reads=[R_X, R_cb], writes=[R_pb[3]])
                                (Yc, R_Yc), (Zc, R_Zc) = Yb[0], (X, R_X)
                                fw.cp("act", Yc[:, :, :], pbv3[:, 0:512].rearrange("p (u n) -> p u n", u=4), reads=[R_pb[3]], writes=[R_Yc])
                                fw.tt("dve", P32[:, :, :], ident_f4, X[:, :, :], ALU.subtract, reads=[R_cf, R_X], writes=[R_P32])
                                fw.cp("pool", Pb[:, :, :], P32[:, :, :], reads=[R_P32], writes=[R_Pb])
                                for s_ in range(1, 6):
                                    Yn, R_Yn = Yb[s_ % 2]
                                    Zn, R_Zn = Zb[s_ % 2]
                                    for u in range(4):
                                        fw.mm(PB[3][:, u * 128:u * 128 + 128], Zc[:, u, :], Yc[:, u, :], reads=[R_Zc, R_Yc], writes=[R_pb[3]])
                                    if s_ <= 4:
                                        for u in range(4):
                                            fw.mm(PB[4][:, u * 128:u * 128 + 128], Yc[:, u, :], Zc[:, u, :], reads=[R_Zc, R_Yc], writes=[R_pb[4]])
                                    fw.cp("act", Yn[:, :, :], v4(3), reads=[R_pb[3]], writes=[R_Yn])
                                    if s_ <= 4:
                                        fw.cp("dve", Zn[:, :, :], v4(4), reads=[R_pb[4]], writes=[R_Zn])
                                    for u in range(4):
                                        fw.mm(PB[3][:, u * 128:u * 128 + 128], Yn[:, u, :], Pb[:, u, :], reads=[R_Yn, R_Pb], writes=[R_pb[3]])
                                    fw.tt("dve", P32[:, :, :], P32[:, :, :], v4(3), ALU.add, reads=[R_P32, R_pb[3]], writes=[R_P32])
                                    fw.cp("pool", Pb[:, :, :], P32[:, :, :], reads=[R_P32], writes=[R_Pb])
                                    (Yc, R_Yc), (Zc, R_Zc) = (Yn, R_Yn), (Zn, R_Zn)
                                for d in range(2):
                                    c0 = ucol(d, 0)
                                    eb = egc[:, Td[d], c0:c0 + 2].unsqueeze(2).broadcast_to([128, 2, 128])
                                    kb_ = kdec[:, Td[d], c0:c0 + 2].unsqueeze(2).broadcast_to([128, 2, 128])
                                    fw.tt("dve", Keg[:, 2 * d:2 * d + 2, :], K_tok[:, Td[d], :, :], eb, ALU.mult, reads=[R_Kt, R_ts], writes=[R_Keg])
                                    fw.tt("dve", Ktil[:, 2 * d:2 * d + 2, :], K_tok[:, Td[d], :, :], kb_, ALU.mult, reads=[R_Kt, R_ts], writes=[R_Ktil])
                                    fw.tt("dve", qtT[:, 2 * d:2 * d + 2, :], bq[:, :, tsl(d)], EG[:, 2 * d:2 * d + 2, :], ALU.mult,
                                          reads=[R_bq, R_EG], writes=[R_qtT])
                                for d in range(2):
                                    for hh in range(2):
                                        u = d * 2 + hh
                                        fw.mm(PB[0][:, u * 128:u * 128 + 128], Pb[:, u, :], V_tok[:, Td[d], hh, :], reads=[R_Pb, R_Vt], writes=[R_pb[0]])
                                        fw.mm(PB[1][:, u * 128:u * 128 + 128], Keg[:, u, :], Pb[:, u, :], reads=[R_Pb, R_Keg], writes=[R_pb[1]])
                                for d in range(2):
                                    c0 = ucol(d, 0)
                                    bb_ = beta[:, Td[d], c0:c0 + 2].unsqueeze(2).broadcast_to([128, 2, 128])
                                    fw.tt("dve", upp[:, 2 * d:2 * d + 2, :], v4(0)[:, 2 * d:2 * d + 2, :], bb_, ALU.mult,
                                          reads=[R_pb[0], R_ts], writes=[R_upp])
                                fw.cp("act", wT[:, :, :], v4(1), reads=[R_pb[1]], writes=[R_wT])
                                for sub in range(2):
                                    for d in range(2):
                                        cc = sub if d == 0 else 1 - sub
                                        rows = slice(cc * 64, cc * 64 + 64)
                                        ccols = slice(cc * 64, cc * 64 + 64)
                                        for hh in range(2):
                                            u = d * 2 + hh
                                            fw.mm(PB[6][:, u * 128:u * 128 + 128], wT[:, u, :], Sb[:, u, :], reads=[R_wT, R_Sb[d]], writes=[R_psv[d]])
                                        for hh in range(2):
                                            u = d * 2 + hh
                                            c_ = ucol(d, hh)
                                            fw.stt(vnew[rows, u, :], PB[6][rows, u * 128:u * 128 + 128], nbeta[rows, Td[d], c_:c_ + 1],
                                                   upp[rows, u, :], ALU.mult, ALU.add, reads=[R_psv[d], R_ts, R_upp], writes=[R_vn[d]])
                                        for hh in range(2):
                                            u = d * 2 + hh
                                            oc_ = u * 128 + cc * 64
                                            fw.mm(PB[5][:, oc_:oc_ + 64], Sb[:, u, :], qtT[:, u, ccols], start=True, stop=False,
                                                  reads=[R_Sb[d], R_qtT], writes=[R_po[d]])
                                            fw.mm(PB[5][:, oc_:oc_ + 64], vnew[rows, u, :], qkT[rows, u, ccols], start=False, stop=True,
                                                  reads=[R_vn[d], R_qkT], writes=[R_po[d]])
                                        for hh in range(2):
                                            u = d * 2 + hh
                                            fw.mm(PB[7][:, u * 128:u * 128 + 128], Ktil[rows, u, :], vnew[rows, u, :],
                                                  reads=[R_Ktil, R_vn[d]], writes=[R_pss[d]])
                                        for hh in range(2):
                                            u = d * 2 + hh
                                            c_ = ucol(d, hh)
                                            fw.stt(S32[:, u, :], S32[:, u, :], gam[:, Td[d], cc, c_:c_ + 1], PB[7][:, u * 128:u * 128 + 128],
                                                   ALU.mult, ALU.add, reads=[R_S32[d], R_ts, R_pss[d]], writes=[R_S32[d]])
                                        fw.cp("pool", Sb[:, 2 * d:2 * d + 2, :], S32[:, 2 * d:2 * d + 2, :], reads=[R_S32[d]], writes=[R_Sb[d]])
                                for d in range(2):
                                    fw.tt("dve", obuf[:, :, tsl(d)], obuf[:, :, tsl(d)], v4(5)[:, 2 * d:2 * d + 2, :], ALU.add,
                                          reads=[R_ob, R_po[d]], writes=[R_ob])
                        fw.barrier()
                        tap("b_o", obuf[:, :, :], [R_ob], dst=(tap_d["b_o"][:, 2 * pp:2 * pp + 2, :] if "b_o" in tap_d else None))
                        with contextlib.ExitStack() as s3:
                            oT = fw.sb("b_oT", [128, 2, S], BF16, s3)
                            R_o = [fw.res(), fw.res()]
                            zs = fw.sb("b_zs", [128, 512], F32, s3)
                            sq = fw.sb("b_sq3", [128, 512], BF16, s3)
                            lnv = fw.sb("b_ln3", [128, 512], F32, s3)
                            rs = fw.sb("b_rs3", [128, 512], F32, s3)
                            t1 = fw.sb("b_t13", [128, 512], F32, s3)
                            R_zs = fw.res(); R_sq = fw.res(); R_ln = fw.res(); R_rs = fw.res(); R_t1 = fw.res()
                            wz, R_wz = load_w(win_d[l, :, P_B + pp * 1024 + 768:P_B + pp * 1024 + 1024], 256)
                            n = 0
                            for hh in range(2):
                                for tb in range(4):
                                    ts_ = slice(tb * 512, tb * 512 + 512)
                                    pz = n % 2
                                    pn = 2 + n % 2
                                    n += 1
                                    proj_fm(wz, R_wz, hh * 128, 128, tb, PB[pz], R_pb[pz])
                                    fw.act(zs[:, :], PB[pz][:, :], AF.Silu, reads=[R_pb[pz]], writes=[R_zs])
                                    fw.act(sq[:, :], obuf[:, hh, ts_], AF.Square, reads=[R_ob], writes=[R_sq])
                                    fw.mm(PB[pn][:, :], ones_b, sq[:, :], reads=[R_sq, R_cb], writes=[R_pb[pn]])
                                    fw.act(lnv[:, :], PB[pn][:, :], AF.Ln, reads=[R_pb[pn]], writes=[R_ln], scale=1.0 / 128, bias=eps_col)
                                    fw.act(rs[:, :], lnv[:, :], AF.Exp, reads=[R_ln], writes=[R_rs], scale=-0.5)
                                    fw.stt(t1[:, :], obuf[:, hh, ts_], cfc(f"onb_{l}"), rs[:, :], ALU.mult, ALU.mult,
                                           reads=[R_ob, R_cf, R_rs], writes=[R_t1])
                                    fw.tt("dve", oT[:, hh, ts_], t1[:, :], zs[:, :], ALU.mult, reads=[R_t1, R_zs], writes=[R_o[hh]])
                            tap("b_oT", oT[:, :, :], R_o, dst=(tap_d["b_oT"][:, 2 * pp:2 * pp + 2, :] if "b_oT" in tap_d else None))
                            wout_update(l, [(256 + (2 * pp + hh) * 128, (lambda tb, hh=hh: oT[:, hh, tb * 512:tb * 512 + 512]), R_o[hh])
                                            for hh in range(2)], s3)
                        fw.barrier()
            fw.barrier()

        def ffn(l):
            with contextlib.ExitStack() as scr0:
                rmsnorm_fm(l, "n2", scr0)
            fw.barrier()
            with contextlib.ExitStack() as scr:
                NH = 12
                actT = fw.sb("f_act", [128, NH, S], BF16, scr)
                sg = [fw.sb(f"f_sg{i}", [128, 512], F32, scr) for i in range(2)]
                R_sg = [fw.res() for _ in range(2)]
                n = 0
                for (f0, nf) in ((0, 12), (12, 10)):
                    R_a = [[fw.res() for _ in range(4)] for _ in range(nf)]
                    for g in range(nf // 2):
                        gg = f0 // 2 + g
                        wv, R_wv = load_w(wgu_d[l, :, gg * 512:gg * 512 + 512], 512)
                        for fi in range(2):
                            f = g * 2 + fi
                            for tb in range(4):
                                pg = (n % 2) * 2
                                pu = pg + 1
                                proj_fm(wv, R_wv, fi * 256, 128, tb, PB[pg], R_pb[pg])
                                proj_fm(wv, R_wv, fi * 256 + 128, 128, tb, PB[pu], R_pb[pu])
                                si = n % 2
                                fw.act(sg[si][:, :], PB[pg][:, :], AF.Silu, reads=[R_pb[pg]], writes=[R_sg[si]])
                                fw.tt("dve", actT[:, f, tb * 512:tb * 512 + 512], sg[si][:, :], PB[pu][:, :], ALU.mult,
                                      reads=[R_sg[si], R_pb[pu]], writes=[R_a[f][tb]])
                                n += 1
                    for oc in range(8):
                        wv, R_wv = load_w(wdn_d[l, f0 * 128:(f0 + nf) * 128, oc * 128:oc * 128 + 128], 128, kchunks=nf)
                        for tb in range(4):
                            pb = 4 + (oc * 4 + tb) % 4
                            ts_ = slice(tb * 512, tb * 512 + 512)
                            for f in range(nf):
                                fw.mm(PB[pb][:, :], wv[:, f, :], actT[:, f, ts_], start=(f == 0), stop=(f == nf - 1),
                                      reads=[R_wv, R_a[f][tb]], writes=[R_pb[pb]])
                            fw.tt("dve", xT[:, oc, ts_], xT[:, oc, ts_], PB[pb][:, :], ALU.add,
                                  reads=[R_x[oc][tb], R_pb[pb]], writes=[R_x[oc][tb]])
                    fw.barrier()
            fw.barrier()

        eps_t = fw.sb("eps_t", [128, 1], F32)
        R_eps = fw.res()
        fw.op("pool", lambda e: e.memset(eps_t[:, :], EPS), writes=[R_eps])
        eps_col = eps_t[:, 0:1]
        fw.barrier()

        for l in range(nl):
            with contextlib.ExitStack() as scr:
                rmsnorm_fm(l, "n1", scr)
            if l == 0:
                tap("hT", hT[:, :, :], R_h)
            fw.barrier()
            if "skipA" not in taps:
                mixer_A(l)
            if "skipC" not in taps:
                mixer_C(l)
            if "skipB" not in taps:
                mixer_B(l)
            if l == 0:
                tap("xmid", xT[:, :, :], [r for rr in R_x for r in rr])
            ffn(l)

        yv = yT_d.rearrange("(c p) t -> p c t", p=128)
        for c in range(8):
            fw.dma("sp", yv[:, c, :], xT[:, c, :], reads=R_x[c], writes=[R_out], sem=s_st)
        fw.wait_all("sp", [R_out, R_tap])
        print("ninst", fw.ninst)
    return nc


TAP_SHAPES = {
    "hT": ((128, 8, S), BF16), "a_qT": ((128, 2, S), BF16), "a_oT": ((128, 2, S), BF16), "c_qT": ((128, 2, S), BF16),
    "c_oT": ((128, 2, S), BF16), "xmid": ((128, 8, S), F32), "f_act": ((128, NFF, S), BF16),
    "b_g": ((128, 16, 8), F32), "b_beta": ((128, 16, 8), F32), "b_gc": ((128, 16, 8), F32),
    "b_q": ((128, 4, S), BF16), "b_k": ((128, 4, S), BF16), "b_v": ((128, 4, S), BF16), "b_o": ((128, 4, S), F32),
    "b_oT": ((128, 4, S), BF16),
}


_PROG_CACHE = {}


def _prep_inputs(inputs, nl):
    perm_in = _win_perm()
    perm_out = _wout_perm()
    perm_gu = _wgu_perm()
    w_in = np.ascontiguousarray(inputs["w_in"][:nl][:, :, perm_in])
    w_out = np.ascontiguousarray(inputs["w_out"][:nl][:, perm_out, :])
    w_gu = np.ascontiguousarray(inputs["w_gate_up"][:nl][:, :, perm_gu])
    w_dn = np.ascontiguousarray(inputs["w_down"][:nl])
    cf, cb16, ropeA, ropeC, _ = _build_consts(inputs, nl)
    shared = {"w_in": w_in, "w_out": w_out, "w_gu": w_gu, "w_dn": w_dn, "cf": cf, "cb": cb16,
              "ropeA": ropeA, "ropeC": ropeC}
    return shared


def kernel(**inputs):
    inputs = {k: np.asarray(v) for k, v in inputs.items()}
    nl = L_FULL
    shared = _prep_inputs(inputs, nl)
    x = inputs["x"]
    in_maps = []
    for b in range(8):
        m = dict(shared)
        m["xT"] = np.ascontiguousarray(x[b].T)
        in_maps.append(m)
    if nl not in _PROG_CACHE:
        _PROG_CACHE[nl] = build_program(nl)
    res = run_bass_kernel_spmd(_PROG_CACHE[nl], in_maps, core_ids=list(range(8)))
    out = np.stack([np.ascontiguousarray(r["yT"].T) for r in res.results], axis=0)
    return out.astype(np.float32)
```

```python
import contextlib
import numpy as np
import ml_dtypes
import concourse.bass as bass
import concourse.mybir as mybir
from concourse.bass_utils import run_bass_kernel_spmd

F32 = mybir.dt.float32
BF16 = mybir.dt.bfloat16
AF = mybir.ActivationFunctionType
ALU = mybir.AluOpType

L_FULL = 4
S = 2048
D = 1024
DFF = 2816
NFF = 22
IN_DIM = 3344
EPS = 1e-6
NEG = -30000.0


class Res:
    __slots__ = ("name", "w", "r", "psum")

    def __init__(self, name, psum=False):
        self.name = name
        self.w = None
        self.r = {}
        self.psum = psum


class FW:
    ENG = ("pe", "act", "dve", "pool", "sp")

    def __init__(self, nc, stack):
        self.nc = nc
        self.stack = stack
        self.e = {"pe": nc.tensor, "act": nc.scalar, "dve": nc.vector, "pool": nc.gpsimd, "sp": nc.sync}
        self.sems = {}
        self.cnt = {}
        self.waited = {k: {} for k in self.ENG}
        for k in self.ENG:
            self.sems[k] = stack.enter_context(nc.semaphore("s_" + k))
            self.cnt[k] = 0
        self.nres = 0
        self.ninst = {k: 0 for k in self.ENG}

    def sb(self, name, shape, dt, stack=None):
        self.nsb = getattr(self, "nsb", 0) + 1
        return (stack or self.stack).enter_context(self.nc.sbuf_tensor(f"{name}_{self.nsb}", list(shape), dt))

    def ps(self, name, shape, dt=F32):
        return self.stack.enter_context(self.nc.psum_tensor(name, list(shape), dt))

    def res(self, name=None, psum=False):
        self.nres += 1
        return Res(name or f"r{self.nres}", psum)

    def new_sem(self, name):
        s = self.stack.enter_context(self.nc.semaphore(name))
        self.sems[name] = s
        self.cnt[name] = 0
        return name

    def _deps(self, eng, reads, writes, waiter=None):
        waiter = waiter or eng
        deps = {}

        def need(sv):
            if sv is None:
                return
            s, v = sv
            if deps.get(s, 0) < v:
                deps[s] = v

        for r in reads:
            need(r.w)
            if r.psum:
                for s, v in r.r.items():
                    if s != eng:
                        need((s, v))
        for w in writes:
            need(w.w)
            for s, v in w.r.items():
                need((s, v))
        out = {}
        for s, v in deps.items():
            if s == eng:
                if eng == "pe":
                    continue
                raw_max = max([r.w[1] for r in reads if r.w is not None and r.w[0] == eng], default=0)
                if raw_max == 0:
                    continue
                v = raw_max
            if self.waited[waiter].get(s, 0) >= v:
                continue
            out[s] = v
        return out

    def _emit_waits(self, eng, deps):
        for s, v in deps.items():
            self.e[eng].wait_ge(self.sems[s], v)
            self.waited[eng][s] = v
            self.ninst[eng] += 1

    def op(self, eng, fn, reads=(), writes=()):
        deps = self._deps(eng, reads, writes)
        self._emit_waits(eng, deps)
        inst = fn(self.e[eng])
        self.cnt[eng] += 1
        self.ninst[eng] += 1
        v = self.cnt[eng]
        inst.then_inc(self.sems[eng], 1)
        for r in reads:
            if r.r.get(eng, 0) < v:
                r.r[eng] = v
        for w in writes:
            w.w = (eng, v)
            w.r = {}
        return inst

    def dma(self, q, out, in_, reads=(), writes=(), sem=None, **kw):
        d2 = self._deps(None, reads, writes, waiter=q)
        self._emit_waits(q, d2)
        inst = self.e[q].dma_start(out=out, in_=in_, **kw)
        self.cnt[sem] += 16
        v = self.cnt[sem]
        inst.then_inc(self.sems[sem], 16)
        self.ninst[q] += 1
        for r in reads:
            if r.r.get(sem, 0) < v:
                r.r[sem] = v
        for w in writes:
            w.w = (sem, v)
            w.r = {}
        return inst

    def wait_all(self, eng, resources):
        d2 = self._deps(None, resources, (), waiter=eng)
        self._emit_waits(eng, d2)

    def barrier(self):
        snap = dict(self.cnt)
        for eng in self.ENG:
            d = {}
            for s, v in snap.items():
                if v > 0 and s != eng and self.waited[eng].get(s, 0) < v:
                    d[s] = v
            self._emit_waits(eng, d)

    def mm(self, out, lhsT, rhs, start=True, stop=True, reads=(), writes=()):
        return self.op("pe", lambda e: e.matmul(out, lhsT, rhs, start=start, stop=stop), reads, writes)

    def tr(self, out, in_, ident, reads=(), writes=()):
        return self.op("pe", lambda e: e.transpose(out, in_, ident), reads, writes)

    def act(self, out, in_, func, reads=(), writes=(), **kw):
        return self.op("act", lambda e: e.activation(out=out, in_=in_, func=func, **kw), reads, writes)

    def tt(self, eng, out, in0, in1, op, reads=(), writes=()):
        return self.op(eng, lambda e: e.tensor_tensor(out=out, in0=in0, in1=in1, op=op), reads, writes)

    def stt(self, out, in0, scalar, in1, op0, op1, reads=(), writes=()):
        return self.op("dve", lambda e: e.scalar_tensor_tensor(out=out, in0=in0, scalar=scalar, in1=in1,
                                                                op0=op0, op1=op1), reads, writes)

    def ts(self, eng, out, in0, s1, s2, op0, op1=None, reads=(), writes=()):
        if op1 is None:
            return self.op(eng, lambda e: e.tensor_scalar(out=out, in0=in0, scalar1=s1, scalar2=None, op0=op0),
                           reads, writes)
        return self.op(eng, lambda e: e.tensor_scalar(out=out, in0=in0, scalar1=s1, scalar2=s2, op0=op0, op1=op1),
                       reads, writes)

    def cp(self, eng, out, in_, reads=(), writes=()):
        if eng == "act":
            return self.act(out, in_, AF.Copy, reads, writes)
        return self.op(eng, lambda e: e.tensor_copy(out=out, in_=in_), reads, writes)


O_QA, O_KA, O_VA = 0, 256, 512
O_QB, O_KB, O_VB, O_ZB = 768, 1280, 1792, 2304
O_AB, O_BB = 2816, 2824
O_QC, O_KC, O_VC = 2832, 3088, 3216

P_QA, P_KA, P_VA, P_AB = 0, 256, 512, 768
P_QC, P_KC, P_VC = 784, 1040, 1168
P_B = 1296


def _win_perm():
    idx = []
    idx += list(range(O_QA, O_QA + 256)) + list(range(O_KA, O_KA + 256)) + list(range(O_VA, O_VA + 256))
    idx += list(range(O_AB, O_AB + 16))
    for h in (0, 2, 1, 3):
        idx += list(range(O_QC + 64 * h, O_QC + 64 * h + 64))
    idx += list(range(O_KC, O_KC + 128)) + list(range(O_VC, O_VC + 128))
    for pp in range(2):
        for base in (O_QB, O_KB, O_VB, O_ZB):
            idx += list(range(base + 256 * pp, base + 256 * pp + 256))
    assert len(idx) == IN_DIM and len(set(idx)) == IN_DIM
    return np.array(idx)


def _wout_perm():
    idx = list(range(0, 768))
    for h in (0, 2, 1, 3):
        idx += list(range(768 + 64 * h, 768 + 64 * h + 64))
    return np.array(idx)


def _wgu_perm():
    idx = []
    for f in range(NFF):
        idx += list(range(f * 128, f * 128 + 128)) + list(range(DFF + f * 128, DFF + f * 128 + 128))
    return np.array(idx)


def _cf_layout(nl):
    off = {}
    c = 0

    def add(name, n):
        nonlocal c
        off[name] = c
        c += n

    for l in range(nl):
        add(f"n1_{l}", 8)
        add(f"n2_{l}", 8)
        add(f"qna_{l}", 1)
        add(f"kna_{l}", 1)
        add(f"qnc_{l}", 1)
        add(f"knc_{l}", 1)
        add(f"onb_{l}", 1)
        add(f"cw_{l}", 60)
        add(f"alog_{l}", 8)
        add(f"dtb_{l}", 8)
    add("ident", 128)
    add("ones", 128)
    add("uuf", 128)
    add("uub", 128)
    add("slf", 128)
    add("slb", 128)
    for d in range(2):
        for cc in range(2):
            add(f"sla_{d}{cc}", 128)
    off["_n"] = c
    return off


CB = {"ident": 0, "ones": 128, "blk64": 256, "ra": 384, "rc": 512, "strict": 640,
      "maskA": 768, "mask16": 1280, "gmask": 3328, "_n": 3840}


def _build_consts(inputs, nl):
    off = _cf_layout(nl)
    cf = np.zeros((128, off["_n"]), np.float32)
    p = np.arange(128)
    for l in range(nl):
        cf[:, off[f"n1_{l}"]:off[f"n1_{l}"] + 8] = inputs["norm1"][l].reshape(8, 128).T
        cf[:, off[f"n2_{l}"]:off[f"n2_{l}"] + 8] = inputs["norm2"][l].reshape(8, 128).T
        cf[:, off[f"qna_{l}"]] = inputs["qn_a"][l][p % 64]
        cf[:, off[f"kna_{l}"]] = inputs["kn_a"][l][p % 64]
        cf[:, off[f"qnc_{l}"]] = inputs["qn_c"][l][p % 64]
        cf[:, off[f"knc_{l}"]] = inputs["kn_c"][l][p % 64]
        cf[:, off[f"onb_{l}"]] = inputs["onorm_b"][l]
        cw = inputs["conv_b"][l]
        cf[:, off[f"cw_{l}"]:off[f"cw_{l}"] + 60] = cw.reshape(5, 12, 128).transpose(2, 1, 0).reshape(128, 60)
        cf[:, off[f"alog_{l}"]:off[f"alog_{l}"] + 8] = inputs["a_log_b"][l].reshape(1, 8)
        cf[:, off[f"dtb_{l}"]:off[f"dtb_{l}"] + 8] = inputs["dt_bias_b"][l].reshape(1, 8)
    j = p[:, None]
    i = p[None, :]
    same = (j // 64) == (i // 64)
    cf[:, off["ident"]:off["ident"] + 128] = (j == i)
    cf[:, off["ones"]:off["ones"] + 128] = 1.0
    cf[:, off["uuf"]:off["uuf"] + 128] = same & (j <= i)
    cf[:, off["uub"]:off["uub"] + 128] = same & (j >= i)
    cf[:, off["slf"]:off["slf"] + 128] = (j == 64 * (i // 64) + 63)
    cf[:, off["slb"]:off["slb"] + 128] = (j == 64 * (i // 64))
    for d in range(2):
        for cc in range(2):
            last = 64 * cc + (63 if d == 0 else 0)
            cf[:, off[f"sla_{d}{cc}"]:off[f"sla_{d}{cc}"] + 128] = (j == last) * np.ones((1, 128))

    cb = np.zeros((128, CB["_n"]), np.float32)
    cb[:, CB["ident"]:CB["ident"] + 128] = (j == i)
    cb[:, CB["ones"]:CB["ones"] + 128] = 1.0
    cb[:, CB["blk64"]:CB["blk64"] + 128] = same
    def rot_mat(pairs_fn):
        R = np.zeros((128, 128), np.float32)
        for m in range(128):
            hd = m % 64
            base = m - hd
            k, sgn = pairs_fn(hd)
            if k is not None:
                R[base + k, m] = sgn
        return R

    def pa(hd):
        if hd < 8:
            return hd + 8, -1.0
        if hd < 16:
            return hd - 8, 1.0
        return None, 0.0

    def pc(hd):
        blk = hd // 32
        r = hd % 32
        if r < 16:
            return blk * 32 + r + 16, -1.0
        return blk * 32 + r - 16, 1.0

    cb[:, CB["ra"]:CB["ra"] + 128] = rot_mat(pa)
    cb[:, CB["rc"]:CB["rc"] + 128] = rot_mat(pc)
    cb[:, CB["strict"]:CB["strict"] + 128] = (j != i)
    pk = p[:, None]
    f128 = np.arange(128)[None, :]
    f64 = np.arange(64)[None, :]
    mfull = np.where(np.abs(f128 - pk) <= 64, 0.0, NEG)
    mprev = np.where(pk >= f64 + 64, 0.0, NEG)
    mnext = np.where(pk <= f64, 0.0, NEG)
    one = np.concatenate([mfull, mprev, mnext], axis=1)
    cb[:, CB["maskA"]:CB["maskA"] + 512] = np.concatenate([one, one], axis=1)
    for B in range(4):
        blk = mfull[:, 32 * B:32 * B + 32]
        cb[:, CB["mask16"] + 512 * B:CB["mask16"] + 512 * B + 512] = np.tile(blk, (1, 16))
    gf = np.where(same & (i >= j), 0.0, NEG)
    gb = np.where(same & (i <= j), 0.0, NEG)
    cb[:, CB["gmask"]:CB["gmask"] + 512] = np.concatenate([gf, gf, gb, gb], axis=1)
    cb16 = cb.astype(ml_dtypes.bfloat16)

    t = np.arange(S, dtype=np.float32)
    ropeA = np.zeros((128, 2, S), np.float32)
    ropeC = np.zeros((128, 2, S), np.float32)
    invA = np.float32(500000.0) ** (-np.arange(8, dtype=np.float32) / 8)
    invC = np.float32(10000.0) ** (-np.arange(16, dtype=np.float32) / 16)
    rowp = (np.arange(S) // 64).astype(np.float32)
    colp = (np.arange(S) % 64).astype(np.float32)
    for m in range(128):
        hd = m % 64
        if hd < 16:
            ang = t * invA[hd % 8]
            ropeA[m, 0] = np.cos(ang)
            ropeA[m, 1] = np.sin(ang)
        else:
            ropeA[m, 0] = 1.0
        pos = rowp if hd < 32 else colp
        ang = pos * invC[(hd % 32) % 16]
        ropeC[m, 0] = np.cos(ang)
        ropeC[m, 1] = np.sin(ang)
    return cf, cb16, ropeA.astype(ml_dtypes.bfloat16), ropeC.astype(ml_dtypes.bfloat16), off


def build_program(nl, taps=()):
    taps = set(taps)
    nc = bass.Bass("TRN2", target_bir_lowering=False)
    off = _cf_layout(nl)
    xT_d = nc.dram_tensor("xT", [D, S], F32, kind="ExternalInput").ap()
    win_d = nc.dram_tensor("w_in", [nl, D, IN_DIM], F32, kind="ExternalInput").ap()
    wout_d = nc.dram_tensor("w_out", [nl, D, D], F32, kind="ExternalInput").ap()
    wgu_d = nc.dram_tensor("w_gu", [nl, D, 2 * DFF], F32, kind="ExternalInput").ap()
    wdn_d = nc.dram_tensor("w_dn", [nl, DFF, D], F32, kind="ExternalInput").ap()
    cf_d = nc.dram_tensor("cf", [128, off["_n"]], F32, kind="ExternalInput").ap()
    cb_d = nc.dram_tensor("cb", [128, CB["_n"]], BF16, kind="ExternalInput").ap()
    ropeA_d = nc.dram_tensor("ropeA", [128, 2, S], BF16, kind="ExternalInput").ap()
    ropeC_d = nc.dram_tensor("ropeC", [128, 2, S], BF16, kind="ExternalInput").ap()
    yT_d = nc.dram_tensor("yT", [D, S], F32, kind="ExternalOutput").ap()
    tap_d = {}
    for name, (shape, tdt) in TAP_SHAPES.items():
        if name in taps:
            tap_d[name] = nc.dram_tensor("tap_" + name, list(shape), tdt, kind="ExternalOutput").ap()

    with contextlib.ExitStack() as st:
        fw = FW(nc, st)
        xT = fw.sb("xT_s", [128, 8, S], F32)
        hT = fw.sb("hT_s", [128, 8, S], BF16)
        cf = fw.sb("cf_s", [128, off["_n"]], F32)
        cb = fw.sb("cb_s", [128, CB["_n"]], BF16)
        wbuf = [fw.sb(f"wbuf{i}", [128, 4096], BF16) for i in range(2)]
        R_x = [[fw.res(f"x{c}_{tb}") for tb in range(4)] for c in range(8)]
        R_h = [fw.res(f"h{tb}") for tb in range(4)]
        R_cf = fw.res("cf")
        R_cb = fw.res("cb")
        R_w = [fw.res("w0"), fw.res("w1")]
        R_out = fw.res("out")
        R_tap = fw.res("tap")
        PB = [fw.ps(f"pb{i}", [128, 512], F32) for i in range(8)]
        R_pb = [fw.res(f"pb{i}", psum=True) for i in range(8)]
        s_ld = fw.new_sem("ld")
        s_w = [fw.new_sem("w0"), fw.new_sem("w1")]
        s_st = fw.new_sem("st")
        s_tap = fw.new_sem("tap")
        wstate = {"i": 0}

        def cfc(name, n=1, o=0):
            return cf[:, off[name] + o: off[name] + o + n]

        def cbm(name, n=128, o=0):
            return cb[:, CB[name] + o: CB[name] + o + n]

        ident_b = cbm("ident")
        ones_b = cbm("ones")
        blk64_b = cbm("blk64")
        ident_f = cfc("ident", 128)
        ones_f = cfc("ones", 128)

        fw.dma("sp", cf[:, :], cf_d[:, :], writes=[R_cf], sem=s_ld)
        fw.dma("sp", cb[:, :], cb_d[:, :], writes=[R_cb], sem=s_ld)
        xv = xT_d.rearrange("(c p) t -> p c t", p=128)
        for c in range(8):
            fw.dma("sp", xT[:, c, :], xv[:, c, :], writes=R_x[c], sem=s_ld)

        def load_w(dram_ap, ncols, kchunks=8):
            i = wstate["i"]
            wstate["i"] = 1 - i
            view = wbuf[i][:, 0:kchunks * ncols].rearrange("p (k n) -> p k n", k=kchunks)
            fw.dma("pool", view, dram_ap.rearrange("(k p) n -> p k n", p=128), writes=[R_w[i]], sem=s_w[i])
            return view, R_w[i]

        def tap(name, src_ap, reads, dst=None):
            if name not in tap_d:
                return
            fw.dma("sp", dst if dst is not None else tap_d[name], src_ap, reads=reads, writes=[R_tap], sem=s_tap)

        def rmsnorm_fm(l, which, scr):
            sq = [fw.sb(f"nsq{i}", [128, 512], BF16, scr) for i in range(2)]
            R_sq = [fw.res() for _ in range(2)]
            lnv = [fw.sb(f"nln{i}", [128, 512], F32, scr) for i in range(2)]
            rstd = [fw.sb(f"nrs{i}", [128, 512], F32, scr) for i in range(2)]
            R_ln = [fw.res() for _ in range(2)]
            R_rs = [fw.res() for _ in range(2)]
            for tb in range(4):
                ts_ = slice(tb * 512, tb * 512 + 512)
                pb = tb % 2
                for c in range(8):
                    i = c % 2
                    fw.act(sq[i][:, :], xT[:, c, ts_], AF.Square, reads=[R_x[c][tb]], writes=[R_sq[i]])
                    fw.mm(PB[pb][:, :], ones_b, sq[i][:, :], start=(c == 0), stop=(c == 7),
                          reads=[R_sq[i], R_cb], writes=[R_pb[pb]])
                fw.act(lnv[pb][:, :], PB[pb][:, :], AF.Ln, reads=[R_pb[pb]], writes=[R_ln[pb]],
                       scale=1.0 / D, bias=eps_col)
                fw.act(rstd[pb][:, :], lnv[pb][:, :], AF.Exp, reads=[R_ln[pb]], writes=[R_rs[pb]], scale=-0.5)
                for c in range(8):
                    fw.stt(hT[:, c, ts_], xT[:, c, ts_], cfc(f"{which}_{l}", 1, c), rstd[pb][:, :],
                           ALU.mult, ALU.mult, reads=[R_x[c][tb], R_rs[pb], R_cf], writes=[R_h[tb]])

        def qk_post(pbank, R_pbank, wcol, rot_b, cos_ap, sin_ap, R_tab, out_ap, R_out_, tmp, scale_q=None):
            (qw, R_qw, sq, R_sq2, lnv, R_ln2, rs, R_rs2, t1, R_t1, t2, R_t2, pss, R_pss, psr, R_psr) = tmp
            fw.act(qw[:, :], pbank[:, :], AF.Copy, reads=[R_pbank, R_cf], writes=[R_qw], scale=wcol)
            fw.act(sq[:, :], pbank[:, :], AF.Square, reads=[R_pbank, R_cf], writes=[R_sq2], scale=wcol)
            fw.mm(pss[:, :], blk64_b, sq[:, :], reads=[R_sq2, R_cb], writes=[R_pss])
            fw.mm(psr[:, :], rot_b, qw[:, :], reads=[R_qw, R_cb], writes=[R_psr])
            fw.act(lnv[:, :], pss[:, :], AF.Ln, reads=[R_pss], writes=[R_ln2], scale=1.0 / 64, bias=eps_col)
            fw.act(rs[:, :], lnv[:, :], AF.Exp, reads=[R_ln2], writes=[R_rs2], scale=-0.5)
            fw.tt("dve", t1[:, :], qw[:, :], cos_ap, ALU.mult, reads=[R_qw, R_tab], writes=[R_t1])
            fw.tt("dve", t2[:, :], psr[:, :], sin_ap, ALU.mult, reads=[R_psr, R_tab], writes=[R_t2])
            fw.tt("dve", t1[:, :], t1[:, :], t2[:, :], ALU.add, reads=[R_t1, R_t2], writes=[R_t1])
            fw.tt("dve", out_ap, t1[:, :], rs[:, :], ALU.mult, reads=[R_t1, R_rs2], writes=[R_out_])

        def qk_tmp(scr, tag, pss_i, psr_i):
            return (fw.sb(f"qw{tag}", [128, 512], BF16, scr), fw.res(),
                    fw.sb(f"qsq{tag}", [128, 512], BF16, scr), fw.res(),
                    fw.sb(f"qln{tag}", [128, 512], F32, scr), fw.res(),
                    fw.sb(f"qrs{tag}", [128, 512], F32, scr), fw.res(),
                    fw.sb(f"qt1{tag}", [128, 512], F32, scr), fw.res(),
                    fw.sb(f"qt2{tag}", [128, 512], F32, scr), fw.res(),
                    PB[pss_i], R_pb[pss_i], PB[psr_i], R_pb[psr_i])

        def proj_fm(wv, R_wv, col0, m, tb, pbank, R_pbank):
            ts_ = slice(tb * 512, tb * 512 + 512)
            for k in range(8):
                fw.mm(pbank[0:m, :], wv[:, k, col0:col0 + m], hT[:, k, ts_], start=(k == 0), stop=(k == 7),
                      reads=[R_wv, R_h[tb]], writes=[R_pbank])

        def wout_update(l, o_tiles, scr):
            for half in range(2):
                views = []
                for (row0, apf, R_o) in o_tiles:
                    wv, R_wv = load_w(wout_d[l, row0:row0 + 128, half * 512:half * 512 + 512], 512, kchunks=1)
                    views.append((wv, R_wv, apf, R_o))
                for oc4 in range(4):
                    oc = half * 4 + oc4
                    for tb in range(4):
                        pb = (oc4 * 4 + tb) % 4
                        n = len(views)
                        for i, (wv, R_wv, apf, R_o) in enumerate(views):
                            fw.mm(PB[pb][:, :], wv[:, 0, oc4 * 128:oc4 * 128 + 128], apf(tb), start=(i == 0),
                                  stop=(i == n - 1), reads=[R_wv, R_o], writes=[R_pb[pb]])
                        ts_ = slice(tb * 512, tb * 512 + 512)
                        fw.tt("dve", xT[:, oc, ts_], xT[:, oc, ts_], PB[pb][:, :], ALU.add,
                              reads=[R_x[oc][tb], R_pb[pb]], writes=[R_x[oc][tb]])

        def run_attn_jobs(jobs, PT, R_pt, sbanks=(2, 3, 4, 5), LA=2):
            n = len(jobs)
            for step in range(n + LA):
                if step < n:
                    j = jobs[step]
                    sb_i = sbanks[step % len(sbanks)]
                    pt_i = step % len(PT)
                    Sb, R_S = PB[sb_i], R_pb[sb_i]
                    nq = len(j["qk"])
                    for bi, (scol, nn, k_ap, q_ap) in enumerate(j["qk"]):
                        fw.mm(Sb[:, scol:scol + nn], k_ap, q_ap, start=(bi == 0), stop=(j["mask"] is None and bi == nq - 1),
                              reads=j["reads"], writes=[R_S])
                    if j["mask"] is not None:
                        fw.mm(Sb[:, :], ident_b, j["mask"], start=False, stop=True, reads=[R_cb], writes=[R_S])
                    fw.act(PT[pt_i][:, :], Sb[:, :], AF.Exp, reads=[R_S], writes=[R_pt[pt_i]], scale=0.125)
                if step >= LA:
                    j = jobs[step - LA]
                    pt_i = (step - LA) % len(PT)
                    for (acc_ap, v_ap, scol, nn, st_, sp_) in j["pv"]:
                        fw.mm(acc_ap, v_ap, PT[pt_i][:, scol:scol + nn], start=st_, stop=sp_,
                              reads=[R_pt[pt_i], j["R_v"]], writes=[j["R_acc"]])
                    if j["fin"] is not None:
                        j["fin"]()

        def attn_finalize(acc, R_acc, hh, out_ap, R_o, tmp):
            lnd, R_lnd, rd, R_rd = tmp
            nr = slice(hh * 64, hh * 64 + 64)
            dr = slice((1 - hh) * 64, (1 - hh) * 64 + 64)
            fw.act(lnd[nr, :], acc[dr, :], AF.Ln, reads=[R_acc], writes=[R_lnd])
            fw.act(rd[nr, :], lnd[nr, :], AF.Exp, reads=[R_lnd], writes=[R_rd], scale=-1.0)
            fw.tt("dve", out_ap, acc[nr, :], rd[nr, :], ALU.mult, reads=[R_acc, R_rd], writes=[R_o])

        def mixer_A(l):
            with contextlib.ExitStack() as scr:
                qT = fw.sb("a_qT", [128, 2, S], BF16, scr)
                kT = fw.sb("a_kT", [128, 2, S], BF16, scr)
                R_q = [fw.res() for _ in range(2)]
                R_k = [fw.res() for _ in range(2)]
                kTz = fw.sb("a_kTz", [128, 2, S], BF16, scr)
                R_kz = fw.res()
                tab = fw.sb("a_tab", [128, 2, S], BF16, scr)
                R_tab = fw.res()
                fw.dma("sp", tab[:, :, :], ropeA_d[:, :, :], writes=[R_tab], sem=s_ld)
                vx = fw.sb("a_vx", [128, 3, 16, 2, 128], BF16, scr)
                R_vx = fw.res()
                oT = fw.sb("a_oT", [128, 2, S], BF16, scr)
                R_o = [fw.res() for _ in range(2)]
                PT = [fw.sb(f"a_pt{i}", [128, 512], BF16, scr) for i in range(3)]
                R_pt = [fw.res() for _ in range(3)]
                lnd = fw.sb("a_lnd", [128, 512], F32, scr)
                rd = fw.sb("a_rd", [128, 512], F32, scr)
                ftmp = (lnd, fw.res(), rd, fw.res())
                tmps = [qk_tmp(scr, "a0", 2, 4)] * 2
                wv, R_wv = load_w(win_d[l, :, P_QA:P_QA + 512], 512)
                wv2, R_wv2 = load_w(win_d[l, :, P_VA:P_VA + 272], 272)
                n = 0
                for ci in range(4):
                    isq = ci < 2
                    dst, Rd = (qT, R_q) if isq else (kT, R_k)
                    c = ci % 2
                    for tb in range(4):
                        ts_ = slice(tb * 512, tb * 512 + 512)
                        pb = n % 2
                        proj_fm(wv, R_wv, ci * 128, 128, tb, PB[pb], R_pb[pb])
                        qk_post(PB[pb], R_pb[pb], cfc(f"qna_{l}" if isq else f"kna_{l}"), cbm("ra"),
                                tab[:, 0, ts_], tab[:, 1, ts_], R_tab, dst[:, c, ts_], Rd[c], tmps[n % 2])
                        n += 1
                tap("a_qT", qT[:, :, :], R_q)
                for hp in range(2):
                    fw.op("pool", lambda e: e.memset(vx[:, :, :, 0, 64:128], 1.0), reads=[], writes=[R_vx])
                    fw.op("pool", lambda e: e.memset(vx[:, :, :, 1, 0:64], 1.0), reads=[], writes=[R_vx])
                    n = 0
                    for pat, dil in enumerate((1, 4, 16)):
                        for tile in range(16):
                            if dil == 1:
                                t0, st_ = tile * 128, 1
                            elif dil == 4:
                                r, m = tile // 4, tile % 4
                                t0, st_ = r + 512 * m, 4
                            else:
                                t0, st_ = tile, 16
                            pb = 6 + (n // 4) % 2
                            sub = n % 4
                            for k in range(8):
                                fw.mm(PB[pb][:, sub * 128:sub * 128 + 128], hT[:, k, t0:t0 + 127 * st_ + 1:st_],
                                      wv2[:, k, hp * 128:hp * 128 + 128], start=(k == 0), stop=(k == 7),
                                      reads=[R_wv2] + R_h, writes=[R_pb[pb]])
                            if sub == 3:
                                tl = tile - 3
                                src = PB[pb][:, :].rearrange("p (t h d) -> p t h d", t=4, h=2)
                                fw.cp("dve", vx[:, pat, tl:tl + 4, 0, 0:64], src[:, :, 0, :],
                                      reads=[R_pb[pb]], writes=[R_vx])
                                fw.cp("dve", vx[:, pat, tl:tl + 4, 1, 64:128], src[:, :, 1, :],
                                      reads=[R_pb[pb]], writes=[R_vx])
                            n += 1
                    fw.op("pool", lambda e: e.memset(kTz[64:128, 0, :], 0.0), writes=[R_kz])
                    fw.op("pool", lambda e: e.memset(kTz[0:64, 1, :], 0.0), writes=[R_kz])
                    fw.cp("pool", kTz[0:64, 0, :], kT[0:64, hp, :], reads=[R_k[hp]], writes=[R_kz])
                    fw.cp("pool", kTz[64:128, 1, :], kT[64:128, hp, :], reads=[R_k[hp]], writes=[R_kz])
                    jobs = []
                    for hh in range(2):
                        pr = slice(hh * 64, hh * 64 + 64)
                        c = hp
                        for B in range(4):
                            acc_i = (hh * 4 + B) % 2
                            acc, R_acc = PB[acc_i], R_pb[acc_i]
                            group = []

                            def add_job(blocks, mask_ap, group=group, acc=acc, R_acc=R_acc, c=c, hh=hh):
                                j = {"qk": [(scol, nn, k_ap, q_ap) for (scol, nn, q_ap, k_ap, pat, tile, acc_ap) in blocks],
                                     "mask": mask_ap, "ncols": 512, "reads": [R_q[c], R_kz],
                                     "pv": [[acc_ap, vx[:, pat, tile, hh, :], scol, nn, False, False]
                                            for (scol, nn, q_ap, k_ap, pat, tile, acc_ap) in blocks],
                                     "R_acc": R_acc, "R_v": R_vx, "fin": None}
                                group.append(j)

                            for half in range(2):
                                blocks = []
                                for qi in range(2):
                                    ml = half * 2 + qi
                                    m = 4 * B + ml
                                    so = qi * 256
                                    blocks.append((so, 128, qT[:, c, m * 128:m * 128 + 128],
                                                   kTz[:, hh, m * 128:m * 128 + 128], 0, m, acc[:, ml * 128:ml * 128 + 128]))
                                    if m > 0:
                                        blocks.append((so + 128, 64, qT[:, c, m * 128:m * 128 + 64],
                                                       kTz[:, hh, (m - 1) * 128:m * 128], 0, m - 1,
                                                       acc[:, ml * 128:ml * 128 + 64]))
                                    if m < 15:
                                        blocks.append((so + 192, 64, qT[:, c, m * 128 + 64:m * 128 + 128],
                                                       kTz[:, hh, (m + 1) * 128:(m + 2) * 128], 0, m + 1,
                                                       acc[:, ml * 128 + 64:ml * 128 + 128]))
                                add_job(blocks, cbm("maskA", 512))
                            for half in range(2):
                                blocks = []
                                for qi in range(2):
                                    r = half * 2 + qi
                                    so = qi * 256
                                    b0 = 512 * B + r

                                    def kt(mm_, r=r, hh=hh):
                                        return kTz[:, hh, 512 * mm_ + r:512 * mm_ + 512:4]
                                    blocks.append((so, 128, qT[:, c, b0:512 * B + 512:4], kt(B), 1, r * 4 + B,
                                                   acc[:, r:512:4]))
                                    if B > 0:
                                        blocks.append((so + 128, 64, qT[:, c, b0:512 * B + 256:4], kt(B - 1), 1,
                                                       r * 4 + B - 1, acc[:, r:256:4]))
                                    if B < 3:
                                        blocks.append((so + 192, 64, qT[:, c, b0 + 256:512 * B + 512:4], kt(B + 1), 1,
                                                       r * 4 + B + 1, acc[:, 256 + r:512:4]))
                                add_job(blocks, cbm("maskA", 512))
                            blocks = []
                            for b in range(16):
                                blocks.append((b * 32, 32, qT[:, c, 512 * B + b:512 * B + 512:16],
                                               kTz[:, hh, b:S:16], 2, b, acc[:, b:512:16]))
                            add_job(blocks, cbm("mask16", 512, 512 * B))
                            group[0]["pv"][0][4] = True
                            group[-1]["pv"][-1][5] = True
                            group[-1]["fin"] = (lambda acc=acc, R_acc=R_acc, hh=hh, pr=pr, c=c, B=B:
                                                attn_finalize(acc, R_acc, hh, oT[pr, c, B * 512:B * 512 + 512], R_o[c], ftmp))
                            jobs += group
                    run_attn_jobs(jobs, PT, R_pt)
                tap("a_oT", oT[:, :, :], R_o)
                wout_update(l, [(c * 128, (lambda tb, c=c: oT[:, c, tb * 512:tb * 512 + 512]), R_o[c]) for c in range(2)], scr)
            fw.barrier()

        def mixer_C(l):
            with contextlib.ExitStack() as scr:
                qT = fw.sb("c_qT", [128, 2, S], BF16, scr)
                kT = fw.sb("c_kT", [128, S], BF16, scr)
                R_q = [fw.res() for _ in range(2)]
                R_k = fw.res()
                kTz = fw.sb("c_kTz", [128, 2, S], BF16, scr)
                R_kz = fw.res()
                tab = fw.sb("c_tab", [128, 2, S], BF16, scr)
                R_tab = fw.res()
                fw.dma("sp", tab[:, :, :], ropeC_d[:, :, :], writes=[R_tab], sem=s_ld)
                vx = fw.sb("c_vx", [128, 16, 2, 128], BF16, scr)
                R_vx = fw.res()
                oT = fw.sb("c_oT", [128, 2, S], BF16, scr)
                R_o = [fw.res() for _ in range(2)]
                PT = [fw.sb(f"c_pt{i}", [128, 512], BF16, scr) for i in range(3)]
                R_pt = [fw.res() for _ in range(3)]
                lnd = fw.sb("c_lnd", [128, 512], F32, scr)
                rd = fw.sb("c_rd", [128, 512], F32, scr)
                ftmp = (lnd, fw.res(), rd, fw.res())
                tmps = [qk_tmp(scr, "c0", 2, 4)] * 2
                wv, R_wv = load_w(win_d[l, :, P_QC:P_QC + 512], 512)
                n = 0
                for ci in range(3):
                    for tb in range(4):
                        ts_ = slice(tb * 512, tb * 512 + 512)
                        pb = n % 2
                        proj_fm(wv, R_wv, ci * 128, 128, tb, PB[pb], R_pb[pb])
                        if ci < 2:
                            dst, Rd, wn = qT[:, ci, ts_], R_q[ci], f"qnc_{l}"
                        else:
                            dst, Rd, wn = kT[:, ts_], R_k, f"knc_{l}"
                        qk_post(PB[pb], R_pb[pb], cfc(wn), cbm("rc"), tab[:, 0, ts_], tab[:, 1, ts_], R_tab,
                                dst, Rd, tmps[n % 2])
                        n += 1
                tap("c_qT", qT[:, :, :], R_q)
                fw.op("pool", lambda e: e.memset(kTz[64:128, 0, :], 0.0), writes=[R_kz])
                fw.op("pool", lambda e: e.memset(kTz[0:64, 1, :], 0.0), writes=[R_kz])
                fw.cp("pool", kTz[0:64, 0, :], kT[0:64, :], reads=[R_k], writes=[R_kz])
                fw.cp("pool", kTz[64:128, 1, :], kT[64:128, :], reads=[R_k], writes=[R_kz])
                fw.op("pool", lambda e: e.memset(vx[:, :, 0, 64:128], 1.0), reads=[], writes=[R_vx])
                fw.op("pool", lambda e: e.memset(vx[:, :, 1, 0:64], 1.0), reads=[], writes=[R_vx])
                for tile in range(16):
                    pb = 6 + (tile // 4) % 2
                    sub = tile % 4
                    for k in range(8):
                        fw.mm(PB[pb][:, sub * 128:sub * 128 + 128], hT[:, k, tile * 128:tile * 128 + 128],
                              wv[:, k, 384:512], start=(k == 0), stop=(k == 7), reads=[R_wv] + R_h, writes=[R_pb[pb]])
                    if sub == 3:
                        tl = tile - 3
                        src = PB[pb][:, :].rearrange("p (t h d) -> p t h d", t=4, h=2)
                        fw.cp("dve", vx[:, tl:tl + 4, 0, 0:64], src[:, :, 0, :], reads=[R_pb[pb]], writes=[R_vx])
                        fw.cp("dve", vx[:, tl:tl + 4, 1, 64:128], src[:, :, 1, :], reads=[R_pb[pb]], writes=[R_vx])
                jobs = []
                for c in range(2):
                    for hh in range(2):
                        pr = slice(hh * 64, hh * 64 + 64)
                        for B in range(4):
                            acc_i = (c * 8 + hh * 4 + B) % 2
                            acc, R_acc = PB[acc_i], R_pb[acc_i]
                            for kt in range(16):
                                j = {"qk": [(0, 512, kTz[:, hh, kt * 128:kt * 128 + 128], qT[:, c, B * 512:B * 512 + 512])],
                                     "mask": None, "ncols": 512, "reads": [R_q[c], R_kz],
                                     "pv": [[acc[:, :], vx[:, kt, hh, :], 0, 512, kt == 0, kt == 15]],
                                     "R_acc": R_acc, "R_v": R_vx, "fin": None}
                                if kt == 15:
                                    j["fin"] = (lambda acc=acc, R_acc=R_acc, hh=hh, pr=pr, c=c, B=B:
                                                attn_finalize(acc, R_acc, hh, oT[pr, c, B * 512:B * 512 + 512], R_o[c], ftmp))
                                jobs.append(j)
                run_attn_jobs(jobs, PT, R_pt)
                tap("c_oT", oT[:, :, :], R_o)
                wout_update(l, [(768 + c * 128, (lambda tb, c=c: oT[:, c, tb * 512:tb * 512 + 512]), R_o[c])
                                for c in range(2)], scr)
            fw.barrier()


        def mixer_B(l):
            with contextlib.ExitStack() as scr:
                def tk(name, n=8):
                    return fw.sb("b_" + name, [128, 16, n], F32, scr)
                ab_tok = tk("ab", 16)
                beta = tk("beta"); nbeta = tk("nbeta"); g_tok = tk("g"); gc = tk("gc"); ngc = tk("ngc")
                egc = tk("egc"); kdec = tk("kdec")
                gam = fw.sb("b_gam", [128, 16, 2, 8], F32, scr)
                ea = fw.sb("b_ea", [128, 8], F32, scr)
                R_ts = fw.res()
                wv, R_wv = load_w(win_d[l, :, P_AB:P_AB + 16], 16)
                for tile in range(16):
                    for k in range(8):
                        fw.mm(PB[0][:, tile * 16:tile * 16 + 16], hT[:, k, tile * 128:tile * 128 + 128], wv[:, k, 0:16],
                              start=(k == 0), stop=(k == 7), reads=[R_wv] + R_h, writes=[R_pb[0]])
                fw.cp("dve", ab_tok[:, :, :], PB[0][:, 0:256].rearrange("p (t n) -> p t n", t=16),
                      reads=[R_pb[0]], writes=[R_ts])
                fw.act(beta[:, :, :], ab_tok[:, :, 8:16], AF.Tanh, reads=[R_ts], writes=[R_ts], scale=0.5)
                fw.ts("dve", beta[:, :, :], beta[:, :, :], 0.5, 0.5, ALU.mult, ALU.add, reads=[R_ts], writes=[R_ts])
                fw.ts("dve", nbeta[:, :, :], beta[:, :, :], -1.0, None, ALU.mult, reads=[R_ts], writes=[R_ts])
                dtb_b = cfc(f"dtb_{l}", 8).unsqueeze(1).broadcast_to([128, 16, 8])
                fw.tt("dve", g_tok[:, :, :], ab_tok[:, :, 0:8], dtb_b, ALU.add, reads=[R_ts, R_cf], writes=[R_ts])
                fw.act(g_tok[:, :, :], g_tok[:, :, :], AF.Exp, reads=[R_ts], writes=[R_ts])
                fw.act(g_tok[:, :, :], g_tok[:, :, :], AF.Ln, reads=[R_ts], writes=[R_ts], bias=cfc("ones", 1))
                fw.act(ea[:, :], cfc(f"alog_{l}", 8), AF.Exp, reads=[R_cf], writes=[R_ts])
                fw.stt(g_tok[:, :, :], g_tok[:, :, :], -1.0, ea[:, :].unsqueeze(1).broadcast_to([128, 16, 8]),
                       ALU.mult, ALU.mult, reads=[R_ts], writes=[R_ts])
                tap("b_g", g_tok[:, :, :], [R_ts])
                tap("b_beta", beta[:, :, :], [R_ts])
                for tile in range(16):
                    for d in range(2):
                        fw.mm(PB[1][:, tile * 8 + d * 4:tile * 8 + d * 4 + 4], cfc("uuf" if d == 0 else "uub", 128),
                              g_tok[:, tile, d * 4:d * 4 + 4], reads=[R_ts, R_cf], writes=[R_pb[1]])
                fw.cp("dve", gc[:, :, :], PB[1][:, 0:128].rearrange("p (t n) -> p t n", t=16), reads=[R_pb[1]], writes=[R_ts])
                fw.ts("dve", ngc[:, :, :], gc[:, :, :], -1.0, None, ALU.mult, reads=[R_ts], writes=[R_ts])
                fw.act(egc[:, :, :], gc[:, :, :], AF.Exp, reads=[R_ts], writes=[R_ts])
                for tile in range(16):
                    for d in range(2):
                        fw.mm(PB[2][:, tile * 8 + d * 4:tile * 8 + d * 4 + 4], cfc("slf" if d == 0 else "slb", 128),
                              gc[:, tile, d * 4:d * 4 + 4], reads=[R_ts, R_cf], writes=[R_pb[2]])
                        for cc in range(2):
                            o_ = (tile * 2 + cc) * 8 + d * 4
                            fw.mm(PB[3][:, o_:o_ + 4], cfc(f"sla_{d}{cc}", 128), gc[:, tile, d * 4:d * 4 + 4],
                                  reads=[R_ts, R_cf], writes=[R_pb[3]])
                fw.tt("dve", kdec[:, :, :], PB[2][:, 0:128].rearrange("p (t n) -> p t n", t=16), gc[:, :, :], ALU.subtract,
                      reads=[R_pb[2], R_ts], writes=[R_ts])
                fw.act(kdec[:, :, :], kdec[:, :, :], AF.Exp, reads=[R_ts], writes=[R_ts])
                fw.act(gam[:, :, :, :], PB[3][:, 0:256].rearrange("p (t c n) -> p t c n", t=16, c=2), AF.Exp,
                       reads=[R_pb[3]], writes=[R_ts])
                tap("b_gc", gc[:, :, :], [R_ts])

                for pp in range(2):
                    with contextlib.ExitStack() as sp_:
                        bq = fw.sb("b_q", [128, 2, S], BF16, sp_)
                        bk = fw.sb("b_k", [128, 2, S], BF16, sp_)
                        K_tok = fw.sb("b_Kt", [128, 16, 2, 128], BF16, sp_)
                        V_tok = fw.sb("b_Vt", [128, 16, 2, 128], BF16, sp_)
                        R_bq = fw.res(); R_bk = fw.res(); R_Kt = fw.res(); R_Vt = fw.res()
                        with contextlib.ExitStack() as s1:
                            bv = fw.sb("b_v", [128, 2, S], BF16, s1)
                            raw = fw.sb("b_raw", [128, S + 4], BF16, s1)
                            dg = fw.sb("b_dg", [128, 5, 128], BF16, s1)
                            sq = fw.sb("b_sq", [128, 512], BF16, s1)
                            lnv = fw.sb("b_ln", [128, 512], F32, s1)
                            rs = fw.sb("b_rs", [128, 512], F32, s1)
                            R_bv = fw.res(); R_raw = fw.res(); R_dg = fw.res(); R_sq = fw.res(); R_ln = fw.res(); R_rs = fw.res()
                            fw.op("pool", lambda e: e.memset(raw[:, 0:2], 0.0), writes=[R_raw])
                            fw.op("pool", lambda e: e.memset(raw[:, S + 2:S + 4], 0.0), writes=[R_raw])
                            wq, R_wq = load_w(win_d[l, :, P_B + pp * 1024:P_B + pp * 1024 + 512], 512)
                            wz, R_wz = load_w(win_d[l, :, P_B + pp * 1024 + 512:P_B + pp * 1024 + 1024], 512)
                            n = 0
                            for kind in range(3):
                                for hh in range(2):
                                    wsrc, R_ws, col0 = (wq, R_wq, kind * 256 + hh * 128) if kind < 2 else (wz, R_wz, hh * 128)
                                    dst, R_dst = ((bq, R_bq), (bk, R_bk), (bv, R_bv))[kind]
                                    cch = kind * 4 + 2 * pp + hh
                                    for k in range(5):
                                        fw.ts("dve", dg[:, k, :], ident_b, cfc(f"cw_{l}", 1, cch * 5 + k), None, ALU.mult,
                                              reads=[R_cb, R_cf], writes=[R_dg])
                                    for tb in range(4):
                                        pb = n % 2
                                        n += 1
                                        proj_fm(wsrc, R_ws, col0, 128, tb, PB[pb], R_pb[pb])
                                        fw.cp("act", raw[:, 2 + tb * 512:2 + tb * 512 + 512], PB[pb][:, :],
                                              reads=[R_pb[pb]], writes=[R_raw])
                                    for tb in range(4):
                                        pb = 2 + n % 2
                                        n += 1
                                        ts_ = slice(tb * 512, tb * 512 + 512)
                                        for k in range(5):
                                            fw.mm(PB[pb][:, :], dg[:, k, :], raw[:, tb * 512 + k:tb * 512 + k + 512],
                                                  start=(k == 0), stop=(k == 4), reads=[R_dg, R_raw], writes=[R_pb[pb]])
                                        fw.act(dst[:, hh, ts_], PB[pb][:, :], AF.Silu, reads=[R_pb[pb]], writes=[R_dst])
                            for kind in range(2):
                                dst, R_dst = ((bq, R_bq), (bk, R_bk))[kind]
                                for hh in range(2):
                                    for tb in range(4):
                                        pb = 4 + n % 2
                                        n += 1
                                        ts_ = slice(tb * 512, tb * 512 + 512)
                                        fw.act(sq[:, :], dst[:, hh, ts_], AF.Square, reads=[R_dst], writes=[R_sq])
                                        fw.mm(PB[pb][:, :], ones_b, sq[:, :], reads=[R_sq, R_cb], writes=[R_pb[pb]])
                                        fw.act(lnv[:, :], PB[pb][:, :], AF.Ln, reads=[R_pb[pb]], writes=[R_ln], bias=eps_col)
                                        fw.act(rs[:, :], lnv[:, :], AF.Exp, reads=[R_ln], writes=[R_rs], scale=-0.5)
                                        fw.stt(dst[:, hh, ts_], dst[:, hh, ts_], (128.0 ** -0.5) if kind == 0 else 1.0, rs[:, :],
                                               ALU.mult, ALU.mult, reads=[R_dst, R_rs], writes=[R_dst])
                            tap("b_q", bq[:, :, :], [R_bq], dst=(tap_d["b_q"][:, 2 * pp:2 * pp + 2, :] if "b_q" in tap_d else None))
                            tap("b_k", bk[:, :, :], [R_bk], dst=(tap_d["b_k"][:, 2 * pp:2 * pp + 2, :] if "b_k" in tap_d else None))
                            tap("b_v", bv[:, :, :], [R_bv], dst=(tap_d["b_v"][:, 2 * pp:2 * pp + 2, :] if "b_v" in tap_d else None))
                            for src, R_src, dstt, R_dt in ((bk, R_bk, K_tok, R_Kt), (bv, R_bv, V_tok, R_Vt)):
                                for hh in range(2):
                                    for t4 in range(4):
                                        pb = 6 + n % 2
                                        n += 1
                                        pbv = PB[pb][:, :].bitcast(BF16)
                                        for i in range(4):
                                            tile = t4 * 4 + i
                                            fw.tr(pbv[:, i * 128:i * 128 + 128], src[:, hh, tile * 128:tile * 128 + 128], ident_b,
                                                  reads=[R_src, R_cb], writes=[R_pb[pb]])
                                        fw.cp("act", dstt[:, t4 * 4:t4 * 4 + 4, hh, :],
                                              pbv[:, 0:512].rearrange("p (t n) -> p t n", t=4), reads=[R_pb[pb]], writes=[R_dt])
                        fw.barrier()
                        obuf = fw.sb("b_ob", [128, 2, S], F32, sp_)
                        R_ob = fw.res()
                        fw.op("pool", lambda e: e.memset(obuf[:, :, :], 0.0), writes=[R_ob])
                        with contextlib.ExitStack() as s2:
                            def t4(name, dt):
                                return fw.sb("b_" + name, [128, 4, 128], dt, s2), fw.res()
                            KKs, R_KKs = t4("KKs", BF16)
                            GU, R_GU = t4("GU", F32)
                            EG, R_EG = t4("EG", BF16)
                            Et, R_Et = t4("Et", BF16)
                            X, R_X = t4("X", BF16)
                            Yb = [t4(f"Y{i}", BF16) for i in range(2)]
                            Zb = [(X, R_X), t4("Z1", BF16)]
                            Pbb = [t4(f"Pb{i}", BF16) for i in range(2)]
                            Keg, R_Keg = t4("Keg", BF16)
                            qkT2 = [t4(f"qkT{i}", BF16) for i in range(2)]
                            Ktil2 = [t4(f"Ktil{i}", BF16) for i in range(2)]
                            upp2 = [t4("upp0", F32)] * 2
                            wT2 = [t4(f"wT{i}", BF16) for i in range(2)]
                            qtT2 = [t4(f"qtT{i}", BF16) for i in range(2)]
                            S32 = fw.sb("b_S32", [128, 4, 128], F32, s2)
                            Sb = fw.sb("b_Sb", [128, 4, 128], BF16, s2)
                            vnew2 = [fw.sb(f"b_vnew{i}", [128, 4, 128], BF16, s2) for i in range(2)]
                            R_S32 = [fw.res(), fw.res()]; R_Sb = [fw.res(), fw.res()]; R_vn = [fw.res(), fw.res()]
                            R_psv = [R_pb[6], R_pb[7]]; R_pss = [R_pb[6], R_pb[7]]; R_po = [R_pb[5], R_pb[5]]
                            fw.op("pool", lambda e: e.memset(S32[:, :, :], 0.0), writes=R_S32)
                            fw.op("pool", lambda e: e.memset(Sb[:, :, :], 0.0), writes=R_Sb)
                            fw.op("pool", lambda e: e.memset(vnew2[0][:, :, :], 0.0), writes=R_vn)
                            fw.op("pool", lambda e: e.memset(vnew2[1][:, :, :], 0.0), writes=R_vn)
                            v4 = lambda i: PB[i][:, :].rearrange("p (u n) -> p u n", u=4)
                            strict_b4 = cbm("strict").unsqueeze(1).broadcast_to([128, 4, 128])
                            ident_b4 = ident_b.unsqueeze(1).broadcast_to([128, 4, 128])
                            ucol = lambda d, hh: d * 4 + 2 * pp + hh

                            def prep(n):
                                Td = (n, 15 - n)
                                tsl = lambda d: slice(Td[d] * 128, Td[d] * 128 + 128)
                                qkT, R_qkT = qkT2[n % 2]
                                Ktil, R_Ktil = Ktil2[n % 2]
                                upp, R_upp = upp2[n % 2]
                                wT, R_wT = wT2[n % 2]
                                qtT, R_qtT = qtT2[n % 2]
                                for d in range(2):
                                    for hh in range(2):
                                        u = d * 2 + hh
                                        fw.mm(PB[0][:, u * 128:u * 128 + 128], bk[:, hh, tsl(d)], bk[:, hh, tsl(d)],
                                              reads=[R_bk], writes=[R_pb[0]])
                                        fw.mm(PB[1][:, u * 128:u * 128 + 128], bk[:, hh, tsl(d)], bq[:, hh, tsl(d)],
                                              reads=[R_bk, R_bq], writes=[R_pb[1]])
                                for d in range(2):
                                    uu = cfc("uuf" if d == 0 else "uub", 128).unsqueeze(1).broadcast_to([128, 2, 128])
                                    gb_ = g_tok[:, Td[d], ucol(d, 0):ucol(d, 0) + 2].unsqueeze(2).broadcast_to([128, 2, 128])
                                    fw.tt("pool", GU[:, 2 * d:2 * d + 2, :], uu, gb_, ALU.mult, reads=[R_cf, R_ts], writes=[R_GU])
                                yield
                                fw.tt("dve", KKs[:, :, :], v4(0), strict_b4, ALU.mult, reads=[R_pb[0], R_cb], writes=[R_KKs])
                                fw.mm(PB[2][:, :], ones_f, GU[:, :, :].rearrange("p u n -> p (u n)"), start=True, stop=False,
                                      reads=[R_GU, R_cf], writes=[R_pb[2]])
                                yield
                                fw.act(EG[:, :, :], v4(2), AF.Exp, reads=[R_pb[2]], writes=[R_EG])
                                fw.mm(PB[2][:, :], ident_b, cbm("gmask", 512), start=False, stop=True, reads=[R_cb], writes=[R_pb[2]])
                                yield
                                for d in range(2):
                                    for hh in range(2):
                                        u = d * 2 + hh
                                        c_ = ucol(d, hh)
                                        fw.act(Et[:, u, :], PB[2][:, u * 128:u * 128 + 128], AF.Exp, reads=[R_pb[2], R_ts],
                                               writes=[R_Et], bias=ngc[:, Td[d], c_:c_ + 1])
                                for d in range(2):
                                    c0 = ucol(d, 0)
                                    eb = egc[:, Td[d], c0:c0 + 2].unsqueeze(2).broadcast_to([128, 2, 128])
                                    kb_ = kdec[:, Td[d], c0:c0 + 2].unsqueeze(2).broadcast_to([128, 2, 128])
                                    fw.tt("pool", Keg[:, 2 * d:2 * d + 2, :], K_tok[:, Td[d], :, :], eb, ALU.mult, reads=[R_Kt, R_ts], writes=[R_Keg])
                                    fw.tt("pool", Ktil[:, 2 * d:2 * d + 2, :], K_tok[:, Td[d], :, :], kb_, ALU.mult, reads=[R_Kt, R_ts], writes=[R_Ktil])
                                    fw.tt("pool", qtT[:, 2 * d:2 * d + 2, :], bq[:, :, tsl(d)], EG[:, 2 * d:2 * d + 2, :], ALU.mult,
                                          reads=[R_bq, R_EG], writes=[R_qtT])
                                yield
                                for d in range(2):
                                    for hh in range(2):
                                        u = d * 2 + hh
                                        c_ = ucol(d, hh)
                                        fw.stt(X[:, u, :], Et[:, u, :], beta[:, Td[d], c_:c_ + 1], KKs[:, u, :], ALU.mult, ALU.mult,
                                               reads=[R_Et, R_ts, R_KKs], writes=[R_X])
                                fw.tt("dve", qkT[:, :, :], v4(1), Et[:, :, :], ALU.mult, reads=[R_pb[1], R_Et], writes=[R_qkT])
                                yield
                                pbv3 = PB[3][:, :].bitcast(BF16)
                                for u in range(4):
                                    fw.tr(pbv3[:, u * 128:u * 128 + 128], X[:, u, :], ident_b, reads=[R_X, R_cb], writes=[R_pb[3]])
                                (Yc, R_Yc), (Zc, R_Zc) = Yb[0], (X, R_X)
                                Pc, R_Pc = Pbb[0]
                                fw.tt("dve", Pc[:, :, :], ident_b4, X[:, :, :], ALU.subtract, reads=[R_cb, R_X], writes=[R_Pc])
                                yield
                                fw.cp("act", Yc[:, :, :], pbv3[:, 0:512].rearrange("p (u n) -> p u n", u=4), reads=[R_pb[3]], writes=[R_Yc])
                                yield
                                for s_ in range(1, 6):
                                    Yn, R_Yn = Yb[s_ % 2]
                                    Zn, R_Zn = Zb[s_ % 2]
                                    Pn, R_Pn = Pbb[s_ % 2]
                                    for u in range(4):
                                        fw.mm(PB[3][:, u * 128:u * 128 + 128], Zc[:, u, :], Yc[:, u, :], reads=[R_Zc, R_Yc], writes=[R_pb[3]])
                                    if s_ <= 4:
                                        for u in range(4):
                                            fw.mm(PB[4][:, u * 128:u * 128 + 128], Yc[:, u, :], Zc[:, u, :], reads=[R_Zc, R_Yc], writes=[R_pb[4]])
                                    yield
                                    fw.cp("act", Yn[:, :, :], v4(3), reads=[R_pb[3]], writes=[R_Yn])
                                    if s_ <= 4:
                                        fw.cp("dve", Zn[:, :, :], v4(4), reads=[R_pb[4]], writes=[R_Zn])
                                    yield
                                    for u in range(4):
                                        fw.mm(PB[0][:, u * 128:u * 128 + 128], ident_b, Pc[:, u, :], start=True, stop=False,
                                              reads=[R_cb, R_Pc], writes=[R_pb[0]])
                                        fw.mm(PB[0][:, u * 128:u * 128 + 128], Yn[:, u, :], Pc[:, u, :], start=False, stop=True,
                                              reads=[R_Yn, R_Pc], writes=[R_pb[0]])
                                    yield
                                    fw.cp("act" if s_ % 2 else "dve", Pn[:, :, :], v4(0), reads=[R_pb[0]], writes=[R_Pn])
                                    (Yc, R_Yc), (Zc, R_Zc), (Pc, R_Pc) = (Yn, R_Yn), (Zn, R_Zn), (Pn, R_Pn)
                                    yield
                                for d in range(2):
                                    for hh in range(2):
                                        u = d * 2 + hh
                                        fw.mm(PB[0][:, u * 128:u * 128 + 128], Pc[:, u, :], V_tok[:, Td[d], hh, :], reads=[R_Pc, R_Vt], writes=[R_pb[0]])
                                        fw.mm(PB[1][:, u * 128:u * 128 + 128], Keg[:, u, :], Pc[:, u, :], reads=[R_Pc, R_Keg], writes=[R_pb[1]])
                                yield
                                for d in range(2):
                                    c0 = ucol(d, 0)
                                    bb_ = beta[:, Td[d], c0:c0 + 2].unsqueeze(2).broadcast_to([128, 2, 128])
                                    fw.tt("dve", upp[:, 2 * d:2 * d + 2, :], v4(0)[:, 2 * d:2 * d + 2, :], bb_, ALU.mult,
                                          reads=[R_pb[0], R_ts], writes=[R_upp])
                                fw.cp("act", wT[:, :, :], v4(1), reads=[R_pb[1]], writes=[R_wT])
                                yield

                            def scan(n):
                                Td = (n, 15 - n)
                                tsl = lambda d: slice(Td[d] * 128, Td[d] * 128 + 128)
                                qkT, R_qkT = qkT2[n % 2]
                                Ktil, R_Ktil = Ktil2[n % 2]
                                upp, R_upp = upp2[n % 2]
                                wT, R_wT = wT2[n % 2]
                                qtT, R_qtT = qtT2[n % 2]
                                for sub in range(2):
                                    info = []
                                    for d in range(2):
                                        cc = sub if d == 0 else 1 - sub
                                        info.append((d, cc, slice(cc * 64, cc * 64 + 64)))
                                    for (d, cc, rows) in info:
                                        for hh in range(2):
                                            u = d * 2 + hh
                                            fw.mm(PB[6 + d][:, hh * 128:hh * 128 + 128], wT[:, u, :], Sb[:, u, :], reads=[R_wT, R_Sb[d]], writes=[R_psv[d]])
                                    yield
                                    for (d, cc, rows) in info:
                                        for hh in range(2):
                                            u = d * 2 + hh
                                            c_ = ucol(d, hh)
                                            fw.stt(vnew2[cc][rows, u, :], PB[6 + d][rows, hh * 128:hh * 128 + 128], nbeta[rows, Td[d], c_:c_ + 1],
                                                   upp[rows, u, :], ALU.mult, ALU.add, reads=[R_psv[d], R_ts, R_upp], writes=[R_vn[d]])
                                    yield
                                    for (d, cc, rows) in info:
                                        for hh in range(2):
                                            u = d * 2 + hh
                                            fw.mm(PB[6 + d][:, 256 + hh * 128:256 + hh * 128 + 128], Ktil[:, u, :], vnew2[cc][:, u, :],
                                                  reads=[R_Ktil, R_vn[d]], writes=[R_pss[d]])
                                        for hh in range(2):
                                            u = d * 2 + hh
                                            oc_ = u * 128 + cc * 64
                                            fw.mm(PB[5][:, oc_:oc_ + 64], Sb[:, u, :], qtT[:, u, rows], start=True, stop=False,
                                                  reads=[R_Sb[d], R_qtT], writes=[R_po[d]])
                                            fw.mm(PB[5][:, oc_:oc_ + 64], vnew2[cc][:, u, :], qkT[:, u, rows], start=False, stop=True,
                                                  reads=[R_vn[d], R_qkT], writes=[R_po[d]])
                                    yield
                                    for (d, cc, rows) in info:
                                        for hh in range(2):
                                            u = d * 2 + hh
                                            c_ = ucol(d, hh)
                                            fw.stt(S32[:, u, :], S32[:, u, :], gam[:, Td[d], cc, c_:c_ + 1], PB[6 + d][:, 256 + hh * 128:256 + hh * 128 + 128],
                                                   ALU.mult, ALU.add, reads=[R_S32[d], R_ts, R_pss[d]], writes=[R_S32[d]])
                                    yield
                                    for (d, cc, rows) in info:
                                        fw.cp("pool", Sb[:, 2 * d:2 * d + 2, :], S32[:, 2 * d:2 * d + 2, :], reads=[R_S32[d]], writes=[R_Sb[d]])
                                    yield
                                for d in range(2):
                                    fw.tt("dve", obuf[:, :, tsl(d)], obuf[:, :, tsl(d)], v4(5)[:, 2 * d:2 * d + 2, :], ALU.add,
                                          reads=[R_ob, R_po[d]], writes=[R_ob])
                                yield

                            def run_interleaved(gens, weights):
                                gens = [g for g in gens if g is not None]
                                alive = list(range(len(gens)))
                                while alive:
                                    for gi in list(alive):
                                        for _ in range(weights[gi]):
                                            try:
                                                next(gens[gi])
                                            except StopIteration:
                                                alive.remove(gi)
                                                break

                            run_interleaved([prep(0)], [1])
                            for n in range(16):
                                run_interleaved([prep(n + 1) if n < 15 else None, scan(n)], [3, 1])
                        fw.barrier()
                        tap("b_o", obuf[:, :, :], [R_ob], dst=(tap_d["b_o"][:, 2 * pp:2 * pp + 2, :] if "b_o" in tap_d else None))
                        with contextlib.ExitStack() as s3:
                            oT = fw.sb("b_oT", [128, 2, S], BF16, s3)
                            R_o = [fw.res(), fw.res()]
                            zs = fw.sb("b_zs", [128, 512], F32, s3)
                            sq = fw.sb("b_sq3", [128, 512], BF16, s3)
                            lnv = fw.sb("b_ln3", [128, 512], F32, s3)
                            rs = fw.sb("b_rs3", [128, 512], F32, s3)
                            t1 = fw.sb("b_t13", [128, 512], F32, s3)
                            R_zs = fw.res(); R_sq = fw.res(); R_ln = fw.res(); R_rs = fw.res(); R_t1 = fw.res()
                            wz, R_wz = load_w(win_d[l, :, P_B + pp * 1024 + 768:P_B + pp * 1024 + 1024], 256)
                            n = 0
                            for hh in range(2):
                                for tb in range(4):
                                    ts_ = slice(tb * 512, tb * 512 + 512)
                                    pz = n % 2
                                    pn = 2 + n % 2
                                    n += 1
                                    proj_fm(wz, R_wz, hh * 128, 128, tb, PB[pz], R_pb[pz])
                                    fw.act(zs[:, :], PB[pz][:, :], AF.Silu, reads=[R_pb[pz]], writes=[R_zs])
                                    fw.act(sq[:, :], obuf[:, hh, ts_], AF.Square, reads=[R_ob], writes=[R_sq])
                                    fw.mm(PB[pn][:, :], ones_b, sq[:, :], reads=[R_sq, R_cb], writes=[R_pb[pn]])
                                    fw.act(lnv[:, :], PB[pn][:, :], AF.Ln, reads=[R_pb[pn]], writes=[R_ln], scale=1.0 / 128, bias=eps_col)
                                    fw.act(rs[:, :], lnv[:, :], AF.Exp, reads=[R_ln], writes=[R_rs], scale=-0.5)
                                    fw.stt(t1[:, :], obuf[:, hh, ts_], cfc(f"onb_{l}"), rs[:, :], ALU.mult, ALU.mult,
                                           reads=[R_ob, R_cf, R_rs], writes=[R_t1])
                                    fw.tt("dve", oT[:, hh, ts_], t1[:, :], zs[:, :], ALU.mult, reads=[R_t1, R_zs], writes=[R_o[hh]])
                            tap("b_oT", oT[:, :, :], R_o, dst=(tap_d["b_oT"][:, 2 * pp:2 * pp + 2, :] if "b_oT" in tap_d else None))
                            wout_update(l, [(256 + (2 * pp + hh) * 128, (lambda tb, hh=hh: oT[:, hh, tb * 512:tb * 512 + 512]), R_o[hh])
                                            for hh in range(2)], s3)
                        fw.barrier()
            fw.barrier()

        def ffn(l):
            with contextlib.ExitStack() as scr0:
                rmsnorm_fm(l, "n2", scr0)
            fw.barrier()
            with contextlib.ExitStack() as scr:
                NH = 12
                actT = fw.sb("f_act", [128, NH, S], BF16, scr)
                sg = [fw.sb(f"f_sg{i}", [128, 512], F32, scr) for i in range(2)]
                R_sg = [fw.res() for _ in range(2)]
                n = 0
                for (f0, nf) in ((0, 12), (12, 10)):
                    R_a = [[fw.res() for _ in range(4)] for _ in range(nf)]
                    for g in range(nf // 2):
                        gg = f0 // 2 + g
                        wv, R_wv = load_w(wgu_d[l, :, gg * 512:gg * 512 + 512], 512)
                        for fi in range(2):
                            f = g * 2 + fi
                            for tb in range(4):
                                pg = (n % 2) * 2
                                pu = pg + 1
                                proj_fm(wv, R_wv, fi * 256, 128, tb, PB[pg], R_pb[pg])
                                proj_fm(wv, R_wv, fi * 256 + 128, 128, tb, PB[pu], R_pb[pu])
                                si = n % 2
                                fw.act(sg[si][:, :], PB[pg][:, :], AF.Silu, reads=[R_pb[pg]], writes=[R_sg[si]])
                                fw.tt("dve", actT[:, f, tb * 512:tb * 512 + 512], sg[si][:, :], PB[pu][:, :], ALU.mult,
                                      reads=[R_sg[si], R_pb[pu]], writes=[R_a[f][tb]])
                                n += 1
                    for oc in range(8):
                        wv, R_wv = load_w(wdn_d[l, f0 * 128:(f0 + nf) * 128, oc * 128:oc * 128 + 128], 128, kchunks=nf)
                        for tb in range(4):
                            pb = 4 + (oc * 4 + tb) % 4
                            ts_ = slice(tb * 512, tb * 512 + 512)
                            for f in range(nf):
                                fw.mm(PB[pb][:, :], wv[:, f, :], actT[:, f, ts_], start=(f == 0), stop=(f == nf - 1),
                                      reads=[R_wv, R_a[f][tb]], writes=[R_pb[pb]])
                            fw.tt("dve", xT[:, oc, ts_], xT[:, oc, ts_], PB[pb][:, :], ALU.add,
                                  reads=[R_x[oc][tb], R_pb[pb]], writes=[R_x[oc][tb]])
                    fw.barrier()
            fw.barrier()

        eps_t = fw.sb("eps_t", [128, 1], F32)
        R_eps = fw.res()
        fw.op("pool", lambda e: e.memset(eps_t[:, :], EPS), writes=[R_eps])
        eps_col = eps_t[:, 0:1]
        fw.barrier()

        for l in range(nl):
            with contextlib.ExitStack() as scr:
                rmsnorm_fm(l, "n1", scr)
            if l == 0:
                tap("hT", hT[:, :, :], R_h)
            fw.barrier()
            if "skipA" not in taps:
                mixer_A(l)
            if "skipC" not in taps:
                mixer_C(l)
            if "skipB" not in taps:
                mixer_B(l)
            if l == 0:
                tap("xmid", xT[:, :, :], [r for rr in R_x for r in rr])
            ffn(l)

        yv = yT_d.rearrange("(c p) t -> p c t", p=128)
        for c in range(8):
            fw.dma("sp", yv[:, c, :], xT[:, c, :], reads=R_x[c], writes=[R_out], sem=s_st)
        fw.wait_all("sp", [R_out, R_tap])
        print("ninst", fw.ninst)
    return nc


TAP_SHAPES = {
    "hT": ((128, 8, S), BF16), "a_qT": ((128, 2, S), BF16), "a_oT": ((128, 2, S), BF16), "c_qT": ((128, 2, S), BF16),
    "c_oT": ((128, 2, S), BF16), "xmid": ((128, 8, S), F32), "f_act": ((128, NFF, S), BF16),
    "b_g": ((128, 16, 8), F32), "b_beta": ((128, 16, 8), F32), "b_gc": ((128, 16, 8), F32),
    "b_q": ((128, 4, S), BF16), "b_k": ((128, 4, S), BF16), "b_v": ((128, 4, S), BF16), "b_o": ((128, 4, S), F32),
    "b_oT": ((128, 4, S), BF16),
}


_PROG_CACHE = {}


def _prep_inputs(inputs, nl):
    perm_in = _win_perm()
    perm_out = _wout_perm()
    perm_gu = _wgu_perm()
    w_in = np.ascontiguousarray(inputs["w_in"][:nl][:, :, perm_in])
    w_out = np.ascontiguousarray(inputs["w_out"][:nl][:, perm_out, :])
    w_gu = np.ascontiguousarray(inputs["w_gate_up"][:nl][:, :, perm_gu])
    w_dn = np.ascontiguousarray(inputs["w_down"][:nl])
    cf, cb16, ropeA, ropeC, _ = _build_consts(inputs, nl)
    shared = {"w_in": w_in, "w_out": w_out, "w_gu": w_gu, "w_dn": w_dn, "cf": cf, "cb": cb16,
              "ropeA": ropeA, "ropeC": ropeC}
    return shared


def kernel(**inputs):
    inputs = {k: np.asarray(v) for k, v in inputs.items()}
    nl = L_FULL
    shared = _prep_inputs(inputs, nl)
    x = inputs["x"]
    in_maps = []
    for b in range(8):
        m = dict(shared)
        m["xT"] = np.ascontiguousarray(x[b].T)
        in_maps.append(m)
    if nl not in _PROG_CACHE:
        _PROG_CACHE[nl] = build_program(nl)
    res = run_bass_kernel_spmd(_PROG_CACHE[nl], in_maps, core_ids=list(range(8)))
    out = np.stack([np.ascontiguousarray(r["yT"].T) for r in res.results], axis=0)
    return out.astype(np.float32)
```

```python
import contextlib
import numpy as np
import ml_dtypes
import concourse.bass as bass
import concourse.mybir as mybir
from concourse.bass_utils import run_bass_kernel_spmd

F32 = mybir.dt.float32
BF16 = mybir.dt.bfloat16
AF = mybir.ActivationFunctionType
ALU = mybir.AluOpType

L_FULL = 4
S = 2048
D = 1024
DFF = 2816
NFF = 22
IN_DIM = 3344
EPS = 1e-6
NEG = -30000.0


class Res:
    __slots__ = ("name", "w", "r", "psum")

    def __init__(self, name, psum=False):
        self.name = name
        self.w = None
        self.r = {}
        self.psum = psum


class FW:
    ENG = ("pe", "act", "dve", "pool", "sp")

    def __init__(self, nc, stack):
        self.nc = nc
        self.stack = stack
        self.e = {"pe": nc.tensor, "act": nc.scalar, "dve": nc.vector, "pool": nc.gpsimd, "sp": nc.sync}
        self.sems = {}
        self.cnt = {}
        self.waited = {k: {} for k in self.ENG}
        for k in self.ENG:
            self.sems[k] = stack.enter_context(nc.semaphore("s_" + k))
            self.cnt[k] = 0
        self.nres = 0
        self.ninst = {k: 0 for k in self.ENG}

    def sb(self, name, shape, dt, stack=None):
        self.nsb = getattr(self, "nsb", 0) + 1
        return (stack or self.stack).enter_context(self.nc.sbuf_tensor(f"{name}_{self.nsb}", list(shape), dt))

    def ps(self, name, shape, dt=F32):
        return self.stack.enter_context(self.nc.psum_tensor(name, list(shape), dt))

    def res(self, name=None, psum=False):
        self.nres += 1
        return Res(name or f"r{self.nres}", psum)

    def new_sem(self, name):
        s = self.stack.enter_context(self.nc.semaphore(name))
        self.sems[name] = s
        self.cnt[name] = 0
        return name

    def _deps(self, eng, reads, writes, waiter=None):
        waiter = waiter or eng
        deps = {}

        def need(sv):
            if sv is None:
                return
            s, v = sv
            if deps.get(s, 0) < v:
                deps[s] = v

        for r in reads:
            need(r.w)
            if r.psum:
                for s, v in r.r.items():
                    if s != eng:
                        need((s, v))
        for w in writes:
            need(w.w)
            for s, v in w.r.items():
                need((s, v))
        out = {}
        for s, v in deps.items():
            if s == eng:
                if eng == "pe":
                    continue
            if self.waited[waiter].get(s, 0) >= v:
                continue
            out[s] = v
        return out

    def _emit_waits(self, eng, deps):
        for s, v in deps.items():
            self.e[eng].wait_ge(self.sems[s], v)
            self.waited[eng][s] = v
            self.ninst[eng] += 1

    def op(self, eng, fn, reads=(), writes=()):
        deps = self._deps(eng, reads, writes)
        self._emit_waits(eng, deps)
        inst = fn(self.e[eng])
        self.cnt[eng] += 1
        self.ninst[eng] += 1
        v = self.cnt[eng]
        inst.then_inc(self.sems[eng], 1)
        for r in reads:
            if r.r.get(eng, 0) < v:
                r.r[eng] = v
        for w in writes:
            w.w = (eng, v)
            w.r = {}
        return inst

    def dma(self, q, out, in_, reads=(), writes=(), sem=None, **kw):
        d2 = self._deps(None, reads, writes, waiter=q)
        self._emit_waits(q, d2)
        inst = self.e[q].dma_start(out=out, in_=in_, **kw)
        self.cnt[sem] += 16
        v = self.cnt[sem]
        inst.then_inc(self.sems[sem], 16)
        self.ninst[q] += 1
        for r in reads:
            if r.r.get(sem, 0) < v:
                r.r[sem] = v
        for w in writes:
            w.w = (sem, v)
            w.r = {}
        return inst

    def wait_all(self, eng, resources):
        d2 = self._deps(None, resources, (), waiter=eng)
        self._emit_waits(eng, d2)

    def barrier(self):
        snap = dict(self.cnt)
        for eng in self.ENG:
            d = {}
            for s, v in snap.items():
                if v > 0 and s != eng and self.waited[eng].get(s, 0) < v:
                    d[s] = v
            self._emit_waits(eng, d)

    def mm(self, out, lhsT, rhs, start=True, stop=True, reads=(), writes=()):
        return self.op("pe", lambda e: e.matmul(out, lhsT, rhs, start=start, stop=stop), reads, writes)

    def tr(self, out, in_, ident, reads=(), writes=()):
        return self.op("pe", lambda e: e.transpose(out, in_, ident), reads, writes)

    def act(self, out, in_, func, reads=(), writes=(), **kw):
        return self.op("act", lambda e: e.activation(out=out, in_=in_, func=func, **kw), reads, writes)

    def tt(self, eng, out, in0, in1, op, reads=(), writes=()):
        return self.op(eng, lambda e: e.tensor_tensor(out=out, in0=in0, in1=in1, op=op), reads, writes)

    def stt(self, out, in0, scalar, in1, op0, op1, reads=(), writes=()):
        return self.op("dve", lambda e: e.scalar_tensor_tensor(out=out, in0=in0, scalar=scalar, in1=in1,
                                                                op0=op0, op1=op1), reads, writes)

    def ts(self, eng, out, in0, s1, s2, op0, op1=None, reads=(), writes=()):
        if op1 is None:
            return self.op(eng, lambda e: e.tensor_scalar(out=out, in0=in0, scalar1=s1, scalar2=None, op0=op0),
                           reads, writes)
        return self.op(eng, lambda e: e.tensor_scalar(out=out, in0=in0, scalar1=s1, scalar2=s2, op0=op0, op1=op1),
                       reads, writes)

    def cp(self, eng, out, in_, reads=(), writes=()):
        if eng == "act":
            return self.act(out, in_, AF.Copy, reads, writes)
        return self.op(eng, lambda e: e.tensor_copy(out=out, in_=in_), reads, writes)


O_QA, O_KA, O_VA = 0, 256, 512
O_QB, O_KB, O_VB, O_ZB = 768, 1280, 1792, 2304
O_AB, O_BB = 2816, 2824
O_QC, O_KC, O_VC = 2832, 3088, 3216

P_QA, P_KA, P_VA, P_AB = 0, 256, 512, 768
P_QC, P_KC, P_VC = 784, 1040, 1168
P_B = 1296


def _win_perm():
    idx = []
    idx += list(range(O_QA, O_QA + 256)) + list(range(O_KA, O_KA + 256)) + list(range(O_VA, O_VA + 256))
    idx += list(range(O_AB, O_AB + 16))
    for h in (0, 2, 1, 3):
        idx += list(range(O_QC + 64 * h, O_QC + 64 * h + 64))
    idx += list(range(O_KC, O_KC + 128)) + list(range(O_VC, O_VC + 128))
    for pp in range(2):
        for base in (O_QB, O_KB, O_VB, O_ZB):
            idx += list(range(base + 256 * pp, base + 256 * pp + 256))
    assert len(idx) == IN_DIM and len(set(idx)) == IN_DIM
    return np.array(idx)


def _wout_perm():
    idx = list(range(0, 768))
    for h in (0, 2, 1, 3):
        idx += list(range(768 + 64 * h, 768 + 64 * h + 64))
    return np.array(idx)


def _wgu_perm():
    idx = []
    for f in range(NFF):
        idx += list(range(f * 128, f * 128 + 128)) + list(range(DFF + f * 128, DFF + f * 128 + 128))
    return np.array(idx)


def _cf_layout(nl):
    off = {}
    c = 0

    def add(name, n):
        nonlocal c
        off[name] = c
        c += n

    for l in range(nl):
        add(f"n1_{l}", 8)
        add(f"n2_{l}", 8)
        add(f"qna_{l}", 1)
        add(f"kna_{l}", 1)
        add(f"qnc_{l}", 1)
        add(f"knc_{l}", 1)
        add(f"onb_{l}", 1)
        add(f"cw_{l}", 60)
        add(f"alog_{l}", 8)
        add(f"dtb_{l}", 8)
    add("ident", 128)
    add("ones", 128)
    add("uuf", 128)
    add("uub", 128)
    add("slf", 128)
    add("slb", 128)
    for d in range(2):
        for cc in range(2):
            add(f"sla_{d}{cc}", 128)
    off["_n"] = c
    return off


CB = {"ident": 0, "ones": 128, "blk64": 256, "ra": 384, "rc": 512, "strict": 640,
      "maskA": 768, "mask16": 1280, "gmask": 3328, "_n": 3840}


def _build_consts(inputs, nl):
    off = _cf_layout(nl)
    cf = np.zeros((128, off["_n"]), np.float32)
    p = np.arange(128)
    for l in range(nl):
        cf[:, off[f"n1_{l}"]:off[f"n1_{l}"] + 8] = inputs["norm1"][l].reshape(8, 128).T
        cf[:, off[f"n2_{l}"]:off[f"n2_{l}"] + 8] = inputs["norm2"][l].reshape(8, 128).T
        cf[:, off[f"qna_{l}"]] = inputs["qn_a"][l][p % 64]
        cf[:, off[f"kna_{l}"]] = inputs["kn_a"][l][p % 64]
        cf[:, off[f"qnc_{l}"]] = inputs["qn_c"][l][p % 64]
        cf[:, off[f"knc_{l}"]] = inputs["kn_c"][l][p % 64]
        cf[:, off[f"onb_{l}"]] = inputs["onorm_b"][l]
        cw = inputs["conv_b"][l]
        cf[:, off[f"cw_{l}"]:off[f"cw_{l}"] + 60] = cw.reshape(5, 12, 128).transpose(2, 1, 0).reshape(128, 60)
        cf[:, off[f"alog_{l}"]:off[f"alog_{l}"] + 8] = inputs["a_log_b"][l].reshape(1, 8)
        cf[:, off[f"dtb_{l}"]:off[f"dtb_{l}"] + 8] = inputs["dt_bias_b"][l].reshape(1, 8)
    j = p[:, None]
    i = p[None, :]
    same = (j // 64) == (i // 64)
    cf[:, off["ident"]:off["ident"] + 128] = (j == i)
    cf[:, off["ones"]:off["ones"] + 128] = 1.0
    cf[:, off["uuf"]:off["uuf"] + 128] = same & (j <= i)
    cf[:, off["uub"]:off["uub"] + 128] = same & (j >= i)
    cf[:, off["slf"]:off["slf"] + 128] = (j == 64 * (i // 64) + 63)
    cf[:, off["slb"]:off["slb"] + 128] = (j == 64 * (i // 64))
    for d in range(2):
        for cc in range(2):
            last = 64 * cc + (63 if d == 0 else 0)
            cf[:, off[f"sla_{d}{cc}"]:off[f"sla_{d}{cc}"] + 128] = (j == last) * np.ones((1, 128))

    cb = np.zeros((128, CB["_n"]), np.float32)
    cb[:, CB["ident"]:CB["ident"] + 128] = (j == i)
    cb[:, CB["ones"]:CB["ones"] + 128] = 1.0
    cb[:, CB["blk64"]:CB["blk64"] + 128] = same
    def rot_mat(pairs_fn):
        R = np.zeros((128, 128), np.float32)
        for m in range(128):
            hd = m % 64
            base = m - hd
            k, sgn = pairs_fn(hd)
            if k is not None:
                R[base + k, m] = sgn
        return R

    def pa(hd):
        if hd < 8:
            return hd + 8, -1.0
        if hd < 16:
            return hd - 8, 1.0
        return None, 0.0

    def pc(hd):
        blk = hd // 32
        r = hd % 32
        if r < 16:
            return blk * 32 + r + 16, -1.0
        return blk * 32 + r - 16, 1.0

    cb[:, CB["ra"]:CB["ra"] + 128] = rot_mat(pa)
    cb[:, CB["rc"]:CB["rc"] + 128] = rot_mat(pc)
    cb[:, CB["strict"]:CB["strict"] + 128] = (j != i)
    pk = p[:, None]
    f128 = np.arange(128)[None, :]
    f64 = np.arange(64)[None, :]
    mfull = np.where(np.abs(f128 - pk) <= 64, 0.0, NEG)
    mprev = np.where(pk >= f64 + 64, 0.0, NEG)
    mnext = np.where(pk <= f64, 0.0, NEG)
    one = np.concatenate([mfull, mprev, mnext], axis=1)
    cb[:, CB["maskA"]:CB["maskA"] + 512] = np.concatenate([one, one], axis=1)
    for B in range(4):
        blk = mfull[:, 32 * B:32 * B + 32]
        cb[:, CB["mask16"] + 512 * B:CB["mask16"] + 512 * B + 512] = np.tile(blk, (1, 16))
    gf = np.where(same & (i >= j), 0.0, NEG)
    gb = np.where(same & (i <= j), 0.0, NEG)
    cb[:, CB["gmask"]:CB["gmask"] + 512] = np.concatenate([gf, gf, gb, gb], axis=1)
    cb16 = cb.astype(ml_dtypes.bfloat16)

    t = np.arange(S, dtype=np.float32)
    ropeA = np.zeros((128, 2, S), np.float32)
    ropeC = np.zeros((128, 2, S), np.float32)
    invA = np.float32(500000.0) ** (-np.arange(8, dtype=np.float32) / 8)
    invC = np.float32(10000.0) ** (-np.arange(16, dtype=np.float32) / 16)
    rowp = (np.arange(S) // 64).astype(np.float32)
    colp = (np.arange(S) % 64).astype(np.float32)
    for m in range(128):
        hd = m % 64
        if hd < 16:
            ang = t * invA[hd % 8]
            ropeA[m, 0] = np.cos(ang)
            ropeA[m, 1] = np.sin(ang)
        else:
            ropeA[m, 0] = 1.0
        pos = rowp if hd < 32 else colp
        ang = pos * invC[(hd % 32) % 16]
        ropeC[m, 0] = np.cos(ang)
        ropeC[m, 1] = np.sin(ang)
    return cf, cb16, ropeA.astype(ml_dtypes.bfloat16), ropeC.astype(ml_dtypes.bfloat16), off


def build_program(nl, taps=()):
    taps = set(taps)
    nc = bass.Bass("TRN2", target_bir_lowering=False)
    off = _cf_layout(nl)
    xT_d = nc.dram_tensor("xT", [D, S], F32, kind="ExternalInput").ap()
    win_d = nc.dram_tensor("w_in", [nl, D, IN_DIM], F32, kind="ExternalInput").ap()
    wout_d = nc.dram_tensor("w_out", [nl, D, D], F32, kind="ExternalInput").ap()
    wgu_d = nc.dram_tensor("w_gu", [nl, D, 2 * DFF], F32, kind="ExternalInput").ap()
    wdn_d = nc.dram_tensor("w_dn", [nl, DFF, D], F32, kind="ExternalInput").ap()
    cf_d = nc.dram_tensor("cf", [128, off["_n"]], F32, kind="ExternalInput").ap()
    cb_d = nc.dram_tensor("cb", [128, CB["_n"]], BF16, kind="ExternalInput").ap()
    ropeA_d = nc.dram_tensor("ropeA", [128, 2, S], BF16, kind="ExternalInput").ap()
    ropeC_d = nc.dram_tensor("ropeC", [128, 2, S], BF16, kind="ExternalInput").ap()
    yT_d = nc.dram_tensor("yT", [D, S], F32, kind="ExternalOutput").ap()
    tap_d = {}
    for name, (shape, tdt) in TAP_SHAPES.items():
        if name in taps:
            tap_d[name] = nc.dram_tensor("tap_" + name, list(shape), tdt, kind="ExternalOutput").ap()

    with contextlib.ExitStack() as st:
        fw = FW(nc, st)
        xT = fw.sb("xT_s", [128, 8, S], F32)
        hT = fw.sb("hT_s", [128, 8, S], BF16)
        cf = fw.sb("cf_s", [128, off["_n"]], F32)
        cb = fw.sb("cb_s", [128, CB["_n"]], BF16)
        wbuf = [fw.sb(f"wbuf{i}", [128, 4096], BF16) for i in range(2)]
        R_x = [[fw.res(f"x{c}_{tb}") for tb in range(4)] for c in range(8)]
        R_h = [fw.res(f"h{tb}") for tb in range(4)]
        R_cf = fw.res("cf")
        R_cb = fw.res("cb")
        R_w = [fw.res("w0"), fw.res("w1")]
        R_out = fw.res("out")
        R_tap = fw.res("tap")
        PB = [fw.ps(f"pb{i}", [128, 512], F32) for i in range(8)]
        R_pb = [fw.res(f"pb{i}", psum=True) for i in range(8)]
        s_ld = fw.new_sem("ld")
        s_w = [fw.new_sem("w0"), fw.new_sem("w1")]
        s_st = fw.new_sem("st")
        s_tap = fw.new_sem("tap")
        wstate = {"i": 0}

        def cfc(name, n=1, o=0):
            return cf[:, off[name] + o: off[name] + o + n]

        def cbm(name, n=128, o=0):
            return cb[:, CB[name] + o: CB[name] + o + n]

        ident_b = cbm("ident")
        ones_b = cbm("ones")
        blk64_b = cbm("blk64")
        ident_f = cfc("ident", 128)
        ones_f = cfc("ones", 128)

        fw.dma("sp", cf[:, :], cf_d[:, :], writes=[R_cf], sem=s_ld)
        fw.dma("sp", cb[:, :], cb_d[:, :], writes=[R_cb], sem=s_ld)
        xv = xT_d.rearrange("(c p) t -> p c t", p=128)
        for c in range(8):
            fw.dma("sp", xT[:, c, :], xv[:, c, :], writes=R_x[c], sem=s_ld)

        def load_w(dram_ap, ncols, kchunks=8):
            i = wstate["i"]
            wstate["i"] = 1 - i
            view = wbuf[i][:, 0:kchunks * ncols].rearrange("p (k n) -> p k n", k=kchunks)
            fw.dma("pool", view, dram_ap.rearrange("(k p) n -> p k n", p=128), writes=[R_w[i]], sem=s_w[i])
            return view, R_w[i]

        def tap(name, src_ap, reads, dst=None):
            if name not in tap_d:
                return
            fw.dma("sp", dst if dst is not None else tap_d[name], src_ap, reads=reads, writes=[R_tap], sem=s_tap)

        def rmsnorm_fm(l, which, scr):
            sq = [fw.sb(f"nsq{i}", [128, 512], BF16, scr) for i in range(2)]
            R_sq = [fw.res() for _ in range(2)]
            lnv = [fw.sb(f"nln{i}", [128, 512], F32, scr) for i in range(2)]
            rstd = [fw.sb(f"nrs{i}", [128, 512], F32, scr) for i in range(2)]
            R_ln = [fw.res() for _ in range(2)]
            R_rs = [fw.res() for _ in range(2)]
            for tb in range(4):
                ts_ = slice(tb * 512, tb * 512 + 512)
                pb = tb % 2
                for c in range(8):
                    i = c % 2
                    fw.act(sq[i][:, :], xT[:, c, ts_], AF.Square, reads=[R_x[c][tb]], writes=[R_sq[i]])
                    fw.mm(PB[pb][:, :], ones_b, sq[i][:, :], start=(c == 0), stop=(c == 7),
                          reads=[R_sq[i], R_cb], writes=[R_pb[pb]])
                fw.act(lnv[pb][:, :], PB[pb][:, :], AF.Ln, reads=[R_pb[pb]], writes=[R_ln[pb]],
                       scale=1.0 / D, bias=eps_col)
                fw.act(rstd[pb][:, :], lnv[pb][:, :], AF.Exp, reads=[R_ln[pb]], writes=[R_rs[pb]], scale=-0.5)
                for c in range(8):
                    fw.stt(hT[:, c, ts_], xT[:, c, ts_], cfc(f"{which}_{l}", 1, c), rstd[pb][:, :],
                           ALU.mult, ALU.mult, reads=[R_x[c][tb], R_rs[pb], R_cf], writes=[R_h[tb]])

        def qk_post(pbank, R_pbank, wcol, rot_b, cos_ap, sin_ap, R_tab, out_ap, R_out_, tmp, scale_q=None):
            (qw, R_qw, sq, R_sq2, lnv, R_ln2, rs, R_rs2, t1, R_t1, t2, R_t2, pss, R_pss, psr, R_psr) = tmp
            fw.act(qw[:, :], pbank[:, :], AF.Copy, reads=[R_pbank, R_cf], writes=[R_qw], scale=wcol)
            fw.act(sq[:, :], pbank[:, :], AF.Square, reads=[R_pbank, R_cf], writes=[R_sq2], scale=wcol)
            fw.mm(pss[:, :], blk64_b, sq[:, :], reads=[R_sq2, R_cb], writes=[R_pss])
            fw.mm(psr[:, :], rot_b, qw[:, :], reads=[R_qw, R_cb], writes=[R_psr])
            fw.act(lnv[:, :], pss[:, :], AF.Ln, reads=[R_pss], writes=[R_ln2], scale=1.0 / 64, bias=eps_col)
            fw.act(rs[:, :], lnv[:, :], AF.Exp, reads=[R_ln2], writes=[R_rs2], scale=-0.5)
            fw.tt("dve", t1[:, :], qw[:, :], cos_ap, ALU.mult, reads=[R_qw, R_tab], writes=[R_t1])
            fw.tt("dve", t2[:, :], psr[:, :], sin_ap, ALU.mult, reads=[R_psr, R_tab], writes=[R_t2])
            fw.tt("dve", t1[:, :], t1[:, :], t2[:, :], ALU.add, reads=[R_t1, R_t2], writes=[R_t1])
            fw.tt("dve", out_ap, t1[:, :], rs[:, :], ALU.mult, reads=[R_t1, R_rs2], writes=[R_out_])

        def qk_tmp(scr, tag, pss_i, psr_i):
            return (fw.sb(f"qw{tag}", [128, 512], BF16, scr), fw.res(),
                    fw.sb(f"qsq{tag}", [128, 512], BF16, scr), fw.res(),
                    fw.sb(f"qln{tag}", [128, 512], F32, scr), fw.res(),
                    fw.sb(f"qrs{tag}", [128, 512], F32, scr), fw.res(),
                    fw.sb(f"qt1{tag}", [128, 512], F32, scr), fw.res(),
                    fw.sb(f"qt2{tag}", [128, 512], F32, scr), fw.res(),
                    PB[pss_i], R_pb[pss_i], PB[psr_i], R_pb[psr_i])

        def proj_fm(wv, R_wv, col0, m, tb, pbank, R_pbank):
            ts_ = slice(tb * 512, tb * 512 + 512)
            for k in range(8):
                fw.mm(pbank[0:m, :], wv[:, k, col0:col0 + m], hT[:, k, ts_], start=(k == 0), stop=(k == 7),
                      reads=[R_wv, R_h[tb]], writes=[R_pbank])

        def wout_update(l, o_tiles, scr):
            for half in range(2):
                views = []
                for (row0, apf, R_o) in o_tiles:
                    wv, R_wv = load_w(wout_d[l, row0:row0 + 128, half * 512:half * 512 + 512], 512, kchunks=1)
                    views.append((wv, R_wv, apf, R_o))
                for oc4 in range(4):
                    oc = half * 4 + oc4
                    for tb in range(4):
                        pb = (oc4 * 4 + tb) % 4
                        n = len(views)
                        for i, (wv, R_wv, apf, R_o) in enumerate(views):
                            fw.mm(PB[pb][:, :], wv[:, 0, oc4 * 128:oc4 * 128 + 128], apf(tb), start=(i == 0),
                                  stop=(i == n - 1), reads=[R_wv, R_o], writes=[R_pb[pb]])
                        ts_ = slice(tb * 512, tb * 512 + 512)
                        fw.tt("dve", xT[:, oc, ts_], xT[:, oc, ts_], PB[pb][:, :], ALU.add,
                              reads=[R_x[oc][tb], R_pb[pb]], writes=[R_x[oc][tb]])

        def run_attn_jobs(jobs, PT, R_pt, sbanks=(2, 3, 4, 5), LA=2):
            n = len(jobs)
            for step in range(n + LA):
                if step < n:
                    j = jobs[step]
                    sb_i = sbanks[step % len(sbanks)]
                    pt_i = step % len(PT)
                    Sb, R_S = PB[sb_i], R_pb[sb_i]
                    nq = len(j["qk"])
                    for bi, (scol, nn, k_ap, q_ap) in enumerate(j["qk"]):
                        fw.mm(Sb[:, scol:scol + nn], k_ap, q_ap, start=(bi == 0), stop=(j["mask"] is None and bi == nq - 1),
                              reads=j["reads"], writes=[R_S])
                    if j["mask"] is not None:
                        fw.mm(Sb[:, :], ident_b, j["mask"], start=False, stop=True, reads=[R_cb], writes=[R_S])
                    fw.act(PT[pt_i][:, :], Sb[:, :], AF.Exp, reads=[R_S], writes=[R_pt[pt_i]], scale=0.125)
                if step >= LA:
                    j = jobs[step - LA]
                    pt_i = (step - LA) % len(PT)
                    for (acc_ap, v_ap, scol, nn, st_, sp_) in j["pv"]:
                        fw.mm(acc_ap, v_ap, PT[pt_i][:, scol:scol + nn], start=st_, stop=sp_,
                              reads=[R_pt[pt_i], j["R_v"]], writes=[j["R_acc"]])
                    if j["fin"] is not None:
                        j["fin"]()

        def attn_finalize(acc, R_acc, hh, out_ap, R_o, tmp):
            lnd, R_lnd, rd, R_rd = tmp
            nr = slice(hh * 64, hh * 64 + 64)
            dr = slice((1 - hh) * 64, (1 - hh) * 64 + 64)
            fw.act(lnd[nr, :], acc[dr, :], AF.Ln, reads=[R_acc], writes=[R_lnd])
            fw.act(rd[nr, :], lnd[nr, :], AF.Exp, reads=[R_lnd], writes=[R_rd], scale=-1.0)
            fw.tt("dve", out_ap, acc[nr, :], rd[nr, :], ALU.mult, reads=[R_acc, R_rd], writes=[R_o])

        def mixer_A(l):
            with contextlib.ExitStack() as scr:
                qT = fw.sb("a_qT", [128, 2, S], BF16, scr)
                kT = fw.sb("a_kT", [128, 2, S], BF16, scr)
                R_q = [fw.res() for _ in range(2)]
                R_k = [fw.res() for _ in range(2)]
                kTz = fw.sb("a_kTz", [128, 2, S], BF16, scr)
                R_kz = fw.res()
                tab = fw.sb("a_tab", [128, 2, S], BF16, scr)
                R_tab = fw.res()
                fw.dma("sp", tab[:, :, :], ropeA_d[:, :, :], writes=[R_tab], sem=s_ld)
                vx = fw.sb("a_vx", [128, 3, 16, 2, 128], BF16, scr)
                R_vx = fw.res()
                oT = fw.sb("a_oT", [128, 2, S], BF16, scr)
                R_o = [fw.res() for _ in range(2)]
                PT = [fw.sb(f"a_pt{i}", [128, 512], BF16, scr) for i in range(3)]
                R_pt = [fw.res() for _ in range(3)]
                t0_ = qk_tmp(scr, "a0", 2, 4)
                t1_ = (fw.sb("qwa1", [128, 512], BF16, scr), fw.res(), fw.sb("qsqa1", [128, 512], BF16, scr), fw.res(),
                       t0_[4], t0_[5], fw.sb("qrsa1", [128, 512], F32, scr), fw.res(), t0_[8], t0_[9], t0_[10], t0_[11],
                       PB[3], R_pb[3], PB[5], R_pb[5])
                tmps = [t0_, t1_]
                ftmp = (t0_[4], t0_[5], t0_[6], t0_[7])
                wv, R_wv = load_w(win_d[l, :, P_QA:P_QA + 512], 512)
                wv2, R_wv2 = load_w(win_d[l, :, P_VA:P_VA + 272], 272)
                n = 0
                for ci in range(4):
                    isq = ci < 2
                    dst, Rd = (qT, R_q) if isq else (kT, R_k)
                    c = ci % 2
                    for tb in range(4):
                        ts_ = slice(tb * 512, tb * 512 + 512)
                        pb = n % 2
                        proj_fm(wv, R_wv, ci * 128, 128, tb, PB[pb], R_pb[pb])
                        qk_post(PB[pb], R_pb[pb], cfc(f"qna_{l}" if isq else f"kna_{l}"), cbm("ra"),
                                tab[:, 0, ts_], tab[:, 1, ts_], R_tab, dst[:, c, ts_], Rd[c], tmps[n % 2])
                        n += 1
                tap("a_qT", qT[:, :, :], R_q)
                for c in range(2):
                    for tb in range(4):
                        pb = n % 2
                        n += 1
                        proj_fm(wv2, R_wv2, c * 128, 128, tb, PB[pb], R_pb[pb])
                        fw.cp("act", oT[:, c, tb * 512:tb * 512 + 512], PB[pb][:, :], reads=[R_pb[pb]], writes=[R_o[c]])
                for hp in range(2):
                    fw.op("pool", lambda e: e.memset(vx[:, :, :, 0, 64:128], 1.0), reads=[], writes=[R_vx])
                    fw.op("pool", lambda e: e.memset(vx[:, :, :, 1, 0:64], 1.0), reads=[], writes=[R_vx])
                    n = 0
                    for pat, dil in enumerate((1, 4, 16)):
                        for tile in range(16):
                            if dil == 1:
                                t0, st_ = tile * 128, 1
                            elif dil == 4:
                                r, m = tile // 4, tile % 4
                                t0, st_ = r + 512 * m, 4
                            else:
                                t0, st_ = tile, 16
                            pb = 6 + (n // 4) % 2
                            sub = n % 4
                            pbv = PB[pb][:, :].bitcast(BF16)
                            fw.tr(pbv[:, sub * 128:sub * 128 + 128], oT[:, hp, t0:t0 + 127 * st_ + 1:st_], ident_b,
                                  reads=[R_o[hp], R_cb], writes=[R_pb[pb]])
                            if sub == 3:
                                tl = tile - 3
                                src = pbv[:, 0:512].rearrange("p (t h d) -> p t h d", t=4, h=2)
                                fw.cp("dve", vx[:, pat, tl:tl + 4, 0, 0:64], src[:, :, 0, :],
                                      reads=[R_pb[pb]], writes=[R_vx])
                                fw.cp("act", vx[:, pat, tl:tl + 4, 1, 64:128], src[:, :, 1, :],
                                      reads=[R_pb[pb]], writes=[R_vx])
                            n += 1
                    fw.op("pool", lambda e: e.memset(kTz[64:128, 0, :], 0.0), writes=[R_kz])
                    fw.op("pool", lambda e: e.memset(kTz[0:64, 1, :], 0.0), writes=[R_kz])
                    fw.cp("pool", kTz[0:64, 0, :], kT[0:64, hp, :], reads=[R_k[hp]], writes=[R_kz])
                    fw.cp("pool", kTz[64:128, 1, :], kT[64:128, hp, :], reads=[R_k[hp]], writes=[R_kz])
                    jobs = []
                    for hh in range(2):
                        pr = slice(hh * 64, hh * 64 + 64)
                        c = hp
                        for B in range(4):
                            acc_i = (hh * 4 + B) % 2
                            acc, R_acc = PB[acc_i], R_pb[acc_i]
                            group = []

                            def add_job(blocks, mask_ap, group=group, acc=acc, R_acc=R_acc, c=c, hh=hh):
                                j = {"qk": [(scol, nn, k_ap, q_ap) for (scol, nn, q_ap, k_ap, pat, tile, acc_ap) in blocks],
                                     "mask": mask_ap, "ncols": 512, "reads": [R_q[c], R_kz],
                                     "pv": [[acc_ap, vx[:, pat, tile, hh, :], scol, nn, False, False]
                                            for (scol, nn, q_ap, k_ap, pat, tile, acc_ap) in blocks],
                                     "R_acc": R_acc, "R_v": R_vx, "fin": None}
                                group.append(j)

                            for half in range(2):
                                blocks = []
                                for qi in range(2):
                                    ml = half * 2 + qi
                                    m = 4 * B + ml
                                    so = qi * 256
                                    blocks.append((so, 128, qT[:, c, m * 128:m * 128 + 128],
                                                   kTz[:, hh, m * 128:m * 128 + 128], 0, m, acc[:, ml * 128:ml * 128 + 128]))
                                    if m > 0:
                                        blocks.append((so + 128, 64, qT[:, c, m * 128:m * 128 + 64],
                                                       kTz[:, hh, (m - 1) * 128:m * 128], 0, m - 1,
                                                       acc[:, ml * 128:ml * 128 + 64]))
                                    if m < 15:
                                        blocks.append((so + 192, 64, qT[:, c, m * 128 + 64:m * 128 + 128],
                                                       kTz[:, hh, (m + 1) * 128:(m + 2) * 128], 0, m + 1,
                                                       acc[:, ml * 128 + 64:ml * 128 + 128]))
                                add_job(blocks, cbm("maskA", 512))
                            for half in range(2):
                                blocks = []
                                for qi in range(2):
                                    r = half * 2 + qi
                                    so = qi * 256
                                    b0 = 512 * B + r

                                    def kt(mm_, r=r, hh=hh):
                                        return kTz[:, hh, 512 * mm_ + r:512 * mm_ + 512:4]
                                    blocks.append((so, 128, qT[:, c, b0:512 * B + 512:4], kt(B), 1, r * 4 + B,
                                                   acc[:, r:512:4]))
                                    if B > 0:
                                        blocks.append((so + 128, 64, qT[:, c, b0:512 * B + 256:4], kt(B - 1), 1,
                                                       r * 4 + B - 1, acc[:, r:256:4]))
                                    if B < 3:
                                        blocks.append((so + 192, 64, qT[:, c, b0 + 256:512 * B + 512:4], kt(B + 1), 1,
                                                       r * 4 + B + 1, acc[:, 256 + r:512:4]))
                                add_job(blocks, cbm("maskA", 512))
                            blocks = []
                            for b in range(16):
                                blocks.append((b * 32, 32, qT[:, c, 512 * B + b:512 * B + 512:16],
                                               kTz[:, hh, b:S:16], 2, b, acc[:, b:512:16]))
                            add_job(blocks, cbm("mask16", 512, 512 * B))
                            group[0]["pv"][0][4] = True
                            group[-1]["pv"][-1][5] = True
                            group[-1]["fin"] = (lambda acc=acc, R_acc=R_acc, hh=hh, pr=pr, c=c, B=B:
                                                attn_finalize(acc, R_acc, hh, oT[pr, c, B * 512:B * 512 + 512], R_o[c], ftmp))
                            jobs += group
                    run_attn_jobs(jobs, PT, R_pt)
                tap("a_oT", oT[:, :, :], R_o)
                wout_update(l, [(c * 128, (lambda tb, c=c: oT[:, c, tb * 512:tb * 512 + 512]), R_o[c]) for c in range(2)], scr)
            fw.barrier()

        def mixer_C(l):
            with contextlib.ExitStack() as scr:
                qT = fw.sb("c_qT", [128, 2, S], BF16, scr)
                kT = fw.sb("c_kT", [128, S], BF16, scr)
                R_q = [fw.res() for _ in range(2)]
                R_k = fw.res()
                kTz = fw.sb("c_kTz", [128, 2, S], BF16, scr)
                R_kz = fw.res()
                tab = fw.sb("c_tab", [128, 2, S], BF16, scr)
                R_tab = fw.res()
                fw.dma("sp", tab[:, :, :], ropeC_d[:, :, :], writes=[R_tab], sem=s_ld)
                vx = fw.sb("c_vx", [128, 16, 2, 128], BF16, scr)
                R_vx = fw.res()
                oT = fw.sb("c_oT", [128, 2, S], BF16, scr)
                R_o = [fw.res() for _ in range(2)]
                PT = [fw.sb(f"c_pt{i}", [128, 512], BF16, scr) for i in range(3)]
                R_pt = [fw.res() for _ in range(3)]
                lnd = fw.sb("c_lnd", [128, 512], F32, scr)
                rd = fw.sb("c_rd", [128, 512], F32, scr)
                ftmp = (lnd, fw.res(), rd, fw.res())
                tmps = [qk_tmp(scr, "c0", 2, 4), qk_tmp(scr, "c1", 3, 5)]
                wv, R_wv = load_w(win_d[l, :, P_QC:P_QC + 512], 512)
                n = 0
                for ci in range(3):
                    for tb in range(4):
                        ts_ = slice(tb * 512, tb * 512 + 512)
                        pb = n % 2
                        proj_fm(wv, R_wv, ci * 128, 128, tb, PB[pb], R_pb[pb])
                        if ci < 2:
                            dst, Rd, wn = qT[:, ci, ts_], R_q[ci], f"qnc_{l}"
                        else:
                            dst, Rd, wn = kT[:, ts_], R_k, f"knc_{l}"
                        qk_post(PB[pb], R_pb[pb], cfc(wn), cbm("rc"), tab[:, 0, ts_], tab[:, 1, ts_], R_tab,
                                dst, Rd, tmps[n % 2])
                        n += 1
                tap("c_qT", qT[:, :, :], R_q)
                fw.op("pool", lambda e: e.memset(kTz[64:128, 0, :], 0.0), writes=[R_kz])
                fw.op("pool", lambda e: e.memset(kTz[0:64, 1, :], 0.0), writes=[R_kz])
                fw.cp("pool", kTz[0:64, 0, :], kT[0:64, :], reads=[R_k], writes=[R_kz])
                fw.cp("pool", kTz[64:128, 1, :], kT[64:128, :], reads=[R_k], writes=[R_kz])
                fw.op("pool", lambda e: e.memset(vx[:, :, 0, 64:128], 1.0), reads=[], writes=[R_vx])
                fw.op("pool", lambda e: e.memset(vx[:, :, 1, 0:64], 1.0), reads=[], writes=[R_vx])
                for tile in range(16):
                    pb = 6 + (tile // 4) % 2
                    sub = tile % 4
                    for k in range(8):
                        fw.mm(PB[pb][:, sub * 128:sub * 128 + 128], hT[:, k, tile * 128:tile * 128 + 128],
                              wv[:, k, 384:512], start=(k == 0), stop=(k == 7), reads=[R_wv] + R_h, writes=[R_pb[pb]])
                    if sub == 3:
                        tl = tile - 3
                        src = PB[pb][:, :].rearrange("p (t h d) -> p t h d", t=4, h=2)
                        fw.cp("dve", vx[:, tl:tl + 4, 0, 0:64], src[:, :, 0, :], reads=[R_pb[pb]], writes=[R_vx])
                        fw.cp("dve", vx[:, tl:tl + 4, 1, 64:128], src[:, :, 1, :], reads=[R_pb[pb]], writes=[R_vx])
                jobs = []
                for c in range(2):
                    for hh in range(2):
                        pr = slice(hh * 64, hh * 64 + 64)
                        for B in range(4):
                            acc_i = (c * 8 + hh * 4 + B) % 2
                            acc, R_acc = PB[acc_i], R_pb[acc_i]
                            for kt in range(16):
                                j = {"qk": [(0, 512, kTz[:, hh, kt * 128:kt * 128 + 128], qT[:, c, B * 512:B * 512 + 512])],
                                     "mask": None, "ncols": 512, "reads": [R_q[c], R_kz],
                                     "pv": [[acc[:, :], vx[:, kt, hh, :], 0, 512, kt == 0, kt == 15]],
                                     "R_acc": R_acc, "R_v": R_vx, "fin": None}
                                if kt == 15:
                                    j["fin"] = (lambda acc=acc, R_acc=R_acc, hh=hh, pr=pr, c=c, B=B:
                                                attn_finalize(acc, R_acc, hh, oT[pr, c, B * 512:B * 512 + 512], R_o[c], ftmp))
                                jobs.append(j)
                run_attn_jobs(jobs, PT, R_pt)
                tap("c_oT", oT[:, :, :], R_o)
                wout_update(l, [(768 + c * 128, (lambda tb, c=c: oT[:, c, tb * 512:tb * 512 + 512]), R_o[c])
                                for c in range(2)], scr)
            fw.barrier()


        def mixer_B(l):
            with contextlib.ExitStack() as scr:
                def tk(name, n=8):
                    return fw.sb("b_" + name, [128, 16, n], F32, scr)
                ab_tok = tk("ab", 16)
                beta = tk("beta"); nbeta = tk("nbeta"); g_tok = tk("g"); gc = tk("gc"); ngc = tk("ngc")
                egc = tk("egc"); kdec = tk("kdec")
                gam = fw.sb("b_gam", [128, 16, 2, 8], F32, scr)
                ea = fw.sb("b_ea", [128, 8], F32, scr)
                R_ts = fw.res()
                wv, R_wv = load_w(win_d[l, :, P_AB:P_AB + 16], 16)
                for tile in range(16):
                    for k in range(8):
                        fw.mm(PB[0][:, tile * 16:tile * 16 + 16], hT[:, k, tile * 128:tile * 128 + 128], wv[:, k, 0:16],
                              start=(k == 0), stop=(k == 7), reads=[R_wv] + R_h, writes=[R_pb[0]])
                fw.cp("dve", ab_tok[:, :, :], PB[0][:, 0:256].rearrange("p (t n) -> p t n", t=16),
                      reads=[R_pb[0]], writes=[R_ts])
                fw.act(beta[:, :, :], ab_tok[:, :, 8:16], AF.Tanh, reads=[R_ts], writes=[R_ts], scale=0.5)
                fw.ts("dve", beta[:, :, :], beta[:, :, :], 0.5, 0.5, ALU.mult, ALU.add, reads=[R_ts], writes=[R_ts])
                fw.ts("dve", nbeta[:, :, :], beta[:, :, :], -1.0, None, ALU.mult, reads=[R_ts], writes=[R_ts])
                dtb_b = cfc(f"dtb_{l}", 8).unsqueeze(1).broadcast_to([128, 16, 8])
                fw.tt("dve", g_tok[:, :, :], ab_tok[:, :, 0:8], dtb_b, ALU.add, reads=[R_ts, R_cf], writes=[R_ts])
                fw.act(g_tok[:, :, :], g_tok[:, :, :], AF.Exp, reads=[R_ts], writes=[R_ts])
                fw.act(g_tok[:, :, :], g_tok[:, :, :], AF.Ln, reads=[R_ts], writes=[R_ts], bias=cfc("ones", 1))
                fw.act(ea[:, :], cfc(f"alog_{l}", 8), AF.Exp, reads=[R_cf], writes=[R_ts])
                fw.stt(g_tok[:, :, :], g_tok[:, :, :], -1.0, ea[:, :].unsqueeze(1).broadcast_to([128, 16, 8]),
                       ALU.mult, ALU.mult, reads=[R_ts], writes=[R_ts])
                tap("b_g", g_tok[:, :, :], [R_ts])
                tap("b_beta", beta[:, :, :], [R_ts])
                for tile in range(16):
                    for d in range(2):
                        fw.mm(PB[1][:, tile * 8 + d * 4:tile * 8 + d * 4 + 4], cfc("uuf" if d == 0 else "uub", 128),
                              g_tok[:, tile, d * 4:d * 4 + 4], reads=[R_ts, R_cf], writes=[R_pb[1]])
                fw.cp("dve", gc[:, :, :], PB[1][:, 0:128].rearrange("p (t n) -> p t n", t=16), reads=[R_pb[1]], writes=[R_ts])
                fw.ts("dve", ngc[:, :, :], gc[:, :, :], -1.0, None, ALU.mult, reads=[R_ts], writes=[R_ts])
                fw.act(egc[:, :, :], gc[:, :, :], AF.Exp, reads=[R_ts], writes=[R_ts])
                for tile in range(16):
                    for d in range(2):
                        fw.mm(PB[2][:, tile * 8 + d * 4:tile * 8 + d * 4 + 4], cfc("slf" if d == 0 else "slb", 128),
                              gc[:, tile, d * 4:d * 4 + 4], reads=[R_ts, R_cf], writes=[R_pb[2]])
                        for cc in range(2):
                            o_ = (tile * 2 + cc) * 8 + d * 4
                            fw.mm(PB[3][:, o_:o_ + 4], cfc(f"sla_{d}{cc}", 128), gc[:, tile, d * 4:d * 4 + 4],
                                  reads=[R_ts, R_cf], writes=[R_pb[3]])
                fw.tt("dve", kdec[:, :, :], PB[2][:, 0:128].rearrange("p (t n) -> p t n", t=16), gc[:, :, :], ALU.subtract,
                      reads=[R_pb[2], R_ts], writes=[R_ts])
                fw.act(kdec[:, :, :], kdec[:, :, :], AF.Exp, reads=[R_ts], writes=[R_ts])
                fw.act(gam[:, :, :, :], PB[3][:, 0:256].rearrange("p (t c n) -> p t c n", t=16, c=2), AF.Exp,
                       reads=[R_pb[3]], writes=[R_ts])
                tap("b_gc", gc[:, :, :], [R_ts])

                for pp in range(2):
                    with contextlib.ExitStack() as sp_:
                        bq = fw.sb("b_q", [128, 2, S], BF16, sp_)
                        bk = fw.sb("b_k", [128, 2, S], BF16, sp_)
                        K_tok = fw.sb("b_Kt", [128, 16, 2, 128], BF16, sp_)
                        V_tok = fw.sb("b_Vt", [128, 16, 2, 128], BF16, sp_)
                        R_bq = fw.res(); R_bk = fw.res(); R_Kt = fw.res(); R_Vt = fw.res()
                        with contextlib.ExitStack() as s1:
                            bv = fw.sb("b_v", [128, 2, S], BF16, s1)
                            raws = [fw.sb(f"b_raw{i}", [128, S + 4], BF16, s1) for i in range(2)]
                            dgs = [fw.sb(f"b_dg{i}", [128, 5, 128], BF16, s1) for i in range(2)]
                            sqs = [fw.sb(f"b_sq{i}", [128, 512], BF16, s1) for i in range(2)]
                            lnvs = [fw.sb(f"b_ln{i}", [128, 512], F32, s1) for i in range(2)]
                            rss = [fw.sb(f"b_rs{i}", [128, 512], F32, s1) for i in range(2)]
                            R_bv = fw.res()
                            R_raws = [fw.res(), fw.res()]; R_dgs = [fw.res(), fw.res()]; R_sqs = [fw.res(), fw.res()]
                            R_lns = [fw.res(), fw.res()]; R_rss = [fw.res(), fw.res()]
                            for i in range(2):
                                fw.op("pool", lambda e: e.memset(raws[i][:, 0:2], 0.0), writes=[R_raws[i]])
                                fw.op("pool", lambda e: e.memset(raws[i][:, S + 2:S + 4], 0.0), writes=[R_raws[i]])
                            nchunk = 0
                            wq, R_wq = load_w(win_d[l, :, P_B + pp * 1024:P_B + pp * 1024 + 512], 512)
                            wz, R_wz = load_w(win_d[l, :, P_B + pp * 1024 + 512:P_B + pp * 1024 + 1024], 512)
                            n = 0
                            for kind in range(3):
                                for hh in range(2):
                                    wsrc, R_ws, col0 = (wq, R_wq, kind * 256 + hh * 128) if kind < 2 else (wz, R_wz, hh * 128)
                                    dst, R_dst = ((bq, R_bq), (bk, R_bk), (bv, R_bv))[kind]
                                    cch = kind * 4 + 2 * pp + hh
                                    raw, R_raw, dg, R_dg = raws[nchunk % 2], R_raws[nchunk % 2], dgs[nchunk % 2], R_dgs[nchunk % 2]
                                    nchunk += 1
                                    for k in range(5):
                                        fw.ts("dve", dg[:, k, :], ident_b, cfc(f"cw_{l}", 1, cch * 5 + k), None, ALU.mult,
                                              reads=[R_cb, R_cf], writes=[R_dg])
                                    for tb in range(4):
                                        pb = n % 2
                                        n += 1
                                        proj_fm(wsrc, R_ws, col0, 128, tb, PB[pb], R_pb[pb])
                                        fw.cp("act", raw[:, 2 + tb * 512:2 + tb * 512 + 512], PB[pb][:, :],
                                              reads=[R_pb[pb]], writes=[R_raw])
                                    for tb in range(4):
                                        pb = 2 + n % 2
                                        n += 1
                                        ts_ = slice(tb * 512, tb * 512 + 512)
                                        for k in range(5):
                                            fw.mm(PB[pb][:, :], dg[:, k, :], raw[:, tb * 512 + k:tb * 512 + k + 512],
                                                  start=(k == 0), stop=(k == 4), reads=[R_dg, R_raw], writes=[R_pb[pb]])
                                        fw.act(dst[:, hh, ts_], PB[pb][:, :], AF.Silu, reads=[R_pb[pb]], writes=[R_dst])
                            calls = [(kind, hh, tb) for kind in range(2) for hh in range(2) for tb in range(4)]
                            R_l2 = {(kind, hh, tb): fw.res() for (kind, hh, tb) in calls}

                            def l2_front(i):
                                kind, hh, tb = calls[i]
                                dst, R_dst = ((bq, R_bq), (bk, R_bk))[kind]
                                ts_ = slice(tb * 512, tb * 512 + 512)
                                fw.act(sqs[i % 2][:, :], dst[:, hh, ts_], AF.Square, reads=[R_dst], writes=[R_sqs[i % 2]])
                                fw.mm(PB[4 + i % 2][:, :], ones_b, sqs[i % 2][:, :], reads=[R_sqs[i % 2], R_cb], writes=[R_pb[4 + i % 2]])

                            def l2_back(i):
                                kind, hh, tb = calls[i]
                                dst, R_dst = ((bq, R_bq), (bk, R_bk))[kind]
                                ts_ = slice(tb * 512, tb * 512 + 512)
                                pb = 4 + i % 2
                                fw.act(lnvs[i % 2][:, :], PB[pb][:, :], AF.Ln, reads=[R_pb[pb]], writes=[R_lns[i % 2]], bias=eps_col)
                                fw.act(rss[i % 2][:, :], lnvs[i % 2][:, :], AF.Exp, reads=[R_lns[i % 2]], writes=[R_rss[i % 2]], scale=-0.5)
                                fw.stt(dst[:, hh, ts_], dst[:, hh, ts_], (128.0 ** -0.5) if kind == 0 else 1.0, rss[i % 2][:, :],
                                       ALU.mult, ALU.mult, reads=[R_dst, R_rss[i % 2]], writes=[R_l2[calls[i]]])

                            l2_front(0)
                            for i in range(len(calls)):
                                if i + 1 < len(calls):
                                    l2_front(i + 1)
                                l2_back(i)
                            fw.op("pool", lambda e: e.memset(dgs[0][:, 0, 0:1], 0.0), reads=[R_l2[c_] for c_ in calls if c_[0] == 0], writes=[R_bq])
                            fw.op("pool", lambda e: e.memset(dgs[0][:, 0, 0:1], 0.0), reads=[R_l2[c_] for c_ in calls if c_[0] == 1], writes=[R_bk])
                            tap("b_q", bq[:, :, :], [R_bq], dst=(tap_d["b_q"][:, 2 * pp:2 * pp + 2, :] if "b_q" in tap_d else None))
                            tap("b_k", bk[:, :, :], [R_bk], dst=(tap_d["b_k"][:, 2 * pp:2 * pp + 2, :] if "b_k" in tap_d else None))
                            tap("b_v", bv[:, :, :], [R_bv], dst=(tap_d["b_v"][:, 2 * pp:2 * pp + 2, :] if "b_v" in tap_d else None))
                            for src, R_src, dstt, R_dt in ((bk, R_bk, K_tok, R_Kt), (bv, R_bv, V_tok, R_Vt)):
                                for hh in range(2):
                                    for t4 in range(4):
                                        pb = 6 + n % 2
                                        n += 1
                                        pbv = PB[pb][:, :].bitcast(BF16)
                                        for i in range(4):
                                            tile = t4 * 4 + i
                                            fw.tr(pbv[:, i * 128:i * 128 + 128], src[:, hh, tile * 128:tile * 128 + 128], ident_b,
                                                  reads=[R_src, R_cb], writes=[R_pb[pb]])
                                        fw.cp("act", dstt[:, t4 * 4:t4 * 4 + 4, hh, :],
                                              pbv[:, 0:512].rearrange("p (t n) -> p t n", t=4), reads=[R_pb[pb]], writes=[R_dt])
                        fw.barrier()
                        obuf = fw.sb("b_ob", [128, 2, S], F32, sp_)
                        R_ob = fw.res()
                        fw.op("pool", lambda e: e.memset(obuf[:, :, :], 0.0), writes=[R_ob])
                        with contextlib.ExitStack() as s2:
                            def t4(name, dt):
                                return fw.sb("b_" + name, [128, 4, 128], dt, s2), fw.res()
                            KKs, R_KKs = t4("KKs", BF16)
                            GU, R_GU = t4("GU", F32)
                            EG, R_EG = t4("EG", BF16)
                            Et, R_Et = t4("Et", BF16)
                            X, R_X = t4("X", BF16)
                            Yb = [t4(f"Y{i}", BF16) for i in range(2)]
                            Zb = [(X, R_X), t4("Z1", BF16)]
                            Pbb = [t4(f"Pb{i}", BF16) for i in range(2)]
                            Keg, R_Keg = t4("Keg", BF16)
                            qkT2 = [t4(f"qkT{i}", BF16) for i in range(2)]
                            Ktil2 = [t4(f"Ktil{i}", BF16) for i in range(2)]
                            upp2 = [t4("upp0", F32)] * 2
                            wT2 = [t4(f"wT{i}", BF16) for i in range(2)]
                            qtT2 = [t4(f"qtT{i}", BF16) for i in range(2)]
                            S32 = fw.sb("b_S32", [128, 4, 128], F32, s2)
                            Sb = fw.sb("b_Sb", [128, 4, 128], BF16, s2)
                            vnew2 = [fw.sb(f"b_vnew{i}", [128, 4, 128], BF16, s2) for i in range(2)]
                            R_S32 = [fw.res(), fw.res()]; R_Sb = [fw.res(), fw.res()]; R_vn = [fw.res(), fw.res()]
                            R_psv = [R_pb[6], R_pb[7]]; R_pss = [R_pb[6], R_pb[7]]; R_po = [R_pb[5], R_pb[5]]
                            fw.op("pool", lambda e: e.memset(S32[:, :, :], 0.0), writes=R_S32)
                            fw.op("pool", lambda e: e.memset(Sb[:, :, :], 0.0), writes=R_Sb)
                            fw.op("pool", lambda e: e.memset(vnew2[0][:, :, :], 0.0), writes=R_vn)
                            fw.op("pool", lambda e: e.memset(vnew2[1][:, :, :], 0.0), writes=R_vn)
                            v4 = lambda i: PB[i][:, :].rearrange("p (u n) -> p u n", u=4)
                            strict_b4 = cbm("strict").unsqueeze(1).broadcast_to([128, 4, 128])
                            ident_b4 = ident_b.unsqueeze(1).broadcast_to([128, 4, 128])
                            ucol = lambda d, hh: d * 4 + 2 * pp + hh

                            def prep(n):
                                Td = (n, 15 - n)
                                tsl = lambda d: slice(Td[d] * 128, Td[d] * 128 + 128)
                                qkT, R_qkT = qkT2[n % 2]
                                Ktil, R_Ktil = Ktil2[n % 2]
                                upp, R_upp = upp2[n % 2]
                                wT, R_wT = wT2[n % 2]
                                qtT, R_qtT = qtT2[n % 2]
                                for d in range(2):
                                    for hh in range(2):
                                        u = d * 2 + hh
                                        fw.mm(PB[0][:, u * 128:u * 128 + 128], bk[:, hh, tsl(d)], bk[:, hh, tsl(d)],
                                              reads=[R_bk], writes=[R_pb[0]])
                                        fw.mm(PB[1][:, u * 128:u * 128 + 128], bk[:, hh, tsl(d)], bq[:, hh, tsl(d)],
                                              reads=[R_bk, R_bq], writes=[R_pb[1]])
                                for d in range(2):
                                    uu = cfc("uuf" if d == 0 else "uub", 128).unsqueeze(1).broadcast_to([128, 2, 128])
                                    gb_ = g_tok[:, Td[d], ucol(d, 0):ucol(d, 0) + 2].unsqueeze(2).broadcast_to([128, 2, 128])
                                    fw.tt("pool", GU[:, 2 * d:2 * d + 2, :], uu, gb_, ALU.mult, reads=[R_cf, R_ts], writes=[R_GU])
                                yield
                                fw.tt("dve", KKs[:, :, :], v4(0), strict_b4, ALU.mult, reads=[R_pb[0], R_cb], writes=[R_KKs])
                                fw.mm(PB[2][:, :], ones_f, GU[:, :, :].rearrange("p u n -> p (u n)"), start=True, stop=False,
                                      reads=[R_GU, R_cf], writes=[R_pb[2]])
                                yield
                                fw.act(EG[:, :, :], v4(2), AF.Exp, reads=[R_pb[2]], writes=[R_EG])
                                fw.mm(PB[2][:, :], ident_b, cbm("gmask", 512), start=False, stop=True, reads=[R_cb], writes=[R_pb[2]])
                                yield
                                for d in range(2):
                                    for hh in range(2):
                                        u = d * 2 + hh
                                        c_ = ucol(d, hh)
                                        fw.act(Et[:, u, :], PB[2][:, u * 128:u * 128 + 128], AF.Exp, reads=[R_pb[2], R_ts],
                                               writes=[R_Et], bias=ngc[:, Td[d], c_:c_ + 1])
                                for d in range(2):
                                    c0 = ucol(d, 0)
                                    eb = egc[:, Td[d], c0:c0 + 2].unsqueeze(2).broadcast_to([128, 2, 128])
                                    kb_ = kdec[:, Td[d], c0:c0 + 2].unsqueeze(2).broadcast_to([128, 2, 128])
                                    fw.tt("pool", Keg[:, 2 * d:2 * d + 2, :], K_tok[:, Td[d], :, :], eb, ALU.mult, reads=[R_Kt, R_ts], writes=[R_Keg])
                                    fw.tt("pool", Ktil[:, 2 * d:2 * d + 2, :], K_tok[:, Td[d], :, :], kb_, ALU.mult, reads=[R_Kt, R_ts], writes=[R_Ktil])
                                    fw.tt("pool", qtT[:, 2 * d:2 * d + 2, :], bq[:, :, tsl(d)], EG[:, 2 * d:2 * d + 2, :], ALU.mult,
                                          reads=[R_bq, R_EG], writes=[R_qtT])
                                yield
                                for d in range(2):
                                    for hh in range(2):
                                        u = d * 2 + hh
                                        c_ = ucol(d, hh)
                                        fw.stt(X[:, u, :], Et[:, u, :], beta[:, Td[d], c_:c_ + 1], KKs[:, u, :], ALU.mult, ALU.mult,
                                               reads=[R_Et, R_ts, R_KKs], writes=[R_X])
                                fw.tt("dve", qkT[:, :, :], v4(1), Et[:, :, :], ALU.mult, reads=[R_pb[1], R_Et], writes=[R_qkT])
                                yield
                                pbv3 = PB[3][:, :].bitcast(BF16)
                                for u in range(4):
                                    fw.tr(pbv3[:, u * 128:u * 128 + 128], X[:, u, :], ident_b, reads=[R_X, R_cb], writes=[R_pb[3]])
                                (Yc, R_Yc), (Zc, R_Zc) = Yb[0], (X, R_X)
                                Pc, R_Pc = Pbb[0]
                                fw.tt("dve", Pc[:, :, :], ident_b4, X[:, :, :], ALU.subtract, reads=[R_cb, R_X], writes=[R_Pc])
                                yield
                                fw.cp("act", Yc[:, :, :], pbv3[:, 0:512].rearrange("p (u n) -> p u n", u=4), reads=[R_pb[3]], writes=[R_Yc])
                                yield
                                for s_ in range(1, 6):
                                    Yn, R_Yn = Yb[s_ % 2]
                                    Zn, R_Zn = Zb[s_ % 2]
                                    Pn, R_Pn = Pbb[s_ % 2]
                                    for u in range(4):
                                        fw.mm(PB[3][:, u * 128:u * 128 + 128], Zc[:, u, :], Yc[:, u, :], reads=[R_Zc, R_Yc], writes=[R_pb[3]])
                                    if s_ <= 4:
                                        for u in range(4):
                                            fw.mm(PB[4][:, u * 128:u * 128 + 128], Yc[:, u, :], Zc[:, u, :], reads=[R_Zc, R_Yc], writes=[R_pb[4]])
                                    yield
                                    fw.cp("act", Yn[:, :, :], v4(3), reads=[R_pb[3]], writes=[R_Yn])
                                    if s_ <= 4:
                                        fw.cp("dve", Zn[:, :, :], v4(4), reads=[R_pb[4]], writes=[R_Zn])
                                    yield
                                    fw.mm(PB[0][:, :], ident_b, Pc[:, :, :].rearrange("p u n -> p (u n)"), start=True, stop=False,
                                          reads=[R_cb, R_Pc], writes=[R_pb[0]])
                                    for u in range(4):
                                        fw.mm(PB[0][:, u * 128:u * 128 + 128], Yn[:, u, :], Pc[:, u, :], start=False, stop=(u == 3),
                                              reads=[R_Yn, R_Pc], writes=[R_pb[0]])
                                    yield
                                    fw.cp("act" if s_ % 2 else "dve", Pn[:, :, :], v4(0), reads=[R_pb[0]], writes=[R_Pn])
                                    (Yc, R_Yc), (Zc, R_Zc), (Pc, R_Pc) = (Yn, R_Yn), (Zn, R_Zn), (Pn, R_Pn)
                                    yield
                                for d in range(2):
                                    for hh in range(2):
                                        u = d * 2 + hh
                                        fw.mm(PB[0][:, u * 128:u * 128 + 128], Pc[:, u, :], V_tok[:, Td[d], hh, :], reads=[R_Pc, R_Vt], writes=[R_pb[0]])
                                        fw.mm(PB[1][:, u * 128:u * 128 + 128], Keg[:, u, :], Pc[:, u, :], reads=[R_Pc, R_Keg], writes=[R_pb[1]])
                                yield
                                for d in range(2):
                                    c0 = ucol(d, 0)
                                    bb_ = beta[:, Td[d], c0:c0 + 2].unsqueeze(2).broadcast_to([128, 2, 128])
                                    fw.tt("dve", upp[:, 2 * d:2 * d + 2, :], v4(0)[:, 2 * d:2 * d + 2, :], bb_, ALU.mult,
                                          reads=[R_pb[0], R_ts], writes=[R_upp])
                                fw.cp("act", wT[:, :, :], v4(1), reads=[R_pb[1]], writes=[R_wT])
                                yield

                            def scan(n):
                                Td = (n, 15 - n)
                                tsl = lambda d: slice(Td[d] * 128, Td[d] * 128 + 128)
                                qkT, R_qkT = qkT2[n % 2]
                                Ktil, R_Ktil = Ktil2[n % 2]
                                upp, R_upp = upp2[n % 2]
                                wT, R_wT = wT2[n % 2]
                                qtT, R_qtT = qtT2[n % 2]
                                for sub in range(2):
                                    info = []
                                    for d in range(2):
                                        cc = sub if d == 0 else 1 - sub
                                        info.append((d, cc, slice(cc * 64, cc * 64 + 64)))
                                    for (d, cc, rows) in info:
                                        for hh in range(2):
                                            u = d * 2 + hh
                                            fw.mm(PB[6 + d][:, hh * 128:hh * 128 + 128], wT[:, u, :], Sb[:, u, :], reads=[R_wT, R_Sb[d]], writes=[R_psv[d]])
                                    yield
                                    for (d, cc, rows) in info:
                                        for hh in range(2):
                                            u = d * 2 + hh
                                            c_ = ucol(d, hh)
                                            fw.stt(vnew2[cc][rows, u, :], PB[6 + d][rows, hh * 128:hh * 128 + 128], nbeta[rows, Td[d], c_:c_ + 1],
                                                   upp[rows, u, :], ALU.mult, ALU.add, reads=[R_psv[d], R_ts, R_upp], writes=[R_vn[d]])
                                    yield
                                    for (d, cc, rows) in info:
                                        for hh in range(2):
                                            u = d * 2 + hh
                                            fw.mm(PB[6 + d][:, 256 + hh * 128:256 + hh * 128 + 128], Ktil[:, u, :], vnew2[cc][:, u, :],
                                                  reads=[R_Ktil, R_vn[d]], writes=[R_pss[d]])
                                        for hh in range(2):
                                            u = d * 2 + hh
                                            oc_ = u * 128 + cc * 64
                                            fw.mm(PB[5][:, oc_:oc_ + 64], Sb[:, u, :], qtT[:, u, rows], start=True, stop=False,
                                                  reads=[R_Sb[d], R_qtT], writes=[R_po[d]])
                                            fw.mm(PB[5][:, oc_:oc_ + 64], vnew2[cc][:, u, :], qkT[:, u, rows], start=False, stop=True,
                                                  reads=[R_vn[d], R_qkT], writes=[R_po[d]])
                                    yield
                                    for (d, cc, rows) in info:
                                        for hh in range(2):
                                            u = d * 2 + hh
                                            c_ = ucol(d, hh)
                                            fw.stt(S32[:, u, :], S32[:, u, :], gam[:, Td[d], cc, c_:c_ + 1], PB[6 + d][:, 256 + hh * 128:256 + hh * 128 + 128],
                                                   ALU.mult, ALU.add, reads=[R_S32[d], R_ts, R_pss[d]], writes=[R_S32[d]])
                                    yield
                                    for (d, cc, rows) in info:
                                        fw.cp("pool", Sb[:, 2 * d:2 * d + 2, :], S32[:, 2 * d:2 * d + 2, :], reads=[R_S32[d]], writes=[R_Sb[d]])
                                    yield
                                for d in range(2):
                                    fw.tt("dve", obuf[:, :, tsl(d)], obuf[:, :, tsl(d)], v4(5)[:, 2 * d:2 * d + 2, :], ALU.add,
                                          reads=[R_ob, R_po[d]], writes=[R_ob])
                                yield

                            def run_interleaved(gens, weights):
                                gens = [g for g in gens if g is not None]
                                alive = list(range(len(gens)))
                                while alive:
                                    for gi in list(alive):
                                        for _ in range(weights[gi]):
                                            try:
                                                next(gens[gi])
                                            except StopIteration:
                                                alive.remove(gi)
                                                break

                            run_interleaved([prep(0)], [1])
                            for n in range(16):
                                run_interleaved([prep(n + 1) if n < 15 else None, scan(n)], [3, 1])
                        fw.barrier()
                        tap("b_o", obuf[:, :, :], [R_ob], dst=(tap_d["b_o"][:, 2 * pp:2 * pp + 2, :] if "b_o" in tap_d else None))
                        with contextlib.ExitStack() as s3:
                            oT = fw.sb("b_oT", [128, 2, S], BF16, s3)
                            R_o = [fw.res(), fw.res()]
                            t13_, R_t13_ = fw.sb("b_t13", [128, 512], F32, s3), fw.res()
                            st3 = [(None, None, fw.sb(f"b_sq3{i}", [128, 512], BF16, s3), fw.res(),
                                    fw.sb(f"b_ln3{i}", [128, 512], F32, s3), fw.res(), fw.sb(f"b_rs3{i}", [128, 512], F32, s3), fw.res(),
                                    t13_, R_t13_) for i in range(2)]
                            wz, R_wz = load_w(win_d[l, :, P_B + pp * 1024 + 768:P_B + pp * 1024 + 1024], 256)
                            zsT = fw.sb("b_zsT", [128, 2, S], BF16, s3)
                            R_zsT = fw.res()
                            n = 0
                            for hh in range(2):
                                for tb in range(4):
                                    ts_ = slice(tb * 512, tb * 512 + 512)
                                    pz = n % 2
                                    n += 1
                                    proj_fm(wz, R_wz, hh * 128, 128, tb, PB[pz], R_pb[pz])
                                    fw.act(zsT[:, hh, ts_], PB[pz][:, :], AF.Silu, reads=[R_pb[pz]], writes=[R_zsT])
                            calls3 = [(hh, tb) for hh in range(2) for tb in range(4)]

                            def n3_front(i):
                                hh, tb = calls3[i]
                                ts_ = slice(tb * 512, tb * 512 + 512)
                                zs, R_zs, sq, R_sq, lnv, R_ln, rs, R_rs, t1, R_t1 = st3[i % 2]
                                fw.act(sq[:, :], obuf[:, hh, ts_], AF.Square, reads=[R_ob], writes=[R_sq])
                                fw.mm(PB[2 + i % 2][:, :], ones_b, sq[:, :], reads=[R_sq, R_cb], writes=[R_pb[2 + i % 2]])

                            def n3_back(i):
                                hh, tb = calls3[i]
                                ts_ = slice(tb * 512, tb * 512 + 512)
                                zs, R_zs, sq, R_sq, lnv, R_ln, rs, R_rs, t1, R_t1 = st3[i % 2]
                                pn = 2 + i % 2
                                fw.act(lnv[:, :], PB[pn][:, :], AF.Ln, reads=[R_pb[pn]], writes=[R_ln], scale=1.0 / 128, bias=eps_col)
                                fw.act(rs[:, :], lnv[:, :], AF.Exp, reads=[R_ln], writes=[R_rs], scale=-0.5)
                                fw.stt(t1[:, :], obuf[:, hh, ts_], cfc(f"onb_{l}"), rs[:, :], ALU.mult, ALU.mult,
                                       reads=[R_ob, R_cf, R_rs], writes=[R_t1])
                                fw.tt("dve", oT[:, hh, ts_], t1[:, :], zsT[:, hh, ts_], ALU.mult, reads=[R_t1, R_zsT], writes=[R_o[hh]])

                            n3_front(0)
                            for i in range(len(calls3)):
                                if i + 1 < len(calls3):
                                    n3_front(i + 1)
                                n3_back(i)
                            tap("b_oT", oT[:, :, :], R_o, dst=(tap_d["b_oT"][:, 2 * pp:2 * pp + 2, :] if "b_oT" in tap_d else None))
                            wout_update(l, [(256 + (2 * pp + hh) * 128, (lambda tb, hh=hh: oT[:, hh, tb * 512:tb * 512 + 512]), R_o[hh])
                                            for hh in range(2)], s3)
                        fw.barrier()
            fw.barrier()

        def ffn(l):
            with contextlib.ExitStack() as scr0:
                rmsnorm_fm(l, "n2", scr0)
            fw.barrier()
            with contextlib.ExitStack() as scr:
                NH = 12
                actT = fw.sb("f_act", [128, NH, S], BF16, scr)
                sg = [fw.sb(f"f_sg{i}", [128, 512], F32, scr) for i in range(2)]
                R_sg = [fw.res() for _ in range(2)]
                n = 0
                for (f0, nf) in ((0, 12), (12, 10)):
                    R_a = [[fw.res() for _ in range(4)] for _ in range(nf)]
                    for g in range(nf // 2):
                        gg = f0 // 2 + g
                        wv, R_wv = load_w(wgu_d[l, :, gg * 512:gg * 512 + 512], 512)
                        for fi in range(2):
                            f = g * 2 + fi
                            for tb in range(4):
                                pg = (n % 2) * 2
                                pu = pg + 1
                                proj_fm(wv, R_wv, fi * 256, 128, tb, PB[pg], R_pb[pg])
                                proj_fm(wv, R_wv, fi * 256 + 128, 128, tb, PB[pu], R_pb[pu])
                                si = n % 2
                                fw.act(sg[si][:, :], PB[pg][:, :], AF.Silu, reads=[R_pb[pg]], writes=[R_sg[si]])
                                fw.tt("dve", actT[:, f, tb * 512:tb * 512 + 512], sg[si][:, :], PB[pu][:, :], ALU.mult,
                                      reads=[R_sg[si], R_pb[pu]], writes=[R_a[f][tb]])
                                n += 1
                    for oc in range(8):
                        wv, R_wv = load_w(wdn_d[l, f0 * 128:(f0 + nf) * 128, oc * 128:oc * 128 + 128], 128, kchunks=nf)
                        for tb in range(4):
                            pb = 4 + (oc * 4 + tb) % 4
                            ts_ = slice(tb * 512, tb * 512 + 512)
                            for f in range(nf):
                                fw.mm(PB[pb][:, :], wv[:, f, :], actT[:, f, ts_], start=(f == 0), stop=(f == nf - 1),
                                      reads=[R_wv, R_a[f][tb]], writes=[R_pb[pb]])
                            fw.tt("dve", xT[:, oc, ts_], xT[:, oc, ts_], PB[pb][:, :], ALU.add,
                                  reads=[R_x[oc][tb], R_pb[pb]], writes=[R_x[oc][tb]])
                    fw.barrier()
            fw.barrier()

        eps_t = fw.sb("eps_t", [128, 1], F32)
        R_eps = fw.res()
        fw.op("pool", lambda e: e.memset(eps_t[:, :], EPS), writes=[R_eps])
        eps_col = eps_t[:, 0:1]
        fw.barrier()

        for l in range(nl):
            with contextlib.ExitStack() as scr:
                rmsnorm_fm(l, "n1", scr)
            if l == 0:
                tap("hT", hT[:, :, :], R_h)
            fw.barrier()
            if "skipA" not in taps:
                mixer_A(l)
            if "skipC" not in taps:
                mixer_C(l)
            if "skipB" not in taps:
                mixer_B(l)
            if l == 0:
                tap("xmid", xT[:, :, :], [r for rr in R_x for r in rr])
            ffn(l)

        yv = yT_d.rearrange("(c p) t -> p c t", p=128)
        for c in range(8):
            fw.dma("sp", yv[:, c, :], xT[:, c, :], reads=R_x[c], writes=[R_out], sem=s_st)
        fw.wait_all("sp", [R_out, R_tap])
        print("ninst", fw.ninst)
    return nc


TAP_SHAPES = {
    "hT": ((128, 8, S), BF16), "a_qT": ((128, 2, S), BF16), "a_oT": ((128, 2, S), BF16), "c_qT": ((128, 2, S), BF16),
    "c_oT": ((128, 2, S), BF16), "xmid": ((128, 8, S), F32), "f_act": ((128, NFF, S), BF16),
    "b_g": ((128, 16, 8), F32), "b_beta": ((128, 16, 8), F32), "b_gc": ((128, 16, 8), F32),
    "b_q": ((128, 4, S), BF16), "b_k": ((128, 4, S), BF16), "b_v": ((128, 4, S), BF16), "b_o": ((128, 4, S), F32),
    "b_oT": ((128, 4, S), BF16),
}


_PROG_CACHE = {}


def _prep_inputs(inputs, nl):
    perm_in = _win_perm()
    perm_out = _wout_perm()
    perm_gu = _wgu_perm()
    w_in = np.ascontiguousarray(inputs["w_in"][:nl][:, :, perm_in])
    w_out = np.ascontiguousarray(inputs["w_out"][:nl][:, perm_out, :])
    w_gu = np.ascontiguousarray(inputs["w_gate_up"][:nl][:, :, perm_gu])
    w_dn = np.ascontiguousarray(inputs["w_down"][:nl])
    cf, cb16, ropeA, ropeC, _ = _build_consts(inputs, nl)
    shared = {"w_in": w_in, "w_out": w_out, "w_gu": w_gu, "w_dn": w_dn, "cf": cf, "cb": cb16,
              "ropeA": ropeA, "ropeC": ropeC}
    return shared


def kernel(**inputs):
    inputs = {k: np.asarray(v) for k, v in inputs.items()}
    nl = L_FULL
    shared = _prep_inputs(inputs, nl)
    x = inputs["x"]
    in_maps = []
    for b in range(8):
        m = dict(shared)
        m["xT"] = np.ascontiguousarray(x[b].T)
        in_maps.append(m)
    if nl not in _PROG_CACHE:
        _PROG_CACHE[nl] = build_program(nl)
    res = run_bass_kernel_spmd(_PROG_CACHE[nl], in_maps, core_ids=list(range(8)))
    out = np.stack([np.ascontiguousarray(r["yT"].T) for r in res.results], axis=0)
    return out.astype(np.float32)
```

```python
import contextlib
import numpy as np
import ml_dtypes
import concourse.bass as bass
import concourse.mybir as mybir
from concourse.bass_utils import run_bass_kernel_spmd

F32 = mybir.dt.float32
BF16 = mybir.dt.bfloat16
AF = mybir.ActivationFunctionType
ALU = mybir.AluOpType

L_FULL = 4
S = 2048
D = 1024
DFF = 2816
NFF = 22
IN_DIM = 3344
EPS = 1e-6
NEG = -30000.0


class Res:
    __slots__ = ("name", "w", "r", "psum")

    def __init__(self, name, psum=False):
        self.name = name
        self.w = None
        self.r = {}
        self.psum = psum


class FW:
    ENG = ("pe", "act", "dve", "pool", "sp")

    def __init__(self, nc, stack):
        self.nc = nc
        self.stack = stack
        self.e = {"pe": nc.tensor, "act": nc.scalar, "dve": nc.vector, "pool": nc.gpsimd, "sp": nc.sync}
        self.sems = {}
        self.cnt = {}
        self.waited = {k: {} for k in self.ENG}
        for k in self.ENG:
            self.sems[k] = stack.enter_context(nc.semaphore("s_" + k))
            self.cnt[k] = 0
        self.nres = 0
        self.ninst = {k: 0 for k in self.ENG}

    def sb(self, name, shape, dt, stack=None):
        self.nsb = getattr(self, "nsb", 0) + 1
        return (stack or self.stack).enter_context(self.nc.sbuf_tensor(f"{name}_{self.nsb}", list(shape), dt))

    def ps(self, name, shape, dt=F32):
        return self.stack.enter_context(self.nc.psum_tensor(name, list(shape), dt))

    def res(self, name=None, psum=False):
        self.nres += 1
        return Res(name or f"r{self.nres}", psum)

    def new_sem(self, name):
        s = self.stack.enter_context(self.nc.semaphore(name))
        self.sems[name] = s
        self.cnt[name] = 0
        return name

    @staticmethod
    def _flat(xs):
        out = []
        for x in xs:
            if isinstance(x, (list, tuple)):
                out.extend(FW._flat(x))
            else:
                out.append(x)
        return out

    def _deps(self, eng, reads, writes, waiter=None):
        waiter = waiter or eng
        deps = {}

        def need(sv):
            if sv is None:
                return
            s, v = sv
            if deps.get(s, 0) < v:
                deps[s] = v

        for r in reads:
            need(r.w)
            if r.psum:
                for s, v in r.r.items():
                    if s != eng:
                        need((s, v))
        for w in writes:
            need(w.w)
            for s, v in w.r.items():
                need((s, v))
        out = {}
        for s, v in deps.items():
            if s == eng:
                if eng == "pe":
                    continue
            if self.waited[waiter].get(s, 0) >= v:
                continue
            out[s] = v
        return out

    def _emit_waits(self, eng, deps):
        for s, v in deps.items():
            self.e[eng].wait_ge(self.sems[s], v)
            self.waited[eng][s] = v
            self.ninst[eng] += 1

    def op(self, eng, fn, reads=(), writes=()):
        reads, writes = self._flat(reads), self._flat(writes)
        deps = self._deps(eng, reads, writes)
        self._emit_waits(eng, deps)
        inst = fn(self.e[eng])
        self.cnt[eng] += 1
        self.ninst[eng] += 1
        v = self.cnt[eng]
        inst.then_inc(self.sems[eng], 1)
        for r in reads:
            if r.r.get(eng, 0) < v:
                r.r[eng] = v
        for w in writes:
            w.w = (eng, v)
            w.r = {}
        return inst

    def dma(self, q, out, in_, reads=(), writes=(), sem=None, **kw):
        reads, writes = self._flat(reads), self._flat(writes)
        d2 = self._deps(None, reads, writes, waiter=q)
        self._emit_waits(q, d2)
        inst = self.e[q].dma_start(out=out, in_=in_, **kw)
        self.cnt[sem] += 16
        v = self.cnt[sem]
        inst.then_inc(self.sems[sem], 16)
        self.ninst[q] += 1
        for r in reads:
            if r.r.get(sem, 0) < v:
                r.r[sem] = v
        for w in writes:
            w.w = (sem, v)
            w.r = {}
        return inst

    def wait_all(self, eng, resources):
        d2 = self._deps(None, self._flat(resources), (), waiter=eng)
        self._emit_waits(eng, d2)

    def barrier(self):
        snap = dict(self.cnt)
        for eng in self.ENG:
            d = {}
            for s, v in snap.items():
                if v > 0 and s != eng and self.waited[eng].get(s, 0) < v:
                    d[s] = v
            self._emit_waits(eng, d)

    def mm(self, out, lhsT, rhs, start=True, stop=True, reads=(), writes=(), skip=False):
        return self.op("pe", lambda e: e.matmul(out, lhsT, rhs, start=start, stop=stop, skip_group_check=skip), reads, writes)

    def tr(self, out, in_, ident, reads=(), writes=()):
        return self.op("pe", lambda e: e.transpose(out, in_, ident), reads, writes)

    def act(self, out, in_, func, reads=(), writes=(), **kw):
        return self.op("act", lambda e: e.activation(out=out, in_=in_, func=func, **kw), reads, writes)

    def tt(self, eng, out, in0, in1, op, reads=(), writes=()):
        return self.op(eng, lambda e: e.tensor_tensor(out=out, in0=in0, in1=in1, op=op), reads, writes)

    def stt(self, out, in0, scalar, in1, op0, op1, reads=(), writes=()):
        return self.op("dve", lambda e: e.scalar_tensor_tensor(out=out, in0=in0, scalar=scalar, in1=in1,
                                                                op0=op0, op1=op1), reads, writes)

    def ts(self, eng, out, in0, s1, s2, op0, op1=None, reads=(), writes=()):
        if op1 is None:
            return self.op(eng, lambda e: e.tensor_scalar(out=out, in0=in0, scalar1=s1, scalar2=None, op0=op0),
                           reads, writes)
        return self.op(eng, lambda e: e.tensor_scalar(out=out, in0=in0, scalar1=s1, scalar2=s2, op0=op0, op1=op1),
                       reads, writes)

    def cp(self, eng, out, in_, reads=(), writes=()):
        if eng == "act":
            return self.act(out, in_, AF.Copy, reads, writes)
        return self.op(eng, lambda e: e.tensor_copy(out=out, in_=in_), reads, writes)


O_QA, O_KA, O_VA = 0, 256, 512
O_QB, O_KB, O_VB, O_ZB = 768, 1280, 1792, 2304
O_AB, O_BB = 2816, 2824
O_QC, O_KC, O_VC = 2832, 3088, 3216

P_QA, P_KA, P_VA, P_AB = 0, 256, 512, 768
P_QC, P_KC, P_VC = 784, 1040, 1168
P_B = 1296


def _win_perm():
    idx = []
    idx += list(range(O_QA, O_QA + 256)) + list(range(O_KA, O_KA + 256)) + list(range(O_VA, O_VA + 256))
    idx += list(range(O_AB, O_AB + 16))
    for h in (0, 2, 1, 3):
        idx += list(range(O_QC + 64 * h, O_QC + 64 * h + 64))
    idx += list(range(O_KC, O_KC + 128)) + list(range(O_VC, O_VC + 128))
    for pp in range(2):
        for base in (O_QB, O_KB, O_VB, O_ZB):
            idx += list(range(base + 256 * pp, base + 256 * pp + 256))
    assert len(idx) == IN_DIM and len(set(idx)) == IN_DIM
    return np.array(idx)


def _wout_perm():
    idx = list(range(0, 768))
    for h in (0, 2, 1, 3):
        idx += list(range(768 + 64 * h, 768 + 64 * h + 64))
    return np.array(idx)


def _wgu_perm():
    idx = []
    for f in range(NFF):
        idx += list(range(f * 128, f * 128 + 128)) + list(range(DFF + f * 128, DFF + f * 128 + 128))
    return np.array(idx)


def _cf_layout(nl):
    off = {}
    c = 0

    def add(name, n):
        nonlocal c
        off[name] = c
        c += n

    for l in range(nl):
        add(f"n1_{l}", 8)
        add(f"n2_{l}", 8)
        add(f"qna_{l}", 1)
        add(f"kna_{l}", 1)
        add(f"qnc_{l}", 1)
        add(f"knc_{l}", 1)
        add(f"onb_{l}", 1)
        add(f"cw_{l}", 60)
        add(f"alog_{l}", 8)
        add(f"dtb_{l}", 8)
    add("ident", 128)
    add("ones", 128)
    add("uuf", 128)
    add("uub", 128)
    add("slf", 128)
    add("slb", 128)
    for d in range(2):
        for cc in range(2):
            add(f"sla_{d}{cc}", 128)
    off["_n"] = c
    return off


CB = {"ident": 0, "ones": 128, "blk64": 256, "ra": 384, "rc": 512, "strict": 640,
      "maskA": 768, "mask16": 1280, "gmask": 3328, "_n": 3840}


def _build_consts(inputs, nl):
    off = _cf_layout(nl)
    cf = np.zeros((128, off["_n"]), np.float32)
    p = np.arange(128)
    for l in range(nl):
        cf[:, off[f"n1_{l}"]:off[f"n1_{l}"] + 8] = inputs["norm1"][l].reshape(8, 128).T
        cf[:, off[f"n2_{l}"]:off[f"n2_{l}"] + 8] = inputs["norm2"][l].reshape(8, 128).T
        cf[:, off[f"qna_{l}"]] = inputs["qn_a"][l][p % 64]
        cf[:, off[f"kna_{l}"]] = inputs["kn_a"][l][p % 64]
        cf[:, off[f"qnc_{l}"]] = inputs["qn_c"][l][p % 64]
        cf[:, off[f"knc_{l}"]] = inputs["kn_c"][l][p % 64]
        cf[:, off[f"onb_{l}"]] = inputs["onorm_b"][l]
        cw = inputs["conv_b"][l]
        cf[:, off[f"cw_{l}"]:off[f"cw_{l}"] + 60] = cw.reshape(5, 12, 128).transpose(2, 1, 0).reshape(128, 60)
        cf[:, off[f"alog_{l}"]:off[f"alog_{l}"] + 8] = inputs["a_log_b"][l].reshape(1, 8)
        cf[:, off[f"dtb_{l}"]:off[f"dtb_{l}"] + 8] = inputs["dt_bias_b"][l].reshape(1, 8)
    j = p[:, None]
    i = p[None, :]
    same = (j // 64) == (i // 64)
    cf[:, off["ident"]:off["ident"] + 128] = (j == i)
    cf[:, off["ones"]:off["ones"] + 128] = 1.0
    cf[:, off["uuf"]:off["uuf"] + 128] = same & (j <= i)
    cf[:, off["uub"]:off["uub"] + 128] = same & (j >= i)
    cf[:, off["slf"]:off["slf"] + 128] = (j == 64 * (i // 64) + 63)
    cf[:, off["slb"]:off["slb"] + 128] = (j == 64 * (i // 64))
    for d in range(2):
        for cc in range(2):
            last = 64 * cc + (63 if d == 0 else 0)
            cf[:, off[f"sla_{d}{cc}"]:off[f"sla_{d}{cc}"] + 128] = (j == last) * np.ones((1, 128))

    cb = np.zeros((128, CB["_n"]), np.float32)
    cb[:, CB["ident"]:CB["ident"] + 128] = (j == i)
    cb[:, CB["ones"]:CB["ones"] + 128] = 1.0
    cb[:, CB["blk64"]:CB["blk64"] + 128] = same
    def rot_mat(pairs_fn):
        R = np.zeros((128, 128), np.float32)
        for m in range(128):
            hd = m % 64
            base = m - hd
            k, sgn = pairs_fn(hd)
            if k is not None:
                R[base + k, m] = sgn
        return R

    def pa(hd):
        if hd < 8:
            return hd + 8, -1.0
        if hd < 16:
            return hd - 8, 1.0
        return None, 0.0

    def pc(hd):
        blk = hd // 32
        r = hd % 32
        if r < 16:
            return blk * 32 + r + 16, -1.0
        return blk * 32 + r - 16, 1.0

    cb[:, CB["ra"]:CB["ra"] + 128] = rot_mat(pa)
    cb[:, CB["rc"]:CB["rc"] + 128] = rot_mat(pc)
    cb[:, CB["strict"]:CB["strict"] + 128] = (j != i)
    pk = p[:, None]
    f128 = np.arange(128)[None, :]
    f64 = np.arange(64)[None, :]
    mfull = np.where(np.abs(f128 - pk) <= 64, 0.0, NEG)
    mprev = np.where(pk >= f64 + 64, 0.0, NEG)
    mnext = np.where(pk <= f64, 0.0, NEG)
    one = np.concatenate([mfull, mprev, mnext], axis=1)
    cb[:, CB["maskA"]:CB["maskA"] + 512] = np.concatenate([one, one], axis=1)
    for B in range(4):
        blk = mfull[:, 32 * B:32 * B + 32]
        cb[:, CB["mask16"] + 512 * B:CB["mask16"] + 512 * B + 512] = np.tile(blk, (1, 16))
    gf = np.where(same & (i >= j), 0.0, NEG)
    gb = np.where(same & (i <= j), 0.0, NEG)
    cb[:, CB["gmask"]:CB["gmask"] + 512] = np.concatenate([gf, gf, gb, gb], axis=1)
    cb16 = cb.astype(ml_dtypes.bfloat16)

    t = np.arange(S, dtype=np.float32)
    ropeA = np.zeros((128, 2, S), np.float32)
    ropeC = np.zeros((128, 2, S), np.float32)
    invA = np.float32(500000.0) ** (-np.arange(8, dtype=np.float32) / 8)
    invC = np.float32(10000.0) ** (-np.arange(16, dtype=np.float32) / 16)
    rowp = (np.arange(S) // 64).astype(np.float32)
    colp = (np.arange(S) % 64).astype(np.float32)
    for m in range(128):
        hd = m % 64
        if hd < 16:
            ang = t * invA[hd % 8]
            ropeA[m, 0] = np.cos(ang)
            ropeA[m, 1] = np.sin(ang)
        else:
            ropeA[m, 0] = 1.0
        pos = rowp if hd < 32 else colp
        ang = pos * invC[(hd % 32) % 16]
        ropeC[m, 0] = np.cos(ang)
        ropeC[m, 1] = np.sin(ang)
    return cf, cb16, ropeA.astype(ml_dtypes.bfloat16), ropeC.astype(ml_dtypes.bfloat16), off


def build_program(nl, taps=()):
    taps = set(taps)
    nc = bass.Bass("TRN2", target_bir_lowering=False)
    off = _cf_layout(nl)
    xT_d = nc.dram_tensor("xT", [D, S], F32, kind="ExternalInput").ap()
    win_d = nc.dram_tensor("w_in", [nl, D, IN_DIM], F32, kind="ExternalInput").ap()
    wout_d = nc.dram_tensor("w_out", [nl, D, D], F32, kind="ExternalInput").ap()
    wgu_d = nc.dram_tensor("w_gu", [nl, D, 2 * DFF], F32, kind="ExternalInput").ap()
    wdn_d = nc.dram_tensor("w_dn", [nl, DFF, D], F32, kind="ExternalInput").ap()
    cf_d = nc.dram_tensor("cf", [128, off["_n"]], F32, kind="ExternalInput").ap()
    cb_d = nc.dram_tensor("cb", [128, CB["_n"]], BF16, kind="ExternalInput").ap()
    ropeA_d = nc.dram_tensor("ropeA", [128, 2, S], BF16, kind="ExternalInput").ap()
    ropeC_d = nc.dram_tensor("ropeC", [128, 2, S], BF16, kind="ExternalInput").ap()
    yT_d = nc.dram_tensor("yT", [D, S], F32, kind="ExternalOutput").ap()
    tap_d = {}
    for name, (shape, tdt) in TAP_SHAPES.items():
        if name in taps:
            tap_d[name] = nc.dram_tensor("tap_" + name, list(shape), tdt, kind="ExternalOutput").ap()

    with contextlib.ExitStack() as st:
        fw = FW(nc, st)
        xT = fw.sb("xT_s", [128, 8, S], F32)
        hT = fw.sb("hT_s", [128, 8, S], BF16)
        cf = fw.sb("cf_s", [128, off["_n"]], F32)
        cb = fw.sb("cb_s", [128, CB["_n"]], BF16)
        wbuf = [fw.sb(f"wbuf{i}", [128, 4096], BF16) for i in range(2)]
        R_x = [[fw.res(f"x{c}_{tb}") for tb in range(4)] for c in range(8)]
        R_h = [fw.res(f"h{tb}") for tb in range(4)]
        R_cf = fw.res("cf")
        R_cb = fw.res("cb")
        R_w = [fw.res("w0"), fw.res("w1")]
        R_out = fw.res("out")
        R_tap = fw.res("tap")
        PB = [fw.ps(f"pb{i}", [128, 512], F32) for i in range(8)]
        R_pb = [fw.res(f"pb{i}", psum=True) for i in range(8)]
        s_ld = fw.new_sem("ld")
        s_w = [fw.new_sem("w0"), fw.new_sem("w1")]
        s_st = fw.new_sem("st")
        s_tap = fw.new_sem("tap")
        wstate = {"i": 0}

        def cfc(name, n=1, o=0):
            return cf[:, off[name] + o: off[name] + o + n]

        def cbm(name, n=128, o=0):
            return cb[:, CB[name] + o: CB[name] + o + n]

        ident_b = cbm("ident")
        ones_b = cbm("ones")
        blk64_b = cbm("blk64")
        ident_f = cfc("ident", 128)
        ones_f = cfc("ones", 128)

        fw.dma("sp", cf[:, :], cf_d[:, :], writes=[R_cf], sem=s_ld)
        fw.dma("sp", cb[:, :], cb_d[:, :], writes=[R_cb], sem=s_ld)
        xv = xT_d.rearrange("(c p) t -> p c t", p=128)
        for c in range(8):
            fw.dma("sp", xT[:, c, :], xv[:, c, :], writes=R_x[c], sem=s_ld)

        def load_w(dram_ap, ncols, kchunks=8):
            i = wstate["i"]
            wstate["i"] = 1 - i
            view = wbuf[i][:, 0:kchunks * ncols].rearrange("p (k n) -> p k n", k=kchunks)
            fw.dma("pool", view, dram_ap.rearrange("(k p) n -> p k n", p=128), writes=[R_w[i]], sem=s_w[i])
            return view, R_w[i]

        def tap(name, src_ap, reads, dst=None):
            if name not in tap_d:
                return
            fw.dma("sp", dst if dst is not None else tap_d[name], src_ap, reads=reads, writes=[R_tap], sem=s_tap)

        def rmsnorm_fm(l, which, scr):
            sq = [fw.sb(f"nsq{i}", [128, 512], BF16, scr) for i in range(2)]
            R_sq = [fw.res() for _ in range(2)]
            lnv = [fw.sb(f"nln{i}", [128, 512], F32, scr) for i in range(2)]
            rstd = [fw.sb(f"nrs{i}", [128, 512], F32, scr) for i in range(2)]
            R_ln = [fw.res() for _ in range(2)]
            R_rs = [fw.res() for _ in range(2)]
            for tb in range(4):
                ts_ = slice(tb * 512, tb * 512 + 512)
                pb = tb % 2
                for c in range(8):
                    i = c % 2
                    fw.act(sq[i][:, :], xT[:, c, ts_], AF.Square, reads=[R_x[c][tb]], writes=[R_sq[i]])
                    fw.mm(PB[pb][:, :], ones_b, sq[i][:, :], start=(c == 0), stop=(c == 7),
                          reads=[R_sq[i], R_cb], writes=[R_pb[pb]])
                fw.act(lnv[pb][:, :], PB[pb][:, :], AF.Ln, reads=[R_pb[pb]], writes=[R_ln[pb]],
                       scale=1.0 / D, bias=eps_col)
                fw.act(rstd[pb][:, :], lnv[pb][:, :], AF.Exp, reads=[R_ln[pb]], writes=[R_rs[pb]], scale=-0.5)
                for c in range(8):
                    fw.stt(hT[:, c, ts_], xT[:, c, ts_], cfc(f"{which}_{l}", 1, c), rstd[pb][:, :],
                           ALU.mult, ALU.mult, reads=[R_x[c][tb], R_rs[pb], R_cf], writes=[R_h[tb]])

        def qk_post(pbank, R_pbank, wcol, rot_b, cos_ap, sin_ap, R_tab, out_ap, R_out_, tmp, scale_q=None):
            (qw, R_qw, sq, R_sq2, lnv, R_ln2, rs, R_rs2, t1, R_t1, t2, R_t2, pss, R_pss, psr, R_psr) = tmp
            fw.act(qw[:, :], pbank[:, :], AF.Copy, reads=[R_pbank, R_cf], writes=[R_qw], scale=wcol)
            fw.act(sq[:, :], pbank[:, :], AF.Square, reads=[R_pbank, R_cf], writes=[R_sq2], scale=wcol)
            fw.mm(pss[:, :], blk64_b, sq[:, :], reads=[R_sq2, R_cb], writes=[R_pss])
            fw.mm(psr[:, :], rot_b, qw[:, :], reads=[R_qw, R_cb], writes=[R_psr])
            fw.act(lnv[:, :], pss[:, :], AF.Ln, reads=[R_pss], writes=[R_ln2], scale=1.0 / 64, bias=eps_col)
            fw.act(rs[:, :], lnv[:, :], AF.Exp, reads=[R_ln2], writes=[R_rs2], scale=-0.5)
            fw.tt("dve", t1[:, :], qw[:, :], cos_ap, ALU.mult, reads=[R_qw, R_tab], writes=[R_t1])
            fw.tt("dve", t2[:, :], psr[:, :], sin_ap, ALU.mult, reads=[R_psr, R_tab], writes=[R_t2])
            fw.tt("dve", t1[:, :], t1[:, :], t2[:, :], ALU.add, reads=[R_t1, R_t2], writes=[R_t1])
            fw.tt("dve", out_ap, t1[:, :], rs[:, :], ALU.mult, reads=[R_t1, R_rs2], writes=[R_out_])

        def qk_tmp(scr, tag, pss_i, psr_i):
            return (fw.sb(f"qw{tag}", [128, 512], BF16, scr), fw.res(),
                    fw.sb(f"qsq{tag}", [128, 512], BF16, scr), fw.res(),
                    fw.sb(f"qln{tag}", [128, 512], F32, scr), fw.res(),
                    fw.sb(f"qrs{tag}", [128, 512], F32, scr), fw.res(),
                    fw.sb(f"qt1{tag}", [128, 512], F32, scr), fw.res(),
                    fw.sb(f"qt2{tag}", [128, 512], F32, scr), fw.res(),
                    PB[pss_i], R_pb[pss_i], PB[psr_i], R_pb[psr_i])

        def proj_fm(wv, R_wv, col0, m, tb, pbank, R_pbank):
            ts_ = slice(tb * 512, tb * 512 + 512)
            for k in range(8):
                fw.mm(pbank[0:m, :], wv[:, k, col0:col0 + m], hT[:, k, ts_], start=(k == 0), stop=(k == 7),
                      reads=[R_wv, R_h[tb]], writes=[R_pbank])

        def wout_update(l, o_tiles, scr):
            for half in range(2):
                views = []
                for (row0, apf, R_o) in o_tiles:
                    wv, R_wv = load_w(wout_d[l, row0:row0 + 128, half * 512:half * 512 + 512], 512, kchunks=1)
                    views.append((wv, R_wv, apf, R_o))
                for oc4 in range(4):
                    oc = half * 4 + oc4
                    for tb in range(4):
                        pb = (oc4 * 4 + tb) % 4
                        n = len(views)
                        for i, (wv, R_wv, apf, R_o) in enumerate(views):
                            fw.mm(PB[pb][:, :], wv[:, 0, oc4 * 128:oc4 * 128 + 128], apf(tb), start=(i == 0),
                                  stop=(i == n - 1), reads=[R_wv, R_o], writes=[R_pb[pb]])
                        ts_ = slice(tb * 512, tb * 512 + 512)
                        fw.tt("dve", xT[:, oc, ts_], xT[:, oc, ts_], PB[pb][:, :], ALU.add,
                              reads=[R_x[oc][tb], R_pb[pb]], writes=[R_x[oc][tb]])

        def run_attn_jobs(jobs, PT, R_pt, sbanks=(2, 3, 4, 5), LA=2):
            n = len(jobs)
            for step in range(n + LA):
                if step < n:
                    j = jobs[step]
                    sb_i = sbanks[step % len(sbanks)]
                    pt_i = step % len(PT)
                    Sb, R_S = PB[sb_i], R_pb[sb_i]
                    nq = len(j["qk"])
                    for bi, (scol, nn, k_ap, q_ap) in enumerate(j["qk"]):
                        fw.mm(Sb[:, scol:scol + nn], k_ap, q_ap, start=(bi == 0), stop=(j["mask"] is None and bi == nq - 1),
                              reads=j["reads"], writes=[R_S])
                    rng = []
                    for (scol, nn, _, _) in sorted(j["qk"], key=lambda t: t[0]):
                        if rng and rng[-1][1] == scol:
                            rng[-1][1] = scol + nn
                        else:
                            rng.append([scol, scol + nn])
                    if j["mask"] is not None:
                        mname, moff = j["mask"]
                        for ri, (lo, hi) in enumerate(rng):
                            fw.mm(Sb[:, lo:hi], ident_b, cbm(mname, hi - lo, moff + lo), start=False, stop=(ri == len(rng) - 1),
                                  reads=[R_cb], writes=[R_S])
                    for (lo, hi) in rng:
                        fw.act(PT[pt_i][:, lo:hi], Sb[:, lo:hi], AF.Exp, reads=[R_S], writes=[R_pt[pt_i]], scale=0.125)
                if step >= LA:
                    j = jobs[step - LA]
                    pt_i = (step - LA) % len(PT)
                    for (acc_ap, v_ap, scol, nn, st_, sp_) in j["pv"]:
                        fw.mm(acc_ap, v_ap, PT[pt_i][:, scol:scol + nn], start=st_, stop=sp_,
                              reads=[R_pt[pt_i], j["R_v"]], writes=[j["R_acc"]])
                    if j["fin"] is not None:
                        j["fin"]()

        def attn_finalize(acc, R_acc, hh, out_ap, R_o, tmp):
            lnd, R_lnd, rd, R_rd = tmp
            nr = slice(hh * 64, hh * 64 + 64)
            dr = slice((1 - hh) * 64, (1 - hh) * 64 + 64)
            fw.act(lnd[nr, :], acc[dr, :], AF.Ln, reads=[R_acc], writes=[R_lnd])
            fw.act(rd[nr, :], lnd[nr, :], AF.Exp, reads=[R_lnd], writes=[R_rd], scale=-1.0)
            fw.tt("dve", out_ap, acc[nr, :], rd[nr, :], ALU.mult, reads=[R_acc, R_rd], writes=[R_o])

        def mixer_A(l):
            with contextlib.ExitStack() as scr:
                qT = fw.sb("a_qT", [128, 2, S], BF16, scr)
                kT = fw.sb("a_kT", [128, 2, S], BF16, scr)
                R_q = [fw.res() for _ in range(2)]
                R_k = [fw.res() for _ in range(2)]
                kTz = fw.sb("a_kTz", [128, 2, S], BF16, scr)
                R_kz = fw.res()
                tab = fw.sb("a_tab", [128, 2, S], BF16, scr)
                R_tab = fw.res()
                fw.dma("sp", tab[:, :, :], ropeA_d[:, :, :], writes=[R_tab], sem=s_ld)
                vx = fw.sb("a_vx", [128, 3, 16, 2, 128], BF16, scr)
                R_vx = fw.res()
                oT = fw.sb("a_oT", [128, 2, S], BF16, scr)
                R_o = [fw.res() for _ in range(2)]
                PT = [fw.sb(f"a_pt{i}", [128, 512], BF16, scr) for i in range(3)]
                R_pt = [fw.res() for _ in range(3)]
                t0_ = qk_tmp(scr, "a0", 2, 4)
                t1_ = (fw.sb("qwa1", [128, 512], BF16, scr), fw.res(), fw.sb("qsqa1", [128, 512], BF16, scr), fw.res(),
                       t0_[4], t0_[5], fw.sb("qrsa1", [128, 512], F32, scr), fw.res(), t0_[8], t0_[9], t0_[10], t0_[11],
                       PB[3], R_pb[3], PB[5], R_pb[5])
                tmps = [t0_, t1_]
                ftmp = (t0_[4], t0_[5], t0_[6], t0_[7])
                wv, R_wv = load_w(win_d[l, :, P_QA:P_QA + 512], 512)
                wv2, R_wv2 = load_w(win_d[l, :, P_VA:P_VA + 272], 272)
                n = 0
                for ci in range(4):
                    isq = ci < 2
                    dst, Rd = (qT, R_q) if isq else (kT, R_k)
                    c = ci % 2
                    for tb in range(4):
                        ts_ = slice(tb * 512, tb * 512 + 512)
                        pb = n % 2
                        proj_fm(wv, R_wv, ci * 128, 128, tb, PB[pb], R_pb[pb])
                        qk_post(PB[pb], R_pb[pb], cfc(f"qna_{l}" if isq else f"kna_{l}"), cbm("ra"),
                                tab[:, 0, ts_], tab[:, 1, ts_], R_tab, dst[:, c, ts_], Rd[c], tmps[n % 2])
                        n += 1
                tap("a_qT", qT[:, :, :], R_q)
                for c in range(2):
                    for tb in range(4):
                        pb = n % 2
                        n += 1
                        proj_fm(wv2, R_wv2, c * 128, 128, tb, PB[pb], R_pb[pb])
                        fw.cp("act", oT[:, c, tb * 512:tb * 512 + 512], PB[pb][:, :], reads=[R_pb[pb]], writes=[R_o[c]])
                for hp in range(2):
                    fw.op("pool", lambda e: e.memset(vx[:, :, :, 0, 64:128], 1.0), reads=[], writes=[R_vx])
                    fw.op("pool", lambda e: e.memset(vx[:, :, :, 1, 0:64], 1.0), reads=[], writes=[R_vx])
                    n = 0
                    for pat, dil in enumerate((1, 4, 16)):
                        for tile in range(16):
                            if dil == 1:
                                t0, st_ = tile * 128, 1
                            elif dil == 4:
                                r, m = tile // 4, tile % 4
                                t0, st_ = r + 512 * m, 4
                            else:
                                t0, st_ = tile, 16
                            pb = 6 + (n // 4) % 2
                            sub = n % 4
                            pbv = PB[pb][:, :].bitcast(BF16)
                            fw.tr(pbv[:, sub * 128:sub * 128 + 128], oT[:, hp, t0:t0 + 127 * st_ + 1:st_], ident_b,
                                  reads=[R_o[hp], R_cb], writes=[R_pb[pb]])
                            if sub == 3:
                                tl = tile - 3
                                src = pbv[:, 0:512].rearrange("p (t h d) -> p t h d", t=4, h=2)
                                fw.cp("dve", vx[:, pat, tl:tl + 4, 0, 0:64], src[:, :, 0, :],
                                      reads=[R_pb[pb]], writes=[R_vx])
                                fw.cp("act", vx[:, pat, tl:tl + 4, 1, 64:128], src[:, :, 1, :],
                                      reads=[R_pb[pb]], writes=[R_vx])
                            n += 1
                    fw.op("pool", lambda e: e.memset(kTz[64:128, 0, :], 0.0), writes=[R_kz])
                    fw.op("pool", lambda e: e.memset(kTz[0:64, 1, :], 0.0), writes=[R_kz])
                    fw.cp("pool", kTz[0:64, 0, :], kT[0:64, hp, :], reads=[R_k[hp]], writes=[R_kz])
                    fw.cp("pool", kTz[64:128, 1, :], kT[64:128, hp, :], reads=[R_k[hp]], writes=[R_kz])
                    jobs = []
                    for hh in range(2):
                        pr = slice(hh * 64, hh * 64 + 64)
                        c = hp
                        for B in range(4):
                            acc_i = (hh * 4 + B) % 2
                            acc, R_acc = PB[acc_i], R_pb[acc_i]
                            group = []

                            def add_job(blocks, mask_ap, group=group, acc=acc, R_acc=R_acc, c=c, hh=hh):
                                j = {"qk": [(scol, nn, k_ap, q_ap) for (scol, nn, q_ap, k_ap, pat, tile, acc_ap) in blocks],
                                     "mask": mask_ap, "ncols": 512, "reads": [R_q[c], R_kz],
                                     "pv": [[acc_ap, vx[:, pat, tile, hh, :], scol, nn, False, False]
                                            for (scol, nn, q_ap, k_ap, pat, tile, acc_ap) in blocks],
                                     "R_acc": R_acc, "R_v": R_vx, "fin": None}
                                group.append(j)

                            for half in range(2):
                                blocks = []
                                for qi in range(2):
                                    ml = half * 2 + qi
                                    m = 4 * B + ml
                                    so = qi * 256
                                    blocks.append((so, 128, qT[:, c, m * 128:m * 128 + 128],
                                                   kTz[:, hh, m * 128:m * 128 + 128], 0, m, acc[:, ml * 128:ml * 128 + 128]))
                                    if m > 0:
                                        blocks.append((so + 128, 64, qT[:, c, m * 128:m * 128 + 64],
                                                       kTz[:, hh, (m - 1) * 128:m * 128], 0, m - 1,
                                                       acc[:, ml * 128:ml * 128 + 64]))
                                    if m < 15:
                                        blocks.append((so + 192, 64, qT[:, c, m * 128 + 64:m * 128 + 128],
                                                       kTz[:, hh, (m + 1) * 128:(m + 2) * 128], 0, m + 1,
                                                       acc[:, ml * 128 + 64:ml * 128 + 128]))
                                add_job(blocks, ("maskA", 0))
                            for half in range(2):
                                blocks = []
                                for qi in range(2):
                                    r = half * 2 + qi
                                    so = qi * 256
                                    b0 = 512 * B + r

                                    def kt(mm_, r=r, hh=hh):
                                        return kTz[:, hh, 512 * mm_ + r:512 * mm_ + 512:4]
                                    blocks.append((so, 128, qT[:, c, b0:512 * B + 512:4], kt(B), 1, r * 4 + B,
                                                   acc[:, r:512:4]))
                                    if B > 0:
                                        blocks.append((so + 128, 64, qT[:, c, b0:512 * B + 256:4], kt(B - 1), 1,
                                                       r * 4 + B - 1, acc[:, r:256:4]))
                                    if B < 3:
                                        blocks.append((so + 192, 64, qT[:, c, b0 + 256:512 * B + 512:4], kt(B + 1), 1,
                                                       r * 4 + B + 1, acc[:, 256 + r:512:4]))
                                add_job(blocks, ("maskA", 0))
                            blocks = []
                            for b in range(16):
                                blocks.append((b * 32, 32, qT[:, c, 512 * B + b:512 * B + 512:16],
                                               kTz[:, hh, b:S:16], 2, b, acc[:, b:512:16]))
                            add_job(blocks, ("mask16", 512 * B))
                            group[0]["pv"][0][4] = True
                            group[-1]["pv"][-1][5] = True
                            group[-1]["fin"] = (lambda acc=acc, R_acc=R_acc, hh=hh, pr=pr, c=c, B=B:
                                                attn_finalize(acc, R_acc, hh, oT[pr, c, B * 512:B * 512 + 512], R_o[c], ftmp))
                            jobs += group
                    run_attn_jobs(jobs, PT, R_pt)
                tap("a_oT", oT[:, :, :], R_o)
                wout_update(l, [(c * 128, (lambda tb, c=c: oT[:, c, tb * 512:tb * 512 + 512]), R_o[c]) for c in range(2)], scr)
            fw.barrier()

        def mixer_C(l):
            with contextlib.ExitStack() as scr:
                qT = fw.sb("c_qT", [128, 2, S], BF16, scr)
                kT = fw.sb("c_kT", [128, S], BF16, scr)
                R_q = [fw.res() for _ in range(2)]
                R_k = fw.res()
                kTz = fw.sb("c_kTz", [128, 2, S], BF16, scr)
                R_kz = fw.res()
                tab = fw.sb("c_tab", [128, 2, S], BF16, scr)
                R_tab = fw.res()
                fw.dma("sp", tab[:, :, :], ropeC_d[:, :, :], writes=[R_tab], sem=s_ld)
                vx = fw.sb("c_vx", [128, 16, 2, 128], BF16, scr)
                R_vx = fw.res()
                oT = fw.sb("c_oT", [128, 2, S], BF16, scr)
                R_o = [fw.res() for _ in range(2)]
                PT = [fw.sb(f"c_pt{i}", [128, 512], BF16, scr) for i in range(3)]
                R_pt = [fw.res() for _ in range(3)]
                lnd = fw.sb("c_lnd", [128, 512], F32, scr)
                rd = fw.sb("c_rd", [128, 512], F32, scr)
                ftmp = (lnd, fw.res(), rd, fw.res())
                tmps = [qk_tmp(scr, "c0", 2, 4), qk_tmp(scr, "c1", 3, 5)]
                wv, R_wv = load_w(win_d[l, :, P_QC:P_QC + 512], 512)
                n = 0
                for ci in range(3):
                    for tb in range(4):
                        ts_ = slice(tb * 512, tb * 512 + 512)
                        pb = n % 2
                        proj_fm(wv, R_wv, ci * 128, 128, tb, PB[pb], R_pb[pb])
                        if ci < 2:
                            dst, Rd, wn = qT[:, ci, ts_], R_q[ci], f"qnc_{l}"
                        else:
                            dst, Rd, wn = kT[:, ts_], R_k, f"knc_{l}"
                        qk_post(PB[pb], R_pb[pb], cfc(wn), cbm("rc"), tab[:, 0, ts_], tab[:, 1, ts_], R_tab,
                                dst, Rd, tmps[n % 2])
                        n += 1
                tap("c_qT", qT[:, :, :], R_q)
                fw.op("pool", lambda e: e.memset(kTz[64:128, 0, :], 0.0), writes=[R_kz])
                fw.op("pool", lambda e: e.memset(kTz[0:64, 1, :], 0.0), writes=[R_kz])
                fw.cp("pool", kTz[0:64, 0, :], kT[0:64, :], reads=[R_k], writes=[R_kz])
                fw.cp("pool", kTz[64:128, 1, :], kT[64:128, :], reads=[R_k], writes=[R_kz])
                fw.op("pool", lambda e: e.memset(vx[:, :, 0, 64:128], 1.0), reads=[], writes=[R_vx])
                fw.op("pool", lambda e: e.memset(vx[:, :, 1, 0:64], 1.0), reads=[], writes=[R_vx])
                for tile in range(16):
                    pb = 6 + (tile // 4) % 2
                    sub = tile % 4
                    for k in range(8):
                        fw.mm(PB[pb][:, sub * 128:sub * 128 + 128], hT[:, k, tile * 128:tile * 128 + 128],
                              wv[:, k, 384:512], start=(k == 0), stop=(k == 7), reads=[R_wv] + R_h, writes=[R_pb[pb]])
                    if sub == 3:
                        tl = tile - 3
                        src = PB[pb][:, :].rearrange("p (t h d) -> p t h d", t=4, h=2)
                        fw.cp("dve", vx[:, tl:tl + 4, 0, 0:64], src[:, :, 0, :], reads=[R_pb[pb]], writes=[R_vx])
                        fw.cp("dve", vx[:, tl:tl + 4, 1, 64:128], src[:, :, 1, :], reads=[R_pb[pb]], writes=[R_vx])
                jobs = []
                for c in range(2):
                    for hh in range(2):
                        pr = slice(hh * 64, hh * 64 + 64)
                        for B in range(4):
                            acc_i = (c * 8 + hh * 4 + B) % 2
                            acc, R_acc = PB[acc_i], R_pb[acc_i]
                            for kt in range(16):
                                j = {"qk": [(0, 512, kTz[:, hh, kt * 128:kt * 128 + 128], qT[:, c, B * 512:B * 512 + 512])],
                                     "mask": None, "ncols": 512, "reads": [R_q[c], R_kz],
                                     "pv": [[acc[:, :], vx[:, kt, hh, :], 0, 512, kt == 0, kt == 15]],
                                     "R_acc": R_acc, "R_v": R_vx, "fin": None}
                                if kt == 15:
                                    j["fin"] = (lambda acc=acc, R_acc=R_acc, hh=hh, pr=pr, c=c, B=B:
                                                attn_finalize(acc, R_acc, hh, oT[pr, c, B * 512:B * 512 + 512], R_o[c], ftmp))
                                jobs.append(j)
                run_attn_jobs(jobs, PT, R_pt)
                tap("c_oT", oT[:, :, :], R_o)
                wout_update(l, [(768 + c * 128, (lambda tb, c=c: oT[:, c, tb * 512:tb * 512 + 512]), R_o[c])
                                for c in range(2)], scr)
            fw.barrier()


        def mixer_B(l):
            with contextlib.ExitStack() as scr:
                def tk(name, n=8):
                    return fw.sb("b_" + name, [128, 16, n], F32, scr)
                ab_tok = tk("ab", 16)
                beta = tk("beta"); nbeta = tk("nbeta"); g_tok = tk("g"); gc = tk("gc"); ngc = tk("ngc")
                egc = tk("egc"); kdec = tk("kdec")
                gam = fw.sb("b_gam", [128, 16, 2, 8], F32, scr)
                ea = fw.sb("b_ea", [128, 8], F32, scr)
                R_ts = fw.res()
                wv, R_wv = load_w(win_d[l, :, P_AB:P_AB + 16], 16)
                for tile in range(16):
                    for k in range(8):
                        fw.mm(PB[0][:, tile * 16:tile * 16 + 16], hT[:, k, tile * 128:tile * 128 + 128], wv[:, k, 0:16],
                              start=(k == 0), stop=(k == 7), reads=[R_wv] + R_h, writes=[R_pb[0]])
                fw.cp("dve", ab_tok[:, :, :], PB[0][:, 0:256].rearrange("p (t n) -> p t n", t=16),
                      reads=[R_pb[0]], writes=[R_ts])
                fw.act(beta[:, :, :], ab_tok[:, :, 8:16], AF.Tanh, reads=[R_ts], writes=[R_ts], scale=0.5)
                fw.ts("dve", beta[:, :, :], beta[:, :, :], 0.5, 0.5, ALU.mult, ALU.add, reads=[R_ts], writes=[R_ts])
                fw.ts("dve", nbeta[:, :, :], beta[:, :, :], -1.0, None, ALU.mult, reads=[R_ts], writes=[R_ts])
                dtb_b = cfc(f"dtb_{l}", 8).unsqueeze(1).broadcast_to([128, 16, 8])
                fw.tt("dve", g_tok[:, :, :], ab_tok[:, :, 0:8], dtb_b, ALU.add, reads=[R_ts, R_cf], writes=[R_ts])
                fw.act(g_tok[:, :, :], g_tok[:, :, :], AF.Exp, reads=[R_ts], writes=[R_ts])
                fw.act(g_tok[:, :, :], g_tok[:, :, :], AF.Ln, reads=[R_ts], writes=[R_ts], bias=cfc("ones", 1))
                fw.act(ea[:, :], cfc(f"alog_{l}", 8), AF.Exp, reads=[R_cf], writes=[R_ts])
                fw.stt(g_tok[:, :, :], g_tok[:, :, :], -1.0, ea[:, :].unsqueeze(1).broadcast_to([128, 16, 8]),
                       ALU.mult, ALU.mult, reads=[R_ts], writes=[R_ts])
                tap("b_g", g_tok[:, :, :], [R_ts])
                tap("b_beta", beta[:, :, :], [R_ts])
                for tile in range(16):
                    for d in range(2):
                        fw.mm(PB[1][:, tile * 8 + d * 4:tile * 8 + d * 4 + 4], cfc("uuf" if d == 0 else "uub", 128),
                              g_tok[:, tile, d * 4:d * 4 + 4], reads=[R_ts, R_cf], writes=[R_pb[1]])
                fw.cp("dve", gc[:, :, :], PB[1][:, 0:128].rearrange("p (t n) -> p t n", t=16), reads=[R_pb[1]], writes=[R_ts])
                fw.ts("dve", ngc[:, :, :], gc[:, :, :], -1.0, None, ALU.mult, reads=[R_ts], writes=[R_ts])
                fw.act(egc[:, :, :], gc[:, :, :], AF.Exp, reads=[R_ts], writes=[R_ts])
                for tile in range(16):
                    for d in range(2):
                        fw.mm(PB[2][:, tile * 8 + d * 4:tile * 8 + d * 4 + 4], cfc("slf" if d == 0 else "slb", 128),
                              gc[:, tile, d * 4:d * 4 + 4], reads=[R_ts, R_cf], writes=[R_pb[2]])
                        for cc in range(2):
                            o_ = (tile * 2 + cc) * 8 + d * 4
                            fw.mm(PB[3][:, o_:o_ + 4], cfc(f"sla_{d}{cc}", 128), gc[:, tile, d * 4:d * 4 + 4],
                                  reads=[R_ts, R_cf], writes=[R_pb[3]])
                fw.tt("dve", kdec[:, :, :], PB[2][:, 0:128].rearrange("p (t n) -> p t n", t=16), gc[:, :, :], ALU.subtract,
                      reads=[R_pb[2], R_ts], writes=[R_ts])
                fw.act(kdec[:, :, :], kdec[:, :, :], AF.Exp, reads=[R_ts], writes=[R_ts])
                fw.act(gam[:, :, :, :], PB[3][:, 0:256].rearrange("p (t c n) -> p t c n", t=16, c=2), AF.Exp,
                       reads=[R_pb[3]], writes=[R_ts])
                tap("b_gc", gc[:, :, :], [R_ts])

                for pp in range(2):
                    with contextlib.ExitStack() as sp_:
                        bq = fw.sb("b_q", [128, 2, S], BF16, sp_)
                        bk = fw.sb("b_k", [128, 2, S], BF16, sp_)
                        K_tok = fw.sb("b_Kt", [128, 16, 2, 128], BF16, sp_)
                        V_tok = fw.sb("b_Vt", [128, 16, 2, 128], BF16, sp_)
                        R_bq = fw.res(); R_bk = fw.res(); R_Kt = fw.res(); R_Vt = fw.res()
                        with contextlib.ExitStack() as s1:
                            bv = fw.sb("b_v", [128, 2, S], BF16, s1)
                            raws = [fw.sb(f"b_raw{i}", [128, S + 4], BF16, s1) for i in range(2)]
                            dgs = [fw.sb(f"b_dg{i}", [128, 5, 128], BF16, s1) for i in range(2)]
                            sqs = [fw.sb(f"b_sq{i}", [128, 512], BF16, s1) for i in range(2)]
                            lnvs = [fw.sb(f"b_ln{i}", [128, 512], F32, s1) for i in range(2)]
                            rss = [fw.sb(f"b_rs{i}", [128, 512], F32, s1) for i in range(2)]
                            R_bv = fw.res()
                            R_raws = [fw.res(), fw.res()]; R_dgs = [fw.res(), fw.res()]; R_sqs = [fw.res(), fw.res()]
                            R_lns = [fw.res(), fw.res()]; R_rss = [fw.res(), fw.res()]
                            for i in range(2):
                                fw.op("pool", lambda e: e.memset(raws[i][:, 0:2], 0.0), writes=[R_raws[i]])
                                fw.op("pool", lambda e: e.memset(raws[i][:, S + 2:S + 4], 0.0), writes=[R_raws[i]])
                            nchunk = 0
                            wq, R_wq = load_w(win_d[l, :, P_B + pp * 1024:P_B + pp * 1024 + 512], 512)
                            wz, R_wz = load_w(win_d[l, :, P_B + pp * 1024 + 512:P_B + pp * 1024 + 1024], 512)
                            n = 0
                            for kind in range(3):
                                for hh in range(2):
                                    wsrc, R_ws, col0 = (wq, R_wq, kind * 256 + hh * 128) if kind < 2 else (wz, R_wz, hh * 128)
                                    dst, R_dst = ((bq, R_bq), (bk, R_bk), (bv, R_bv))[kind]
                                    cch = kind * 4 + 2 * pp + hh
                                    raw, R_raw, dg, R_dg = raws[nchunk % 2], R_raws[nchunk % 2], dgs[nchunk % 2], R_dgs[nchunk % 2]
                                    nchunk += 1
                                    for k in range(5):
                                        fw.ts("dve", dg[:, k, :], ident_b, cfc(f"cw_{l}", 1, cch * 5 + k), None, ALU.mult,
                                              reads=[R_cb, R_cf], writes=[R_dg])
                                    for tb in range(4):
                                        pb = n % 2
                                        n += 1
                                        proj_fm(wsrc, R_ws, col0, 128, tb, PB[pb], R_pb[pb])
                                        fw.cp("act", raw[:, 2 + tb * 512:2 + tb * 512 + 512], PB[pb][:, :],
                                              reads=[R_pb[pb]], writes=[R_raw])
                                    for tb in range(4):
                                        pb = 2 + n % 2
                                        n += 1
                                        ts_ = slice(tb * 512, tb * 512 + 512)
                                        for k in range(5):
                                            fw.mm(PB[pb][:, :], dg[:, k, :], raw[:, tb * 512 + k:tb * 512 + k + 512],
                                                  start=(k == 0), stop=(k == 4), reads=[R_dg, R_raw], writes=[R_pb[pb]])
                                        fw.act(dst[:, hh, ts_], PB[pb][:, :], AF.Silu, reads=[R_pb[pb]], writes=[R_dst])
                            calls = [(kind, hh, tb) for kind in range(2) for hh in range(2) for tb in range(4)]
                            R_l2 = {(kind, hh, tb): fw.res() for (kind, hh, tb) in calls}

                            def l2_front(i):
                                kind, hh, tb = calls[i]
                                dst, R_dst = ((bq, R_bq), (bk, R_bk))[kind]
                                ts_ = slice(tb * 512, tb * 512 + 512)
                                fw.act(sqs[i % 2][:, :], dst[:, hh, ts_], AF.Square, reads=[R_dst], writes=[R_sqs[i % 2]])
                                fw.mm(PB[4 + i % 2][:, :], ones_b, sqs[i % 2][:, :], reads=[R_sqs[i % 2], R_cb], writes=[R_pb[4 + i % 2]])

                            def l2_back(i):
                                kind, hh, tb = calls[i]
                                dst, R_dst = ((bq, R_bq), (bk, R_bk))[kind]
                                ts_ = slice(tb * 512, tb * 512 + 512)
                                pb = 4 + i % 2
                                fw.act(lnvs[i % 2][:, :], PB[pb][:, :], AF.Ln, reads=[R_pb[pb]], writes=[R_lns[i % 2]], bias=eps_col)
                                fw.act(rss[i % 2][:, :], lnvs[i % 2][:, :], AF.Exp, reads=[R_lns[i % 2]], writes=[R_rss[i % 2]], scale=-0.5)
                                fw.stt(dst[:, hh, ts_], dst[:, hh, ts_], (128.0 ** -0.5) if kind == 0 else 1.0, rss[i % 2][:, :],
                                       ALU.mult, ALU.mult, reads=[R_dst, R_rss[i % 2]], writes=[R_l2[calls[i]]])

                            l2_front(0)
                            for i in range(len(calls)):
                                if i + 1 < len(calls):
                                    l2_front(i + 1)
                                l2_back(i)
                            fw.op("pool", lambda e: e.memset(dgs[0][:, 0, 0:1], 0.0), reads=[R_l2[c_] for c_ in calls if c_[0] == 0], writes=[R_bq])
                            fw.op("pool", lambda e: e.memset(dgs[0][:, 0, 0:1], 0.0), reads=[R_l2[c_] for c_ in calls if c_[0] == 1], writes=[R_bk])
                            tap("b_q", bq[:, :, :], [R_bq], dst=(tap_d["b_q"][:, 2 * pp:2 * pp + 2, :] if "b_q" in tap_d else None))
                            tap("b_k", bk[:, :, :], [R_bk], dst=(tap_d["b_k"][:, 2 * pp:2 * pp + 2, :] if "b_k" in tap_d else None))
                            tap("b_v", bv[:, :, :], [R_bv], dst=(tap_d["b_v"][:, 2 * pp:2 * pp + 2, :] if "b_v" in tap_d else None))
                            for src, R_src, dstt, R_dt in ((bk, R_bk, K_tok, R_Kt), (bv, R_bv, V_tok, R_Vt)):
                                for hh in range(2):
                                    for t4 in range(4):
                                        pb = 6 + n % 2
                                        n += 1
                                        pbv = PB[pb][:, :].bitcast(BF16)
                                        for i in range(4):
                                            tile = t4 * 4 + i
                                            fw.tr(pbv[:, i * 128:i * 128 + 128], src[:, hh, tile * 128:tile * 128 + 128], ident_b,
                                                  reads=[R_src, R_cb], writes=[R_pb[pb]])
                                        fw.cp("act", dstt[:, t4 * 4:t4 * 4 + 4, hh, :],
                                              pbv[:, 0:512].rearrange("p (t n) -> p t n", t=4), reads=[R_pb[pb]], writes=[R_dt])
                        fw.barrier()
                        obuf = fw.sb("b_ob", [128, 2, S], F32, sp_)
                        R_ob = fw.res()
                        fw.op("pool", lambda e: e.memset(obuf[:, :, :], 0.0), writes=[R_ob])
                        with contextlib.ExitStack() as s2:
                            def t4(name, dt, nres=1):
                                return fw.sb("b_" + name, [128, 4, 128], dt, s2), (fw.res() if nres == 1 else [fw.res() for _ in range(nres)])
                            KKs, R_KKs = t4("KKs", BF16)
                            GU, R_GU = t4("GU", F32, 2)
                            EG, R_EG = t4("EG", BF16)
                            Et, R_Et = t4("Et", BF16, 4)
                            X, R_X = t4("X", BF16, 4)
                            Yb = [t4(f"Y{i}", BF16) for i in range(2)]
                            Zb = [(X, R_X), t4("Z1", BF16)]
                            Pbb = [t4(f"Pb{i}", BF16) for i in range(2)]
                            Keg, R_Keg = t4("Keg", BF16, 2)
                            qkT2 = [t4(f"qkT{i}", BF16) for i in range(2)]
                            Ktil2 = [t4(f"Ktil{i}", BF16, 2) for i in range(2)]
                            upp2 = [t4("upp0", F32, 2)] * 2
                            wT2 = [t4(f"wT{i}", BF16) for i in range(2)]
                            qtT2 = [t4(f"qtT{i}", BF16, 2) for i in range(2)]
                            S32 = fw.sb("b_S32", [128, 4, 128], F32, s2)
                            Sb = fw.sb("b_Sb", [128, 4, 128], BF16, s2)
                            vnew2 = [fw.sb(f"b_vnew{i}", [128, 4, 128], BF16, s2) for i in range(2)]
                            R_S32 = [[fw.res(), fw.res()], [fw.res(), fw.res()]]; R_Sb = [fw.res(), fw.res()]
                            R_vn = [[fw.res(), fw.res()], [fw.res(), fw.res()]]
                            R_psv = [R_pb[6], R_pb[7]]; R_pss = [R_pb[6], R_pb[7]]; R_po = [R_pb[5], R_pb[5]]
                            fw.op("pool", lambda e: e.memset(S32[:, :, :], 0.0), writes=R_S32)
                            fw.op("pool", lambda e: e.memset(Sb[:, :, :], 0.0), writes=R_Sb)
                            fw.op("pool", lambda e: e.memset(vnew2[0][:, :, :], 0.0), writes=R_vn)
                            fw.op("pool", lambda e: e.memset(vnew2[1][:, :, :], 0.0), writes=R_vn)
                            v4 = lambda i: PB[i][:, :].rearrange("p (u n) -> p u n", u=4)
                            strict_b4 = cbm("strict").unsqueeze(1).broadcast_to([128, 4, 128])
                            ident_b4 = ident_b.unsqueeze(1).broadcast_to([128, 4, 128])
                            ucol = lambda d, hh: d * 4 + 2 * pp + hh

                            def prep(n):
                                Td = (n, 15 - n)
                                tsl = lambda d: slice(Td[d] * 128, Td[d] * 128 + 128)
                                qkT, R_qkT = qkT2[n % 2]
                                Ktil, R_Ktil = Ktil2[n % 2]
                                upp, R_upp = upp2[n % 2]
                                wT, R_wT = wT2[n % 2]
                                qtT, R_qtT = qtT2[n % 2]
                                for d in range(2):
                                    for hh in range(2):
                                        u = d * 2 + hh
                                        fw.mm(PB[0][:, u * 128:u * 128 + 128], bk[:, hh, tsl(d)], bk[:, hh, tsl(d)],
                                              reads=[R_bk], writes=[R_pb[0]])
                                        fw.mm(PB[1][:, u * 128:u * 128 + 128], bk[:, hh, tsl(d)], bq[:, hh, tsl(d)],
                                              reads=[R_bk, R_bq], writes=[R_pb[1]])
                                for d in range(2):
                                    uu = cfc("uuf" if d == 0 else "uub", 128).unsqueeze(1).broadcast_to([128, 2, 128])
                                    gb_ = g_tok[:, Td[d], ucol(d, 0):ucol(d, 0) + 2].unsqueeze(2).broadcast_to([128, 2, 128])
                                    fw.tt("pool", GU[:, 2 * d:2 * d + 2, :], uu, gb_, ALU.mult, reads=[R_cf, R_ts], writes=[R_GU[d]])
                                yield
                                fw.tt("dve", KKs[:, :, :], v4(0), strict_b4, ALU.mult, reads=[R_pb[0], R_cb], writes=[R_KKs])
                                fw.mm(PB[4][:, :], ones_f, GU[:, :, :].rearrange("p u n -> p (u n)"), start=True, stop=True,
                                      reads=[R_GU, R_cf], writes=[R_pb[4]])
                                fw.mm(PB[2][:, :], ones_f, GU[:, :, :].rearrange("p u n -> p (u n)"), start=True, stop=False,
                                      reads=[R_GU, R_cf], writes=[R_pb[2]])
                                fw.mm(PB[2][:, :], ident_b, cbm("gmask", 512), start=False, stop=True, reads=[R_cb], writes=[R_pb[2]])
                                yield
                                fw.act(EG[:, :, :], v4(4), AF.Exp, reads=[R_pb[4]], writes=[R_EG])
                                yield
                                for d in range(2):
                                    for hh in range(2):
                                        u = d * 2 + hh
                                        c_ = ucol(d, hh)
                                        fw.act(Et[:, u, :], PB[2][:, u * 128:u * 128 + 128], AF.Exp, reads=[R_pb[2], R_ts],
                                               writes=[R_Et[u]], bias=ngc[:, Td[d], c_:c_ + 1])
                                for d in range(2):
                                    c0 = ucol(d, 0)
                                    eb = egc[:, Td[d], c0:c0 + 2].unsqueeze(2).broadcast_to([128, 2, 128])
                                    kb_ = kdec[:, Td[d], c0:c0 + 2].unsqueeze(2).broadcast_to([128, 2, 128])
                                    fw.tt("pool", Keg[:, 2 * d:2 * d + 2, :], K_tok[:, Td[d], :, :], eb, ALU.mult, reads=[R_Kt, R_ts], writes=[R_Keg[d]])
                                    fw.tt("pool", Ktil[:, 2 * d:2 * d + 2, :], K_tok[:, Td[d], :, :], kb_, ALU.mult, reads=[R_Kt, R_ts], writes=[R_Ktil[d]])
                                    fw.tt("pool", qtT[:, 2 * d:2 * d + 2, :], bq[:, :, tsl(d)], EG[:, 2 * d:2 * d + 2, :], ALU.mult,
                                          reads=[R_bq, R_EG], writes=[R_qtT[d]])
                                yield
                                for d in range(2):
                                    for hh in range(2):
                                        u = d * 2 + hh
                                        c_ = ucol(d, hh)
                                        fw.stt(X[:, u, :], Et[:, u, :], beta[:, Td[d], c_:c_ + 1], KKs[:, u, :], ALU.mult, ALU.mult,
                                               reads=[R_Et[u], R_ts, R_KKs], writes=[R_X[u]])
                                fw.tt("dve", qkT[:, :, :], v4(1), Et[:, :, :], ALU.mult, reads=[R_pb[1], R_Et], writes=[R_qkT])
                                yield
                                pbv3 = PB[3][:, :].bitcast(BF16)
                                for u in range(4):
                                    fw.tr(pbv3[:, u * 128:u * 128 + 128], X[:, u, :], ident_b, reads=[R_X[u], R_cb], writes=[R_pb[3]])
                                (Yc, R_Yc), (Zc, R_Zc) = Yb[0], (X, R_X)
                                Pc, R_Pc = Pbb[0]
                                fw.tt("dve", Pc[:, :, :], ident_b4, X[:, :, :], ALU.subtract, reads=[R_cb, R_X], writes=[R_Pc])
                                yield
                                fw.cp("act", Yc[:, :, :], pbv3[:, 0:512].rearrange("p (u n) -> p u n", u=4), reads=[R_pb[3]], writes=[R_Yc])
                                yield
                                for s_ in range(1, 6):
                                    Yn, R_Yn = Yb[s_ % 2]
                                    Zn, R_Zn = Zb[s_ % 2]
                                    Pn, R_Pn = Pbb[s_ % 2]
                                    for u in range(4):
                                        fw.mm(PB[3][:, u * 128:u * 128 + 128], Zc[:, u, :], Yc[:, u, :], reads=[R_Zc, R_Yc], writes=[R_pb[3]])
                                    if s_ <= 4:
                                        for u in range(4):
                                            fw.mm(PB[4][:, u * 128:u * 128 + 128], Yc[:, u, :], Zc[:, u, :], reads=[R_Zc, R_Yc], writes=[R_pb[4]])
                                    yield
                                    fw.cp("act", Yn[:, :, :], v4(3), reads=[R_pb[3]], writes=[R_Yn])
                                    if s_ <= 4:
                                        fw.cp("dve", Zn[:, :, :], v4(4), reads=[R_pb[4]], writes=[R_Zn])
                                    yield
                                    fw.mm(PB[0][:, :], ident_b, Pc[:, :, :].rearrange("p u n -> p (u n)"), start=True, stop=False,
                                          reads=[R_cb, R_Pc], writes=[R_pb[0]])
                                    for u in range(4):
                                        fw.mm(PB[0][:, u * 128:u * 128 + 128], Yn[:, u, :], Pc[:, u, :], start=False, stop=(u == 3),
                                              reads=[R_Yn, R_Pc], writes=[R_pb[0]])
                                    yield
                                    fw.cp("act" if s_ % 2 else "dve", Pn[:, :, :], v4(0), reads=[R_pb[0]], writes=[R_Pn])
                                    (Yc, R_Yc), (Zc, R_Zc), (Pc, R_Pc) = (Yn, R_Yn), (Zn, R_Zn), (Pn, R_Pn)
                                    yield
                                for d in range(2):
                                    for hh in range(2):
                                        u = d * 2 + hh
                                        fw.mm(PB[0][:, u * 128:u * 128 + 128], Pc[:, u, :], V_tok[:, Td[d], hh, :], reads=[R_Pc, R_Vt], writes=[R_pb[0]])
                                        fw.mm(PB[1][:, u * 128:u * 128 + 128], Keg[:, u, :], Pc[:, u, :], reads=[R_Pc, R_Keg[d]], writes=[R_pb[1]])
                                yield
                                for d in range(2):
                                    c0 = ucol(d, 0)
                                    bb_ = beta[:, Td[d], c0:c0 + 2].unsqueeze(2).broadcast_to([128, 2, 128])
                                    fw.tt("dve", upp[:, 2 * d:2 * d + 2, :], v4(0)[:, 2 * d:2 * d + 2, :], bb_, ALU.mult,
                                          reads=[R_pb[0], R_ts], writes=[R_upp[d]])
                                fw.cp("act", wT[:, :, :], v4(1), reads=[R_pb[1]], writes=[R_wT])
                                yield

                            def scan(n):
                                Td = (n, 15 - n)
                                tsl = lambda d: slice(Td[d] * 128, Td[d] * 128 + 128)
                                qkT, R_qkT = qkT2[n % 2]
                                Ktil, R_Ktil = Ktil2[n % 2]
                                upp, R_upp = upp2[n % 2]
                                wT, R_wT = wT2[n % 2]
                                qtT, R_qtT = qtT2[n % 2]
                                for sub in range(2):
                                    info = []
                                    for d in range(2):
                                        cc = sub if d == 0 else 1 - sub
                                        info.append((d, cc, slice(cc * 64, cc * 64 + 64)))
                                    for (d, cc, rows) in info:
                                        for hh in range(2):
                                            u = d * 2 + hh
                                            fw.mm(PB[6 + d][:, hh * 128:hh * 128 + 128], wT[:, u, :], Sb[:, u, :], reads=[R_wT, R_Sb[d]], writes=[R_psv[d]])
                                    yield
                                    for (d, cc, rows) in info:
                                        for hh in range(2):
                                            u = d * 2 + hh
                                            c_ = ucol(d, hh)
                                            fw.stt(vnew2[cc][rows, u, :], PB[6 + d][rows, hh * 128:hh * 128 + 128], nbeta[rows, Td[d], c_:c_ + 1],
                                                   upp[rows, u, :], ALU.mult, ALU.add, reads=[R_psv[d], R_ts, R_upp[d]], writes=[R_vn[d][hh]])
                                    yield
                                    for (d, cc, rows) in info:
                                        for hh in range(2):
                                            u = d * 2 + hh
                                            fw.mm(PB[6 + d][:, 256 + hh * 128:256 + hh * 128 + 128], Ktil[:, u, :], vnew2[cc][:, u, :],
                                                  reads=[R_Ktil[d], R_vn[d][hh]], writes=[R_pss[d]])
                                        for hh in range(2):
                                            u = d * 2 + hh
                                            oc_ = u * 128 + cc * 64
                                            fw.mm(PB[5][:, oc_:oc_ + 64], Sb[:, u, :], qtT[:, u, rows], start=True, stop=False,
                                                  reads=[R_Sb[d], R_qtT[d]], writes=[R_po[d]])
                                            fw.mm(PB[5][:, oc_:oc_ + 64], vnew2[cc][:, u, :], qkT[:, u, rows], start=False, stop=True,
                                                  reads=[R_vn[d][hh], R_qkT], writes=[R_po[d]])
                                    yield
                                    for (d, cc, rows) in info:
                                        for hh in range(2):
                                            u = d * 2 + hh
                                            c_ = ucol(d, hh)
                                            fw.stt(S32[:, u, :], S32[:, u, :], gam[:, Td[d], cc, c_:c_ + 1], PB[6 + d][:, 256 + hh * 128:256 + hh * 128 + 128],
                                                   ALU.mult, ALU.add, reads=[R_S32[d][hh], R_ts, R_pss[d]], writes=[R_S32[d][hh]])
                                    yield
                                    for (d, cc, rows) in info:
                                        fw.cp("pool", Sb[:, 2 * d:2 * d + 2, :], S32[:, 2 * d:2 * d + 2, :], reads=[R_S32[d]], writes=[R_Sb[d]])
                                    yield
                                for d in range(2):
                                    fw.tt("dve", obuf[:, :, tsl(d)], obuf[:, :, tsl(d)], v4(5)[:, 2 * d:2 * d + 2, :], ALU.add,
                                          reads=[R_ob, R_po[d]], writes=[R_ob])
                                yield

                            def run_interleaved(gens, weights):
                                gens = [g for g in gens if g is not None]
                                alive = list(range(len(gens)))
                                while alive:
                                    for gi in list(alive):
                                        for _ in range(weights[gi]):
                                            try:
                                                next(gens[gi])
                                            except StopIteration:
                                                alive.remove(gi)
                                                break

                            run_interleaved([prep(0)], [1])
                            for n in range(16):
                                run_interleaved([prep(n + 1) if n < 15 else None, scan(n)], [3, 1])
                        fw.barrier()
                        tap("b_o", obuf[:, :, :], [R_ob], dst=(tap_d["b_o"][:, 2 * pp:2 * pp + 2, :] if "b_o" in tap_d else None))
                        with contextlib.ExitStack() as s3:
                            oT = fw.sb("b_oT", [128, 2, S], BF16, s3)
                            R_o = [fw.res(), fw.res()]
                            t13_, R_t13_ = fw.sb("b_t13", [128, 512], F32, s3), fw.res()
                            st3 = [(None, None, fw.sb(f"b_sq3{i}", [128, 512], BF16, s3), fw.res(),
                                    fw.sb(f"b_ln3{i}", [128, 512], F32, s3), fw.res(), fw.sb(f"b_rs3{i}", [128, 512], F32, s3), fw.res(),
                                    t13_, R_t13_) for i in range(2)]
                            wz, R_wz = load_w(win_d[l, :, P_B + pp * 1024 + 768:P_B + pp * 1024 + 1024], 256)
                            zsT = fw.sb("b_zsT", [128, 2, S], BF16, s3)
                            R_zsT = fw.res()
                            n = 0
                            for hh in range(2):
                                for tb in range(4):
                                    ts_ = slice(tb * 512, tb * 512 + 512)
                                    pz = n % 2
                                    n += 1
                                    proj_fm(wz, R_wz, hh * 128, 128, tb, PB[pz], R_pb[pz])
                                    fw.act(zsT[:, hh, ts_], PB[pz][:, :], AF.Silu, reads=[R_pb[pz]], writes=[R_zsT])
                            calls3 = [(hh, tb) for hh in range(2) for tb in range(4)]

                            def n3_front(i):
                                hh, tb = calls3[i]
                                ts_ = slice(tb * 512, tb * 512 + 512)
                                zs, R_zs, sq, R_sq, lnv, R_ln, rs, R_rs, t1, R_t1 = st3[i % 2]
                                fw.act(sq[:, :], obuf[:, hh, ts_], AF.Square, reads=[R_ob], writes=[R_sq])
                                fw.mm(PB[2 + i % 2][:, :], ones_b, sq[:, :], reads=[R_sq, R_cb], writes=[R_pb[2 + i % 2]])

                            def n3_back(i):
                                hh, tb = calls3[i]
                                ts_ = slice(tb * 512, tb * 512 + 512)
                                zs, R_zs, sq, R_sq, lnv, R_ln, rs, R_rs, t1, R_t1 = st3[i % 2]
                                pn = 2 + i % 2
                                fw.act(lnv[:, :], PB[pn][:, :], AF.Ln, reads=[R_pb[pn]], writes=[R_ln], scale=1.0 / 128, bias=eps_col)
                                fw.act(rs[:, :], lnv[:, :], AF.Exp, reads=[R_ln], writes=[R_rs], scale=-0.5)
                                fw.stt(t1[:, :], obuf[:, hh, ts_], cfc(f"onb_{l}"), rs[:, :], ALU.mult, ALU.mult,
                                       reads=[R_ob, R_cf, R_rs], writes=[R_t1])
                                fw.tt("dve", oT[:, hh, ts_], t1[:, :], zsT[:, hh, ts_], ALU.mult, reads=[R_t1, R_zsT], writes=[R_o[hh]])

                            n3_front(0)
                            for i in range(len(calls3)):
                                if i + 1 < len(calls3):
                                    n3_front(i + 1)
                                n3_back(i)
                            tap("b_oT", oT[:, :, :], R_o, dst=(tap_d["b_oT"][:, 2 * pp:2 * pp + 2, :] if "b_oT" in tap_d else None))
                            wout_update(l, [(256 + (2 * pp + hh) * 128, (lambda tb, hh=hh: oT[:, hh, tb * 512:tb * 512 + 512]), R_o[hh])
                                            for hh in range(2)], s3)
                        fw.barrier()
            fw.barrier()

        def ffn(l):
            with contextlib.ExitStack() as scr0:
                rmsnorm_fm(l, "n2", scr0)
            fw.barrier()
            with contextlib.ExitStack() as scr:
                NH = 12
                actT = fw.sb("f_act", [128, NH, S], BF16, scr)
                sg = [fw.sb(f"f_sg{i}", [128, 512], F32, scr) for i in range(2)]
                R_sg = [fw.res() for _ in range(2)]
                n = 0
                for (f0, nf) in ((0, 12), (12, 10)):
                    R_a = [[fw.res() for _ in range(4)] for _ in range(nf)]
                    for g in range(nf // 2):
                        gg = f0 // 2 + g
                        wv, R_wv = load_w(wgu_d[l, :, gg * 512:gg * 512 + 512], 512)
                        for fi in range(2):
                            f = g * 2 + fi
                            for tb in range(4):
                                pg = (n % 2) * 2
                                pu = pg + 1
                                proj_fm(wv, R_wv, fi * 256, 128, tb, PB[pg], R_pb[pg])
                                proj_fm(wv, R_wv, fi * 256 + 128, 128, tb, PB[pu], R_pb[pu])
                                si = n % 2
                                fw.act(sg[si][:, :], PB[pg][:, :], AF.Silu, reads=[R_pb[pg]], writes=[R_sg[si]])
                                fw.tt("dve", actT[:, f, tb * 512:tb * 512 + 512], sg[si][:, :], PB[pu][:, :], ALU.mult,
                                      reads=[R_sg[si], R_pb[pu]], writes=[R_a[f][tb]])
                                n += 1
                    for oc in range(8):
                        wv, R_wv = load_w(wdn_d[l, f0 * 128:(f0 + nf) * 128, oc * 128:oc * 128 + 128], 128, kchunks=nf)
                        for tb in range(4):
                            pb = 4 + (oc * 4 + tb) % 4
                            ts_ = slice(tb * 512, tb * 512 + 512)
                            for f in range(nf):
                                fw.mm(PB[pb][:, :], wv[:, f, :], actT[:, f, ts_], start=(f == 0), stop=(f == nf - 1),
                                      reads=[R_wv, R_a[f][tb]], writes=[R_pb[pb]])
                            fw.tt("dve", xT[:, oc, ts_], xT[:, oc, ts_], PB[pb][:, :], ALU.add,
                                  reads=[R_x[oc][tb], R_pb[pb]], writes=[R_x[oc][tb]])
                    fw.barrier()
            fw.barrier()

        eps_t = fw.sb("eps_t", [128, 1], F32)
        R_eps = fw.res()
        fw.op("pool", lambda e: e.memset(eps_t[:, :], EPS), writes=[R_eps])
        eps_col = eps_t[:, 0:1]
        fw.barrier()

        for l in range(nl):
            with contextlib.ExitStack() as scr:
                rmsnorm_fm(l, "n1", scr)
            if l == 0:
                tap("hT", hT[:, :, :], R_h)
            fw.barrier()
            if "skipA" not in taps:
                mixer_A(l)
            if "skipC" not in taps:
                mixer_C(l)
            if "skipB" not in taps:
                mixer_B(l)
            if l == 0:
                tap("xmid", xT[:, :, :], [r for rr in R_x for r in rr])
            ffn(l)

        yv = yT_d.rearrange("(c p) t -> p c t", p=128)
        for c in range(8):
            fw.dma("sp", yv[:, c, :], xT[:, c, :], reads=R_x[c], writes=[R_out], sem=s_st)
        fw.wait_all("sp", [R_out, R_tap])
        print("ninst", fw.ninst)
    return nc


TAP_SHAPES = {
    "hT": ((128, 8, S), BF16), "a_qT": ((128, 2, S), BF16), "a_oT": ((128, 2, S), BF16), "c_qT": ((128, 2, S), BF16),
    "c_oT": ((128, 2, S), BF16), "xmid": ((128, 8, S), F32), "f_act": ((128, NFF, S), BF16),
    "b_g": ((128, 16, 8), F32), "b_beta": ((128, 16, 8), F32), "b_gc": ((128, 16, 8), F32),
    "b_q": ((128, 4, S), BF16), "b_k": ((128, 4, S), BF16), "b_v": ((128, 4, S), BF16), "b_o": ((128, 4, S), F32),
    "b_oT": ((128, 4, S), BF16),
}


_PROG_CACHE = {}


def _prep_inputs(inputs, nl):
    perm_in = _win_perm()
    perm_out = _wout_perm()
    perm_gu = _wgu_perm()
    w_in = np.ascontiguousarray(inputs["w_in"][:nl][:, :, perm_in])
    w_out = np.ascontiguousarray(inputs["w_out"][:nl][:, perm_out, :])
    w_gu = np.ascontiguousarray(inputs["w_gate_up"][:nl][:, :, perm_gu])
    w_dn = np.ascontiguousarray(inputs["w_down"][:nl])
    cf, cb16, ropeA, ropeC, _ = _build_consts(inputs, nl)
    shared = {"w_in": w_in, "w_out": w_out, "w_gu": w_gu, "w_dn": w_dn, "cf": cf, "cb": cb16,
              "ropeA": ropeA, "ropeC": ropeC}
    return shared


def kernel(**inputs):
    inputs = {k: np.asarray(v) for k, v in inputs.items()}
    nl = L_FULL
    shared = _prep_inputs(inputs, nl)
    x = inputs["x"]
    in_maps = []
    for b in range(8):
        m = dict(shared)
        m["xT"] = np.ascontiguousarray(x[b].T)
        in_maps.append(m)
    if nl not in _PROG_CACHE:
        _PROG_CACHE[nl] = build_program(nl)
    res = run_bass_kernel_spmd(_PROG_CACHE[nl], in_maps, core_ids=list(range(8)))
    out = np.stack([np.ascontiguousarray(r["yT"].T) for r in res.results], axis=0)
    return out.astype(np.float32)
```

```python
import contextlib
import numpy as np
import ml_dtypes
import concourse.bass as bass
import concourse.mybir as mybir
from concourse.bass_utils import run_bass_kernel_spmd

F32 = mybir.dt.float32
BF16 = mybir.dt.bfloat16
AF = mybir.ActivationFunctionType
ALU = mybir.AluOpType

L_FULL = 4
S = 2048
D = 1024
DFF = 2816
NFF = 22
IN_DIM = 3344
EPS = 1e-6
NEG = -30000.0


class Res:
    __slots__ = ("name", "w", "r", "psum")

    def __init__(self, name, psum=False):
        self.name = name
        self.w = None
        self.r = {}
        self.psum = psum


class FW:
    ENG = ("pe", "act", "dve", "pool", "sp")

    def __init__(self, nc, stack):
        self.nc = nc
        self.stack = stack
        self.e = {"pe": nc.tensor, "act": nc.scalar, "dve": nc.vector, "pool": nc.gpsimd, "sp": nc.sync}
        self.sems = {}
        self.cnt = {}
        self.waited = {k: {} for k in self.ENG}
        for k in self.ENG:
            self.sems[k] = stack.enter_context(nc.semaphore("s_" + k))
            self.cnt[k] = 0
        self.nres = 0
        self.ninst = {k: 0 for k in self.ENG}

    def sb(self, name, shape, dt, stack=None):
        self.nsb = getattr(self, "nsb", 0) + 1
        return (stack or self.stack).enter_context(self.nc.sbuf_tensor(f"{name}_{self.nsb}", list(shape), dt))

    def ps(self, name, shape, dt=F32):
        return self.stack.enter_context(self.nc.psum_tensor(name, list(shape), dt))

    def res(self, name=None, psum=False):
        self.nres += 1
        return Res(name or f"r{self.nres}", psum)

    def new_sem(self, name):
        s = self.stack.enter_context(self.nc.semaphore(name))
        self.sems[name] = s
        self.cnt[name] = 0
        return name

    @staticmethod
    def _flat(xs):
        out = []
        for x in xs:
            if isinstance(x, (list, tuple)):
                out.extend(FW._flat(x))
            else:
                out.append(x)
        return out

    def _deps(self, eng, reads, writes, waiter=None):
        waiter = waiter or eng
        deps = {}

        def need(sv):
            if sv is None:
                return
            s, v = sv
            if deps.get(s, 0) < v:
                deps[s] = v

        for r in reads:
            need(r.w)
            if r.psum:
                for s, v in r.r.items():
                    if s != eng:
                        need((s, v))
        for w in writes:
            need(w.w)
            for s, v in w.r.items():
                need((s, v))
        out = {}
        for s, v in deps.items():
            if s == eng:
                if eng == "pe":
                    continue
            if self.waited[waiter].get(s, 0) >= v:
                continue
            out[s] = v
        return out

    def _emit_waits(self, eng, deps):
        for s, v in deps.items():
            self.e[eng].wait_ge(self.sems[s], v)
            self.waited[eng][s] = v
            self.ninst[eng] += 1

    def op(self, eng, fn, reads=(), writes=()):
        reads, writes = self._flat(reads), self._flat(writes)
        deps = self._deps(eng, reads, writes)
        self._emit_waits(eng, deps)
        inst = fn(self.e[eng])
        self.cnt[eng] += 1
        self.ninst[eng] += 1
        v = self.cnt[eng]
        inst.then_inc(self.sems[eng], 1)
        for r in reads:
            if r.r.get(eng, 0) < v:
                r.r[eng] = v
        for w in writes:
            w.w = (eng, v)
            w.r = {}
        return inst

    def dma(self, q, out, in_, reads=(), writes=(), sem=None, **kw):
        reads, writes = self._flat(reads), self._flat(writes)
        d2 = self._deps(None, reads, writes, waiter=q)
        self._emit_waits(q, d2)
        inst = self.e[q].dma_start(out=out, in_=in_, **kw)
        self.cnt[sem] += 16
        v = self.cnt[sem]
        inst.then_inc(self.sems[sem], 16)
        self.ninst[q] += 1
        for r in reads:
            if r.r.get(sem, 0) < v:
                r.r[sem] = v
        for w in writes:
            w.w = (sem, v)
            w.r = {}
        return inst

    def wait_all(self, eng, resources):
        d2 = self._deps(None, self._flat(resources), (), waiter=eng)
        self._emit_waits(eng, d2)

    def barrier(self):
        snap = dict(self.cnt)
        for eng in self.ENG:
            d = {}
            for s, v in snap.items():
                if v > 0 and s != eng and self.waited[eng].get(s, 0) < v:
                    d[s] = v
            self._emit_waits(eng, d)

    def mm(self, out, lhsT, rhs, start=True, stop=True, reads=(), writes=(), skip=False):
        return self.op("pe", lambda e: e.matmul(out, lhsT, rhs, start=start, stop=stop, skip_group_check=skip), reads, writes)

    def tr(self, out, in_, ident, reads=(), writes=()):
        return self.op("pe", lambda e: e.transpose(out, in_, ident), reads, writes)

    def act(self, out, in_, func, reads=(), writes=(), **kw):
        return self.op("act", lambda e: e.activation(out=out, in_=in_, func=func, **kw), reads, writes)

    def tt(self, eng, out, in0, in1, op, reads=(), writes=()):
        return self.op(eng, lambda e: e.tensor_tensor(out=out, in0=in0, in1=in1, op=op), reads, writes)

    def stt(self, out, in0, scalar, in1, op0, op1, reads=(), writes=()):
        return self.op("dve", lambda e: e.scalar_tensor_tensor(out=out, in0=in0, scalar=scalar, in1=in1,
                                                                op0=op0, op1=op1), reads, writes)

    def ts(self, eng, out, in0, s1, s2, op0, op1=None, reads=(), writes=()):
        if op1 is None:
            return self.op(eng, lambda e: e.tensor_scalar(out=out, in0=in0, scalar1=s1, scalar2=None, op0=op0),
                           reads, writes)
        return self.op(eng, lambda e: e.tensor_scalar(out=out, in0=in0, scalar1=s1, scalar2=s2, op0=op0, op1=op1),
                       reads, writes)

    def cp(self, eng, out, in_, reads=(), writes=()):
        if eng == "act":
            return self.act(out, in_, AF.Copy, reads, writes)
        return self.op(eng, lambda e: e.tensor_copy(out=out, in_=in_), reads, writes)


O_QA, O_KA, O_VA = 0, 256, 512
O_QB, O_KB, O_VB, O_ZB = 768, 1280, 1792, 2304
O_AB, O_BB = 2816, 2824
O_QC, O_KC, O_VC = 2832, 3088, 3216

P_QA, P_KA, P_VA, P_AB = 0, 256, 512, 768
P_QC, P_KC, P_VC = 784, 1040, 1168
P_B = 1296


def _win_perm():
    idx = []
    idx += list(range(O_QA, O_QA + 256)) + list(range(O_KA, O_KA + 256)) + list(range(O_VA, O_VA + 256))
    idx += list(range(O_AB, O_AB + 16))
    for h in (0, 2, 1, 3):
        idx += list(range(O_QC + 64 * h, O_QC + 64 * h + 64))
    idx += list(range(O_KC, O_KC + 128)) + list(range(O_VC, O_VC + 128))
    for pp in range(2):
        for base in (O_QB, O_KB, O_VB, O_ZB):
            idx += list(range(base + 256 * pp, base + 256 * pp + 256))
    assert len(idx) == IN_DIM and len(set(idx)) == IN_DIM
    return np.array(idx)


def _wout_perm():
    idx = list(range(0, 768))
    for h in (0, 2, 1, 3):
        idx += list(range(768 + 64 * h, 768 + 64 * h + 64))
    return np.array(idx)


def _wgu_perm():
    idx = []
    for f in range(NFF):
        idx += list(range(f * 128, f * 128 + 128)) + list(range(DFF + f * 128, DFF + f * 128 + 128))
    return np.array(idx)


def _cf_layout(nl):
    off = {}
    c = 0

    def add(name, n):
        nonlocal c
        off[name] = c
        c += n

    for l in range(nl):
        add(f"n1_{l}", 8)
        add(f"n2_{l}", 8)
        add(f"qna_{l}", 1)
        add(f"kna_{l}", 1)
        add(f"qnc_{l}", 1)
        add(f"knc_{l}", 1)
        add(f"onb_{l}", 1)
        add(f"cw_{l}", 60)
        add(f"alog_{l}", 8)
        add(f"dtb_{l}", 8)
    add("ident", 128)
    add("ones", 128)
    add("uuf", 128)
    add("uub", 128)
    add("slf", 128)
    add("slb", 128)
    for d in range(2):
        for cc in range(2):
            add(f"sla_{d}{cc}", 128)
    off["_n"] = c
    return off


CB = {"ident": 0, "ones": 128, "blk64": 256, "ra": 384, "rc": 512, "strict": 640,
      "maskA": 768, "mask16": 1280, "gmask": 3328, "_n": 3840}


def _build_consts(inputs, nl):
    off = _cf_layout(nl)
    cf = np.zeros((128, off["_n"]), np.float32)
    p = np.arange(128)
    for l in range(nl):
        cf[:, off[f"n1_{l}"]:off[f"n1_{l}"] + 8] = inputs["norm1"][l].reshape(8, 128).T
        cf[:, off[f"n2_{l}"]:off[f"n2_{l}"] + 8] = inputs["norm2"][l].reshape(8, 128).T
        cf[:, off[f"qna_{l}"]] = inputs["qn_a"][l][p % 64]
        cf[:, off[f"kna_{l}"]] = inputs["kn_a"][l][p % 64]
        cf[:, off[f"qnc_{l}"]] = inputs["qn_c"][l][p % 64]
        cf[:, off[f"knc_{l}"]] = inputs["kn_c"][l][p % 64]
        cf[:, off[f"onb_{l}"]] = inputs["onorm_b"][l]
        cw = inputs["conv_b"][l]
        cf[:, off[f"cw_{l}"]:off[f"cw_{l}"] + 60] = cw.reshape(5, 12, 128).transpose(2, 1, 0).reshape(128, 60)
        cf[:, off[f"alog_{l}"]:off[f"alog_{l}"] + 8] = inputs["a_log_b"][l].reshape(1, 8)
        cf[:, off[f"dtb_{l}"]:off[f"dtb_{l}"] + 8] = inputs["dt_bias_b"][l].reshape(1, 8)
    j = p[:, None]
    i = p[None, :]
    same = (j // 64) == (i // 64)
    cf[:, off["ident"]:off["ident"] + 128] = (j == i)
    cf[:, off["ones"]:off["ones"] + 128] = 1.0
    cf[:, off["uuf"]:off["uuf"] + 128] = same & (j <= i)
    cf[:, off["uub"]:off["uub"] + 128] = same & (j >= i)
    cf[:, off["slf"]:off["slf"] + 128] = (j == 64 * (i // 64) + 63)
    cf[:, off["slb"]:off["slb"] + 128] = (j == 64 * (i // 64))
    for d in range(2):
        for cc in range(2):
            last = 64 * cc + (63 if d == 0 else 0)
            cf[:, off[f"sla_{d}{cc}"]:off[f"sla_{d}{cc}"] + 128] = (j == last) * np.ones((1, 128))

    cb = np.zeros((128, CB["_n"]), np.float32)
    cb[:, CB["ident"]:CB["ident"] + 128] = (j == i)
    cb[:, CB["ones"]:CB["ones"] + 128] = 1.0
    cb[:, CB["blk64"]:CB["blk64"] + 128] = same
    def rot_mat(pairs_fn):
        R = np.zeros((128, 128), np.float32)
        for m in range(128):
            hd = m % 64
            base = m - hd
            k, sgn = pairs_fn(hd)
            if k is not None:
                R[base + k, m] = sgn
        return R

    def pa(hd):
        if hd < 8:
            return hd + 8, -1.0
        if hd < 16:
            return hd - 8, 1.0
        return None, 0.0

    def pc(hd):
        blk = hd // 32
        r = hd % 32
        if r < 16:
            return blk * 32 + r + 16, -1.0
        return blk * 32 + r - 16, 1.0

    cb[:, CB["ra"]:CB["ra"] + 128] = rot_mat(pa)
    cb[:, CB["rc"]:CB["rc"] + 128] = rot_mat(pc)
    cb[:, CB["strict"]:CB["strict"] + 128] = (j != i)
    pk = p[:, None]
    f128 = np.arange(128)[None, :]
    f64 = np.arange(64)[None, :]
    mfull = np.where(np.abs(f128 - pk) <= 64, 0.0, NEG)
    mprev = np.where(pk >= f64 + 64, 0.0, NEG)
    mnext = np.where(pk <= f64, 0.0, NEG)
    one = np.concatenate([mfull, mprev, mnext], axis=1)
    cb[:, CB["maskA"]:CB["maskA"] + 512] = np.concatenate([one, one], axis=1)
    for B in range(4):
        blk = mfull[:, 32 * B:32 * B + 32]
        cb[:, CB["mask16"] + 512 * B:CB["mask16"] + 512 * B + 512] = np.tile(blk, (1, 16))
    gf = np.where(same & (i >= j), 0.0, NEG)
    gb = np.where(same & (i <= j), 0.0, NEG)
    cb[:, CB["gmask"]:CB["gmask"] + 512] = np.concatenate([gf, gf, gb, gb], axis=1)
    cb16 = cb.astype(ml_dtypes.bfloat16)

    t = np.arange(S, dtype=np.float32)
    ropeA = np.zeros((128, 2, S), np.float32)
    ropeC = np.zeros((128, 2, S), np.float32)
    invA = np.float32(500000.0) ** (-np.arange(8, dtype=np.float32) / 8)
    invC = np.float32(10000.0) ** (-np.arange(16, dtype=np.float32) / 16)
    rowp = (np.arange(S) // 64).astype(np.float32)
    colp = (np.arange(S) % 64).astype(np.float32)
    for m in range(128):
        hd = m % 64
        if hd < 16:
            ang = t * invA[hd % 8]
            ropeA[m, 0] = np.cos(ang)
            ropeA[m, 1] = np.sin(ang)
        else:
            ropeA[m, 0] = 1.0
        pos = rowp if hd < 32 else colp
        ang = pos * invC[(hd % 32) % 16]
        ropeC[m, 0] = np.cos(ang)
        ropeC[m, 1] = np.sin(ang)
    return cf, cb16, ropeA.astype(ml_dtypes.bfloat16), ropeC.astype(ml_dtypes.bfloat16), off


def build_program(nl, taps=()):
    taps = set(taps)
    nc = bass.Bass("TRN2", target_bir_lowering=False)
    off = _cf_layout(nl)
    xT_d = nc.dram_tensor("xT", [D, S], F32, kind="ExternalInput").ap()
    win_d = nc.dram_tensor("w_in", [nl, D, IN_DIM], F32, kind="ExternalInput").ap()
    wout_d = nc.dram_tensor("w_out", [nl, D, D], F32, kind="ExternalInput").ap()
    wgu_d = nc.dram_tensor("w_gu", [nl, D, 2 * DFF], F32, kind="ExternalInput").ap()
    wdn_d = nc.dram_tensor("w_dn", [nl, DFF, D], F32, kind="ExternalInput").ap()
    cf_d = nc.dram_tensor("cf", [128, off["_n"]], F32, kind="ExternalInput").ap()
    cb_d = nc.dram_tensor("cb", [128, CB["_n"]], BF16, kind="ExternalInput").ap()
    ropeA_d = nc.dram_tensor("ropeA", [128, 2, S], BF16, kind="ExternalInput").ap()
    ropeC_d = nc.dram_tensor("ropeC", [128, 2, S], BF16, kind="ExternalInput").ap()
    yT_d = nc.dram_tensor("yT", [D, S], F32, kind="ExternalOutput").ap()
    tap_d = {}
    for name, (shape, tdt) in TAP_SHAPES.items():
        if name in taps:
            tap_d[name] = nc.dram_tensor("tap_" + name, list(shape), tdt, kind="ExternalOutput").ap()

    with contextlib.ExitStack() as st:
        fw = FW(nc, st)
        xT = fw.sb("xT_s", [128, 8, S], F32)
        hT = fw.sb("hT_s", [128, 8, S], BF16)
        cf = fw.sb("cf_s", [128, off["_n"]], F32)
        cb = fw.sb("cb_s", [128, CB["_n"]], BF16)
        wbuf = [fw.sb(f"wbuf{i}", [128, 4096], BF16) for i in range(2)]
        R_x = [[fw.res(f"x{c}_{tb}") for tb in range(4)] for c in range(8)]
        R_h = [fw.res(f"h{tb}") for tb in range(4)]
        R_cf = fw.res("cf")
        R_cb = fw.res("cb")
        R_w = [fw.res("w0"), fw.res("w1")]
        R_out = fw.res("out")
        R_tap = fw.res("tap")
        PB = [fw.ps(f"pb{i}", [128, 512], F32) for i in range(8)]
        R_pb = [fw.res(f"pb{i}", psum=True) for i in range(8)]
        s_ld = fw.new_sem("ld")
        s_w = [fw.new_sem("w0"), fw.new_sem("w1")]
        s_st = fw.new_sem("st")
        s_tap = fw.new_sem("tap")
        wstate = {"i": 0}

        def cfc(name, n=1, o=0):
            return cf[:, off[name] + o: off[name] + o + n]

        def cbm(name, n=128, o=0):
            return cb[:, CB[name] + o: CB[name] + o + n]

        ident_b = cbm("ident")
        ones_b = cbm("ones")
        blk64_b = cbm("blk64")
        ident_f = cfc("ident", 128)
        ones_f = cfc("ones", 128)

        fw.dma("sp", cf[:, :], cf_d[:, :], writes=[R_cf], sem=s_ld)
        fw.dma("sp", cb[:, :], cb_d[:, :], writes=[R_cb], sem=s_ld)
        xv = xT_d.rearrange("(c p) t -> p c t", p=128)
        for c in range(8):
            fw.dma("sp", xT[:, c, :], xv[:, c, :], writes=R_x[c], sem=s_ld)

        def load_w(dram_ap, ncols, kchunks=8):
            i = wstate["i"]
            wstate["i"] = 1 - i
            view = wbuf[i][:, 0:kchunks * ncols].rearrange("p (k n) -> p k n", k=kchunks)
            fw.dma("pool", view, dram_ap.rearrange("(k p) n -> p k n", p=128), writes=[R_w[i]], sem=s_w[i])
            return view, R_w[i]

        def tap(name, src_ap, reads, dst=None):
            if name not in tap_d:
                return
            fw.dma("sp", dst if dst is not None else tap_d[name], src_ap, reads=reads, writes=[R_tap], sem=s_tap)

        def rmsnorm_fm(l, which, scr):
            sq = [fw.sb(f"nsq{i}", [128, 512], BF16, scr) for i in range(2)]
            R_sq = [fw.res() for _ in range(2)]
            lnv = [fw.sb(f"nln{i}", [128, 512], F32, scr) for i in range(2)]
            rstd = [fw.sb(f"nrs{i}", [128, 512], F32, scr) for i in range(2)]
            R_ln = [fw.res() for _ in range(2)]
            R_rs = [fw.res() for _ in range(2)]
            for tb in range(4):
                ts_ = slice(tb * 512, tb * 512 + 512)
                pb = tb % 2
                for c in range(8):
                    i = c % 2
                    fw.act(sq[i][:, :], xT[:, c, ts_], AF.Square, reads=[R_x[c][tb]], writes=[R_sq[i]])
                    fw.mm(PB[pb][:, :], ones_b, sq[i][:, :], start=(c == 0), stop=(c == 7),
                          reads=[R_sq[i], R_cb], writes=[R_pb[pb]])
                fw.act(lnv[pb][:, :], PB[pb][:, :], AF.Ln, reads=[R_pb[pb]], writes=[R_ln[pb]],
                       scale=1.0 / D, bias=eps_col)
                fw.act(rstd[pb][:, :], lnv[pb][:, :], AF.Exp, reads=[R_ln[pb]], writes=[R_rs[pb]], scale=-0.5)
                for c in range(8):
                    fw.stt(hT[:, c, ts_], xT[:, c, ts_], cfc(f"{which}_{l}", 1, c), rstd[pb][:, :],
                           ALU.mult, ALU.mult, reads=[R_x[c][tb], R_rs[pb], R_cf], writes=[R_h[tb]])

        def qk_post(pbank, R_pbank, wcol, rot_b, cos_ap, sin_ap, R_tab, out_ap, R_out_, tmp, scale_q=None):
            (qw, R_qw, sq, R_sq2, lnv, R_ln2, rs, R_rs2, t1, R_t1, t2, R_t2, pss, R_pss, psr, R_psr) = tmp
            fw.act(qw[:, :], pbank[:, :], AF.Copy, reads=[R_pbank, R_cf], writes=[R_qw], scale=wcol)
            fw.act(sq[:, :], pbank[:, :], AF.Square, reads=[R_pbank, R_cf], writes=[R_sq2], scale=wcol)
            fw.mm(pss[:, :], blk64_b, sq[:, :], reads=[R_sq2, R_cb], writes=[R_pss])
            fw.mm(psr[:, :], rot_b, qw[:, :], reads=[R_qw, R_cb], writes=[R_psr])
            fw.act(lnv[:, :], pss[:, :], AF.Ln, reads=[R_pss], writes=[R_ln2], scale=1.0 / 64, bias=eps_col)
            fw.act(rs[:, :], lnv[:, :], AF.Exp, reads=[R_ln2], writes=[R_rs2], scale=-0.5)
            fw.tt("dve", t1[:, :], qw[:, :], cos_ap, ALU.mult, reads=[R_qw, R_tab], writes=[R_t1])
            fw.tt("dve", t2[:, :], psr[:, :], sin_ap, ALU.mult, reads=[R_psr, R_tab], writes=[R_t2])
            fw.tt("dve", t1[:, :], t1[:, :], t2[:, :], ALU.add, reads=[R_t1, R_t2], writes=[R_t1])
            fw.tt("dve", out_ap, t1[:, :], rs[:, :], ALU.mult, reads=[R_t1, R_rs2], writes=[R_out_])

        def qk_tmp(scr, tag, pss_i, psr_i):
            return (fw.sb(f"qw{tag}", [128, 512], BF16, scr), fw.res(),
                    fw.sb(f"qsq{tag}", [128, 512], BF16, scr), fw.res(),
                    fw.sb(f"qln{tag}", [128, 512], F32, scr), fw.res(),
                    fw.sb(f"qrs{tag}", [128, 512], F32, scr), fw.res(),
                    fw.sb(f"qt1{tag}", [128, 512], F32, scr), fw.res(),
                    fw.sb(f"qt2{tag}", [128, 512], F32, scr), fw.res(),
                    PB[pss_i], R_pb[pss_i], PB[psr_i], R_pb[psr_i])

        def proj_fm(wv, R_wv, col0, m, tb, pbank, R_pbank):
            ts_ = slice(tb * 512, tb * 512 + 512)
            for k in range(8):
                fw.mm(pbank[0:m, :], wv[:, k, col0:col0 + m], hT[:, k, ts_], start=(k == 0), stop=(k == 7),
                      reads=[R_wv, R_h[tb]], writes=[R_pbank])

        def wout_update(l, o_tiles, scr):
            for half in range(2):
                views = []
                for (row0, apf, R_o) in o_tiles:
                    wv, R_wv = load_w(wout_d[l, row0:row0 + 128, half * 512:half * 512 + 512], 512, kchunks=1)
                    views.append((wv, R_wv, apf, R_o))
                for oc4 in range(4):
                    oc = half * 4 + oc4
                    for tb in range(4):
                        pb = (oc4 * 4 + tb) % 4
                        n = len(views)
                        for i, (wv, R_wv, apf, R_o) in enumerate(views):
                            fw.mm(PB[pb][:, :], wv[:, 0, oc4 * 128:oc4 * 128 + 128], apf(tb), start=(i == 0),
                                  stop=(i == n - 1), reads=[R_wv, R_o], writes=[R_pb[pb]])
                        ts_ = slice(tb * 512, tb * 512 + 512)
                        fw.tt("dve", xT[:, oc, ts_], xT[:, oc, ts_], PB[pb][:, :], ALU.add,
                              reads=[R_x[oc][tb], R_pb[pb]], writes=[R_x[oc][tb]])

        def run_attn_jobs(jobs, PT, R_pt, sbanks=(2, 3, 4, 5), LA=2):
            n = len(jobs)
            for step in range(n + LA):
                if step < n:
                    j = jobs[step]
                    sb_i = sbanks[step % len(sbanks)]
                    pt_i = step % len(PT)
                    Sb, R_S = PB[sb_i], R_pb[sb_i]
                    nq = len(j["qk"])
                    for bi, (scol, nn, k_ap, q_ap) in enumerate(j["qk"]):
                        fw.mm(Sb[:, scol:scol + nn], k_ap, q_ap, start=(bi == 0), stop=(j["mask"] is None and bi == nq - 1),
                              reads=j["reads"], writes=[R_S])
                    rng = []
                    for (scol, nn, _, _) in sorted(j["qk"], key=lambda t: t[0]):
                        if rng and rng[-1][1] == scol:
                            rng[-1][1] = scol + nn
                        else:
                            rng.append([scol, scol + nn])
                    if j["mask"] is not None:
                        mname, moff = j["mask"]
                        for ri, (lo, hi) in enumerate(rng):
                            fw.mm(Sb[:, lo:hi], ident_b, cbm(mname, hi - lo, moff + lo), start=False, stop=(ri == len(rng) - 1),
                                  reads=[R_cb], writes=[R_S])
                    for (lo, hi) in rng:
                        fw.act(PT[pt_i][:, lo:hi], Sb[:, lo:hi], AF.Exp, reads=[R_S], writes=[R_pt[pt_i]], scale=0.125)
                if step >= LA:
                    j = jobs[step - LA]
                    pt_i = (step - LA) % len(PT)
                    for (acc_ap, v_ap, scol, nn, st_, sp_) in j["pv"]:
                        fw.mm(acc_ap, v_ap, PT[pt_i][:, scol:scol + nn], start=st_, stop=sp_,
                              reads=[R_pt[pt_i], j["R_v"]], writes=[j["R_acc"]])
                    if j["fin"] is not None:
                        j["fin"]()

        def attn_finalize(acc, R_acc, hh, out_ap, R_o, tmp):
            lnd, R_lnd, rd, R_rd = tmp
            nr = slice(hh * 64, hh * 64 + 64)
            dr = slice((1 - hh) * 64, (1 - hh) * 64 + 64)
            fw.act(lnd[nr, :], acc[dr, :], AF.Ln, reads=[R_acc], writes=[R_lnd])
            fw.act(rd[nr, :], lnd[nr, :], AF.Exp, reads=[R_lnd], writes=[R_rd], scale=-1.0)
            fw.tt("dve", out_ap, acc[nr, :], rd[nr, :], ALU.mult, reads=[R_acc, R_rd], writes=[R_o])

        def mixer_A(l):
            with contextlib.ExitStack() as scr:
                qT = fw.sb("a_qT", [128, 2, S], BF16, scr)
                kT = fw.sb("a_kT", [128, 2, S], BF16, scr)
                R_q = [fw.res() for _ in range(2)]
                R_k = [fw.res() for _ in range(2)]
                kTz = fw.sb("a_kTz", [128, 2, S], BF16, scr)
                R_kz = fw.res()
                tab = fw.sb("a_tab", [128, 2, S], BF16, scr)
                R_tab = fw.res()
                fw.dma("sp", tab[:, :, :], ropeA_d[:, :, :], writes=[R_tab], sem=s_ld)
                vx = fw.sb("a_vx", [128, 3, 16, 2, 128], BF16, scr)
                R_vx = fw.res()
                oT = fw.sb("a_oT", [128, 2, S], BF16, scr)
                R_o = [fw.res() for _ in range(2)]
                PT = [fw.sb(f"a_pt{i}", [128, 512], BF16, scr) for i in range(3)]
                R_pt = [fw.res() for _ in range(3)]
                t0_ = qk_tmp(scr, "a0", 2, 4)
                t1_ = (fw.sb("qwa1", [128, 512], BF16, scr), fw.res(), fw.sb("qsqa1", [128, 512], BF16, scr), fw.res(),
                       t0_[4], t0_[5], fw.sb("qrsa1", [128, 512], F32, scr), fw.res(), t0_[8], t0_[9], t0_[10], t0_[11],
                       PB[3], R_pb[3], PB[5], R_pb[5])
                tmps = [t0_, t1_]
                ftmp = (t0_[4], t0_[5], t0_[6], t0_[7])
                wv, R_wv = load_w(win_d[l, :, P_QA:P_QA + 512], 512)
                wv2, R_wv2 = load_w(win_d[l, :, P_VA:P_VA + 272], 272)
                n = 0
                for ci in range(4):
                    isq = ci < 2
                    dst, Rd = (qT, R_q) if isq else (kT, R_k)
                    c = ci % 2
                    for tb in range(4):
                        ts_ = slice(tb * 512, tb * 512 + 512)
                        pb = n % 2
                        proj_fm(wv, R_wv, ci * 128, 128, tb, PB[pb], R_pb[pb])
                        qk_post(PB[pb], R_pb[pb], cfc(f"qna_{l}" if isq else f"kna_{l}"), cbm("ra"),
                                tab[:, 0, ts_], tab[:, 1, ts_], R_tab, dst[:, c, ts_], Rd[c], tmps[n % 2])
                        n += 1
                tap("a_qT", qT[:, :, :], R_q)
                for c in range(2):
                    for tb in range(4):
                        pb = n % 2
                        n += 1
                        proj_fm(wv2, R_wv2, c * 128, 128, tb, PB[pb], R_pb[pb])
                        fw.cp("act", oT[:, c, tb * 512:tb * 512 + 512], PB[pb][:, :], reads=[R_pb[pb]], writes=[R_o[c]])
                for hp in range(2):
                    fw.op("pool", lambda e: e.memset(vx[:, :, :, 0, 64:128], 1.0), reads=[], writes=[R_vx])
                    fw.op("pool", lambda e: e.memset(vx[:, :, :, 1, 0:64], 1.0), reads=[], writes=[R_vx])
                    n = 0
                    for pat, dil in enumerate((1, 4, 16)):
                        for tile in range(16):
                            if dil == 1:
                                t0, st_ = tile * 128, 1
                            elif dil == 4:
                                r, m = tile // 4, tile % 4
                                t0, st_ = r + 512 * m, 4
                            else:
                                t0, st_ = tile, 16
                            pb = 6 + (n // 4) % 2
                            sub = n % 4
                            pbv = PB[pb][:, :].bitcast(BF16)
                            fw.tr(pbv[:, sub * 128:sub * 128 + 128], oT[:, hp, t0:t0 + 127 * st_ + 1:st_], ident_b,
                                  reads=[R_o[hp], R_cb], writes=[R_pb[pb]])
                            if sub == 3:
                                tl = tile - 3
                                src = pbv[:, 0:512].rearrange("p (t h d) -> p t h d", t=4, h=2)
                                fw.cp("dve", vx[:, pat, tl:tl + 4, 0, 0:64], src[:, :, 0, :],
                                      reads=[R_pb[pb]], writes=[R_vx])
                                fw.cp("act", vx[:, pat, tl:tl + 4, 1, 64:128], src[:, :, 1, :],
                                      reads=[R_pb[pb]], writes=[R_vx])
                            n += 1
                    fw.op("pool", lambda e: e.memset(kTz[64:128, 0, :], 0.0), writes=[R_kz])
                    fw.op("pool", lambda e: e.memset(kTz[0:64, 1, :], 0.0), writes=[R_kz])
                    fw.cp("pool", kTz[0:64, 0, :], kT[0:64, hp, :], reads=[R_k[hp]], writes=[R_kz])
                    fw.cp("pool", kTz[64:128, 1, :], kT[64:128, hp, :], reads=[R_k[hp]], writes=[R_kz])
                    jobs = []
                    for hh in range(2):
                        pr = slice(hh * 64, hh * 64 + 64)
                        c = hp
                        for B in range(4):
                            acc_i = (hh * 4 + B) % 2
                            acc, R_acc = PB[acc_i], R_pb[acc_i]
                            group = []

                            def add_job(blocks, mask_ap, group=group, acc=acc, R_acc=R_acc, c=c, hh=hh):
                                j = {"qk": [(scol, nn, k_ap, q_ap) for (scol, nn, q_ap, k_ap, pat, tile, acc_ap) in blocks],
                                     "mask": mask_ap, "ncols": 512, "reads": [R_q[c], R_kz],
                                     "pv": [[acc_ap, vx[:, pat, tile, hh, :], scol, nn, False, False]
                                            for (scol, nn, q_ap, k_ap, pat, tile, acc_ap) in blocks],
                                     "R_acc": R_acc, "R_v": R_vx, "fin": None}
                                group.append(j)

                            for half in range(2):
                                blocks = []
                                for qi in range(2):
                                    ml = half * 2 + qi
                                    m = 4 * B + ml
                                    so = qi * 256
                                    blocks.append((so, 128, qT[:, c, m * 128:m * 128 + 128],
                                                   kTz[:, hh, m * 128:m * 128 + 128], 0, m, acc[:, ml * 128:ml * 128 + 128]))
                                    if m > 0:
                                        blocks.append((so + 128, 64, qT[:, c, m * 128:m * 128 + 64],
                                                       kTz[:, hh, (m - 1) * 128:m * 128], 0, m - 1,
                                                       acc[:, ml * 128:ml * 128 + 64]))
                                    if m < 15:
                                        blocks.append((so + 192, 64, qT[:, c, m * 128 + 64:m * 128 + 128],
                                                       kTz[:, hh, (m + 1) * 128:(m + 2) * 128], 0, m + 1,
                                                       acc[:, ml * 128 + 64:ml * 128 + 128]))
                                add_job(blocks, ("maskA", 0))
                            for half in range(2):
                                blocks = []
                                for qi in range(2):
                                    r = half * 2 + qi
                                    so = qi * 256
                                    b0 = 512 * B + r

                                    def kt(mm_, r=r, hh=hh):
                                        return kTz[:, hh, 512 * mm_ + r:512 * mm_ + 512:4]
                                    blocks.append((so, 128, qT[:, c, b0:512 * B + 512:4], kt(B), 1, r * 4 + B,
                                                   acc[:, r:512:4]))
                                    if B > 0:
                                        blocks.append((so + 128, 64, qT[:, c, b0:512 * B + 256:4], kt(B - 1), 1,
                                                       r * 4 + B - 1, acc[:, r:256:4]))
                                    if B < 3:
                                        blocks.append((so + 192, 64, qT[:, c, b0 + 256:512 * B + 512:4], kt(B + 1), 1,
                                                       r * 4 + B + 1, acc[:, 256 + r:512:4]))
                                add_job(blocks, ("maskA", 0))
                            blocks = []
                            for b in range(16):
                                blocks.append((b * 32, 32, qT[:, c, 512 * B + b:512 * B + 512:16],
                                               kTz[:, hh, b:S:16], 2, b, acc[:, b:512:16]))
                            add_job(blocks, ("mask16", 512 * B))
                            group[0]["pv"][0][4] = True
                            group[-1]["pv"][-1][5] = True
                            group[-1]["fin"] = (lambda acc=acc, R_acc=R_acc, hh=hh, pr=pr, c=c, B=B:
                                                attn_finalize(acc, R_acc, hh, oT[pr, c, B * 512:B * 512 + 512], R_o[c], ftmp))
                            jobs += group
                    run_attn_jobs(jobs, PT, R_pt)
                tap("a_oT", oT[:, :, :], R_o)
                wout_update(l, [(c * 128, (lambda tb, c=c: oT[:, c, tb * 512:tb * 512 + 512]), R_o[c]) for c in range(2)], scr)
            fw.barrier()

        def mixer_C(l):
            with contextlib.ExitStack() as scr:
                qT = fw.sb("c_qT", [128, 2, S], BF16, scr)
                kT = fw.sb("c_kT", [128, S], BF16, scr)
                R_q = [fw.res() for _ in range(2)]
                R_k = fw.res()
                kTz = fw.sb("c_kTz", [128, 2, S], BF16, scr)
                R_kz = fw.res()
                tab = fw.sb("c_tab", [128, 2, S], BF16, scr)
                R_tab = fw.res()
                fw.dma("sp", tab[:, :, :], ropeC_d[:, :, :], writes=[R_tab], sem=s_ld)
                vx = fw.sb("c_vx", [128, 16, 2, 128], BF16, scr)
                R_vx = fw.res()
                oT = fw.sb("c_oT", [128, 2, S], BF16, scr)
                R_o = [fw.res() for _ in range(2)]
                PT = [fw.sb(f"c_pt{i}", [128, 512], BF16, scr) for i in range(3)]
                R_pt = [fw.res() for _ in range(3)]
                lnd = fw.sb("c_lnd", [128, 512], F32, scr)
                rd = fw.sb("c_rd", [128, 512], F32, scr)
                ftmp = (lnd, fw.res(), rd, fw.res())
                tmps = [qk_tmp(scr, "c0", 2, 4), qk_tmp(scr, "c1", 3, 5)]
                wv, R_wv = load_w(win_d[l, :, P_QC:P_QC + 512], 512)
                n = 0
                for ci in range(3):
                    for tb in range(4):
                        ts_ = slice(tb * 512, tb * 512 + 512)
                        pb = n % 2
                        proj_fm(wv, R_wv, ci * 128, 128, tb, PB[pb], R_pb[pb])
                        if ci < 2:
                            dst, Rd, wn = qT[:, ci, ts_], R_q[ci], f"qnc_{l}"
                        else:
                            dst, Rd, wn = kT[:, ts_], R_k, f"knc_{l}"
                        qk_post(PB[pb], R_pb[pb], cfc(wn), cbm("rc"), tab[:, 0, ts_], tab[:, 1, ts_], R_tab,
                                dst, Rd, tmps[n % 2])
                        n += 1
                tap("c_qT", qT[:, :, :], R_q)
                fw.op("pool", lambda e: e.memset(kTz[64:128, 0, :], 0.0), writes=[R_kz])
                fw.op("pool", lambda e: e.memset(kTz[0:64, 1, :], 0.0), writes=[R_kz])
                fw.cp("pool", kTz[0:64, 0, :], kT[0:64, :], reads=[R_k], writes=[R_kz])
                fw.cp("pool", kTz[64:128, 1, :], kT[64:128, :], reads=[R_k], writes=[R_kz])
                fw.op("pool", lambda e: e.memset(vx[:, :, 0, 64:128], 1.0), reads=[], writes=[R_vx])
                fw.op("pool", lambda e: e.memset(vx[:, :, 1, 0:64], 1.0), reads=[], writes=[R_vx])
                for tile in range(16):
                    pb = 6 + (tile // 4) % 2
                    sub = tile % 4
                    for k in range(8):
                        fw.mm(PB[pb][:, sub * 128:sub * 128 + 128], hT[:, k, tile * 128:tile * 128 + 128],
                              wv[:, k, 384:512], start=(k == 0), stop=(k == 7), reads=[R_wv] + R_h, writes=[R_pb[pb]])
                    if sub == 3:
                        tl = tile - 3
                        src = PB[pb][:, :].rearrange("p (t h d) -> p t h d", t=4, h=2)
                        fw.cp("dve", vx[:, tl:tl + 4, 0, 0:64], src[:, :, 0, :], reads=[R_pb[pb]], writes=[R_vx])
                        fw.cp("dve", vx[:, tl:tl + 4, 1, 64:128], src[:, :, 1, :], reads=[R_pb[pb]], writes=[R_vx])
                jobs = []
                for c in range(2):
                    for hh in range(2):
                        pr = slice(hh * 64, hh * 64 + 64)
                        for B in range(4):
                            acc_i = (c * 8 + hh * 4 + B) % 2
                            acc, R_acc = PB[acc_i], R_pb[acc_i]
                            for kt in range(16):
                                j = {"qk": [(0, 512, kTz[:, hh, kt * 128:kt * 128 + 128], qT[:, c, B * 512:B * 512 + 512])],
                                     "mask": None, "ncols": 512, "reads": [R_q[c], R_kz],
                                     "pv": [[acc[:, :], vx[:, kt, hh, :], 0, 512, kt == 0, kt == 15]],
                                     "R_acc": R_acc, "R_v": R_vx, "fin": None}
                                if kt == 15:
                                    j["fin"] = (lambda acc=acc, R_acc=R_acc, hh=hh, pr=pr, c=c, B=B:
                                                attn_finalize(acc, R_acc, hh, oT[pr, c, B * 512:B * 512 + 512], R_o[c], ftmp))
                                jobs.append(j)
                run_attn_jobs(jobs, PT, R_pt)
                tap("c_oT", oT[:, :, :], R_o)
                wout_update(l, [(768 + c * 128, (lambda tb, c=c: oT[:, c, tb * 512:tb * 512 + 512]), R_o[c])
                                for c in range(2)], scr)
            fw.barrier()


        def mixer_B(l):
            with contextlib.ExitStack() as scr:
                def tk(name, n=8):
                    return fw.sb("b_" + name, [128, 16, n], F32, scr)
                ab_tok = tk("ab", 16)
                beta = tk("beta"); nbeta = tk("nbeta"); g_tok = tk("g"); gc = tk("gc"); ngc = tk("ngc")
                egc = tk("egc"); kdec = tk("kdec")
                gam = fw.sb("b_gam", [128, 16, 2, 8], F32, scr)
                ea = fw.sb("b_ea", [128, 8], F32, scr)
                R_ts = fw.res()
                wv, R_wv = load_w(win_d[l, :, P_AB:P_AB + 16], 16)
                for tile in range(16):
                    for k in range(8):
                        fw.mm(PB[0][:, tile * 16:tile * 16 + 16], hT[:, k, tile * 128:tile * 128 + 128], wv[:, k, 0:16],
                              start=(k == 0), stop=(k == 7), reads=[R_wv] + R_h, writes=[R_pb[0]])
                fw.cp("dve", ab_tok[:, :, :], PB[0][:, 0:256].rearrange("p (t n) -> p t n", t=16),
                      reads=[R_pb[0]], writes=[R_ts])
                fw.act(beta[:, :, :], ab_tok[:, :, 8:16], AF.Tanh, reads=[R_ts], writes=[R_ts], scale=0.5)
                fw.ts("dve", beta[:, :, :], beta[:, :, :], 0.5, 0.5, ALU.mult, ALU.add, reads=[R_ts], writes=[R_ts])
                fw.ts("dve", nbeta[:, :, :], beta[:, :, :], -1.0, None, ALU.mult, reads=[R_ts], writes=[R_ts])
                dtb_b = cfc(f"dtb_{l}", 8).unsqueeze(1).broadcast_to([128, 16, 8])
                fw.tt("dve", g_tok[:, :, :], ab_tok[:, :, 0:8], dtb_b, ALU.add, reads=[R_ts, R_cf], writes=[R_ts])
                fw.act(g_tok[:, :, :], g_tok[:, :, :], AF.Exp, reads=[R_ts], writes=[R_ts])
                fw.act(g_tok[:, :, :], g_tok[:, :, :], AF.Ln, reads=[R_ts], writes=[R_ts], bias=cfc("ones", 1))
                fw.act(ea[:, :], cfc(f"alog_{l}", 8), AF.Exp, reads=[R_cf], writes=[R_ts])
                fw.stt(g_tok[:, :, :], g_tok[:, :, :], -1.0, ea[:, :].unsqueeze(1).broadcast_to([128, 16, 8]),
                       ALU.mult, ALU.mult, reads=[R_ts], writes=[R_ts])
                tap("b_g", g_tok[:, :, :], [R_ts])
                tap("b_beta", beta[:, :, :], [R_ts])
                for tile in range(16):
                    for d in range(2):
                        fw.mm(PB[1][:, tile * 8 + d * 4:tile * 8 + d * 4 + 4], cfc("uuf" if d == 0 else "uub", 128),
                              g_tok[:, tile, d * 4:d * 4 + 4], reads=[R_ts, R_cf], writes=[R_pb[1]])
                fw.cp("dve", gc[:, :, :], PB[1][:, 0:128].rearrange("p (t n) -> p t n", t=16), reads=[R_pb[1]], writes=[R_ts])
                fw.ts("dve", ngc[:, :, :], gc[:, :, :], -1.0, None, ALU.mult, reads=[R_ts], writes=[R_ts])
                fw.act(egc[:, :, :], gc[:, :, :], AF.Exp, reads=[R_ts], writes=[R_ts])
                for tile in range(16):
                    for d in range(2):
                        fw.mm(PB[2][:, tile * 8 + d * 4:tile * 8 + d * 4 + 4], cfc("slf" if d == 0 else "slb", 128),
                              gc[:, tile, d * 4:d * 4 + 4], reads=[R_ts, R_cf], writes=[R_pb[2]])
                        for cc in range(2):
                            o_ = (tile * 2 + cc) * 8 + d * 4
                            fw.mm(PB[3][:, o_:o_ + 4], cfc(f"sla_{d}{cc}", 128), gc[:, tile, d * 4:d * 4 + 4],
                                  reads=[R_ts, R_cf], writes=[R_pb[3]])
                fw.tt("dve", kdec[:, :, :], PB[2][:, 0:128].rearrange("p (t n) -> p t n", t=16), gc[:, :, :], ALU.subtract,
                      reads=[R_pb[2], R_ts], writes=[R_ts])
                fw.act(kdec[:, :, :], kdec[:, :, :], AF.Exp, reads=[R_ts], writes=[R_ts])
                fw.act(gam[:, :, :, :], PB[3][:, 0:256].rearrange("p (t c n) -> p t c n", t=16, c=2), AF.Exp,
                       reads=[R_pb[3]], writes=[R_ts])
                tap("b_gc", gc[:, :, :], [R_ts])

                for pp in range(2):
                    with contextlib.ExitStack() as sp_:
                        bq = fw.sb("b_q", [128, 2, S], BF16, sp_)
                        bk = fw.sb("b_k", [128, 2, S], BF16, sp_)
                        K_tok = fw.sb("b_Kt", [128, 16, 2, 128], BF16, sp_)
                        V_tok = fw.sb("b_Vt", [128, 16, 2, 128], BF16, sp_)
                        R_bq = fw.res(); R_bk = fw.res(); R_Kt = fw.res(); R_Vt = fw.res()
                        with contextlib.ExitStack() as s1:
                            bv = fw.sb("b_v", [128, 2, S], BF16, s1)
                            raws = [fw.sb(f"b_raw{i}", [128, S + 4], BF16, s1) for i in range(2)]
                            dgs = [fw.sb(f"b_dg{i}", [128, 5, 128], BF16, s1) for i in range(2)]
                            sqs = [fw.sb(f"b_sq{i}", [128, 512], BF16, s1) for i in range(2)]
                            lnvs = [fw.sb(f"b_ln{i}", [128, 512], F32, s1) for i in range(2)]
                            rss = [fw.sb(f"b_rs{i}", [128, 512], F32, s1) for i in range(2)]
                            R_bv = fw.res()
                            R_raws = [fw.res(), fw.res()]; R_dgs = [fw.res(), fw.res()]; R_sqs = [fw.res(), fw.res()]
                            R_lns = [fw.res(), fw.res()]; R_rss = [fw.res(), fw.res()]
                            for i in range(2):
                                fw.op("pool", lambda e: e.memset(raws[i][:, 0:2], 0.0), writes=[R_raws[i]])
                                fw.op("pool", lambda e: e.memset(raws[i][:, S + 2:S + 4], 0.0), writes=[R_raws[i]])
                            nchunk = 0
                            wq, R_wq = load_w(win_d[l, :, P_B + pp * 1024:P_B + pp * 1024 + 512], 512)
                            wz, R_wz = load_w(win_d[l, :, P_B + pp * 1024 + 512:P_B + pp * 1024 + 1024], 512)
                            n = 0
                            for kind in range(3):
                                for hh in range(2):
                                    wsrc, R_ws, col0 = (wq, R_wq, kind * 256 + hh * 128) if kind < 2 else (wz, R_wz, hh * 128)
                                    dst, R_dst = ((bq, R_bq), (bk, R_bk), (bv, R_bv))[kind]
                                    cch = kind * 4 + 2 * pp + hh
                                    raw, R_raw, dg, R_dg = raws[nchunk % 2], R_raws[nchunk % 2], dgs[nchunk % 2], R_dgs[nchunk % 2]
                                    nchunk += 1
                                    for k in range(5):
                                        fw.ts("dve", dg[:, k, :], ident_b, cfc(f"cw_{l}", 1, cch * 5 + k), None, ALU.mult,
                                              reads=[R_cb, R_cf], writes=[R_dg])
                                    for tb in range(4):
                                        pb = n % 2
                                        n += 1
                                        proj_fm(wsrc, R_ws, col0, 128, tb, PB[pb], R_pb[pb])
                                        fw.cp("act", raw[:, 2 + tb * 512:2 + tb * 512 + 512], PB[pb][:, :],
                                              reads=[R_pb[pb]], writes=[R_raw])
                                    for tb in range(4):
                                        pb = 2 + n % 2
                                        n += 1
                                        ts_ = slice(tb * 512, tb * 512 + 512)
                                        for k in range(5):
                                            fw.mm(PB[pb][:, :], dg[:, k, :], raw[:, tb * 512 + k:tb * 512 + k + 512],
                                                  start=(k == 0), stop=(k == 4), reads=[R_dg, R_raw], writes=[R_pb[pb]])
                                        fw.act(dst[:, hh, ts_], PB[pb][:, :], AF.Silu, reads=[R_pb[pb]], writes=[R_dst])
                            calls = [(kind, hh, tb) for kind in range(2) for hh in range(2) for tb in range(4)]
                            R_l2 = {(kind, hh, tb): fw.res() for (kind, hh, tb) in calls}

                            def l2_front(i):
                                kind, hh, tb = calls[i]
                                dst, R_dst = ((bq, R_bq), (bk, R_bk))[kind]
                                ts_ = slice(tb * 512, tb * 512 + 512)
                                fw.act(sqs[i % 2][:, :], dst[:, hh, ts_], AF.Square, reads=[R_dst], writes=[R_sqs[i % 2]])
                                fw.mm(PB[4 + i % 2][:, :], ones_b, sqs[i % 2][:, :], reads=[R_sqs[i % 2], R_cb], writes=[R_pb[4 + i % 2]])

                            def l2_back(i):
                                kind, hh, tb = calls[i]
                                dst, R_dst = ((bq, R_bq), (bk, R_bk))[kind]
                                ts_ = slice(tb * 512, tb * 512 + 512)
                                pb = 4 + i % 2
                                fw.act(lnvs[i % 2][:, :], PB[pb][:, :], AF.Ln, reads=[R_pb[pb]], writes=[R_lns[i % 2]], bias=eps_col)
                                fw.act(rss[i % 2][:, :], lnvs[i % 2][:, :], AF.Exp, reads=[R_lns[i % 2]], writes=[R_rss[i % 2]], scale=-0.5)
                                fw.stt(dst[:, hh, ts_], dst[:, hh, ts_], (128.0 ** -0.5) if kind == 0 else 1.0, rss[i % 2][:, :],
                                       ALU.mult, ALU.mult, reads=[R_dst, R_rss[i % 2]], writes=[R_l2[calls[i]]])

                            l2_front(0)
                            for i in range(len(calls)):
                                if i + 1 < len(calls):
                                    l2_front(i + 1)
                                l2_back(i)
                            dmy = fw.sb("b_dmy", [128, 2], F32, s1)
                            R_dmy = fw.res()
                            fw.op("pool", lambda e: e.memset(dmy[:, 0:1], 0.0), reads=[R_l2[c_] for c_ in calls if c_[0] == 0], writes=[R_bq, R_dmy])
                            fw.op("pool", lambda e: e.memset(dmy[:, 1:2], 0.0), reads=[R_l2[c_] for c_ in calls if c_[0] == 1], writes=[R_bk, R_dmy])
                            tap("b_q", bq[:, :, :], [R_bq], dst=(tap_d["b_q"][:, 2 * pp:2 * pp + 2, :] if "b_q" in tap_d else None))
                            tap("b_k", bk[:, :, :], [R_bk], dst=(tap_d["b_k"][:, 2 * pp:2 * pp + 2, :] if "b_k" in tap_d else None))
                            tap("b_v", bv[:, :, :], [R_bv], dst=(tap_d["b_v"][:, 2 * pp:2 * pp + 2, :] if "b_v" in tap_d else None))
                            for src, R_src, dstt, R_dt in ((bk, R_bk, K_tok, R_Kt), (bv, R_bv, V_tok, R_Vt)):
                                for hh in range(2):
                                    for t4 in range(4):
                                        pb = 6 + n % 2
                                        n += 1
                                        pbv = PB[pb][:, :].bitcast(BF16)
                                        for i in range(4):
                                            tile = t4 * 4 + i
                                            fw.tr(pbv[:, i * 128:i * 128 + 128], src[:, hh, tile * 128:tile * 128 + 128], ident_b,
                                                  reads=[R_src, R_cb], writes=[R_pb[pb]])
                                        fw.cp("act", dstt[:, t4 * 4:t4 * 4 + 4, hh, :],
                                              pbv[:, 0:512].rearrange("p (t n) -> p t n", t=4), reads=[R_pb[pb]], writes=[R_dt])
                        fw.barrier()
                        obuf = fw.sb("b_ob", [128, 2, S], F32, sp_)
                        R_ob = fw.res()
                        fw.op("pool", lambda e: e.memset(obuf[:, :, :], 0.0), writes=[R_ob])
                        with contextlib.ExitStack() as s2:
                            def t4(name, dt, nres=1):
                                return fw.sb("b_" + name, [128, 4, 128], dt, s2), (fw.res() if nres == 1 else [fw.res() for _ in range(nres)])
                            KKs, R_KKs = t4("KKs", BF16)
                            GU, R_GU = t4("GU", F32, 2)
                            EG, R_EG = t4("EG", BF16)
                            Et, R_Et = t4("Et", BF16, 4)
                            X, R_X = t4("X", BF16, 4)
                            Yb = [t4(f"Y{i}", BF16) for i in range(2)]
                            Zb = [(X, R_X), t4("Z1", BF16)]
                            Pbb = [t4(f"Pb{i}", BF16) for i in range(2)]
                            Keg, R_Keg = t4("Keg", BF16, 2)
                            qkT2 = [t4(f"qkT{i}", BF16) for i in range(2)]
                            Ktil2 = [t4(f"Ktil{i}", BF16, 2) for i in range(2)]
                            upp2 = [t4("upp0", F32, 2)] * 2
                            wT2 = [t4(f"wT{i}", BF16) for i in range(2)]
                            qtT2 = [t4(f"qtT{i}", BF16, 2) for i in range(2)]
                            S32 = fw.sb("b_S32", [128, 4, 128], F32, s2)
                            Sb = fw.sb("b_Sb", [128, 4, 128], BF16, s2)
                            vnew2 = [fw.sb(f"b_vnew{i}", [128, 4, 128], BF16, s2) for i in range(2)]
                            R_S32 = [[fw.res(), fw.res()], [fw.res(), fw.res()]]; R_Sb = [fw.res(), fw.res()]
                            R_vn = [[fw.res(), fw.res()], [fw.res(), fw.res()]]
                            R_psv = [R_pb[6], R_pb[7]]; R_pss = [R_pb[6], R_pb[7]]; R_po = [R_pb[5], R_pb[5]]
                            fw.op("pool", lambda e: e.memset(S32[:, :, :], 0.0), writes=R_S32)
                            fw.op("pool", lambda e: e.memset(Sb[:, :, :], 0.0), writes=R_Sb)
                            fw.op("pool", lambda e: e.memset(vnew2[0][:, :, :], 0.0), writes=R_vn)
                            fw.op("pool", lambda e: e.memset(vnew2[1][:, :, :], 0.0), writes=R_vn)
                            v4 = lambda i: PB[i][:, :].rearrange("p (u n) -> p u n", u=4)
                            strict_b4 = cbm("strict").unsqueeze(1).broadcast_to([128, 4, 128])
                            ident_b4 = ident_b.unsqueeze(1).broadcast_to([128, 4, 128])
                            ucol = lambda d, hh: d * 4 + 2 * pp + hh

                            def prep(n):
                                Td = (n, 15 - n)
                                tsl = lambda d: slice(Td[d] * 128, Td[d] * 128 + 128)
                                qkT, R_qkT = qkT2[n % 2]
                                Ktil, R_Ktil = Ktil2[n % 2]
                                upp, R_upp = upp2[n % 2]
                                wT, R_wT = wT2[n % 2]
                                qtT, R_qtT = qtT2[n % 2]
                                for d in range(2):
                                    for hh in range(2):
                                        u = d * 2 + hh
                                        fw.mm(PB[0][:, u * 128:u * 128 + 128], bk[:, hh, tsl(d)], bk[:, hh, tsl(d)],
                                              reads=[R_bk], writes=[R_pb[0]])
                                        fw.mm(PB[1][:, u * 128:u * 128 + 128], bk[:, hh, tsl(d)], bq[:, hh, tsl(d)],
                                              reads=[R_bk, R_bq], writes=[R_pb[1]])
                                for d in range(2):
                                    uu = cfc("uuf" if d == 0 else "uub", 128).unsqueeze(1).broadcast_to([128, 2, 128])
                                    gb_ = g_tok[:, Td[d], ucol(d, 0):ucol(d, 0) + 2].unsqueeze(2).broadcast_to([128, 2, 128])
                                    fw.tt("pool", GU[:, 2 * d:2 * d + 2, :], uu, gb_, ALU.mult, reads=[R_cf, R_ts], writes=[R_GU[d]])
                                yield
                                fw.tt("dve", KKs[:, :, :], v4(0), strict_b4, ALU.mult, reads=[R_pb[0], R_cb], writes=[R_KKs])
                                fw.mm(PB[4][:, :], ones_f, GU[:, :, :].rearrange("p u n -> p (u n)"), start=True, stop=True,
                                      reads=[R_GU, R_cf], writes=[R_pb[4]])
                                fw.mm(PB[2][:, :], ones_f, GU[:, :, :].rearrange("p u n -> p (u n)"), start=True, stop=False,
                                      reads=[R_GU, R_cf], writes=[R_pb[2]])
                                fw.mm(PB[2][:, :], ident_b, cbm("gmask", 512), start=False, stop=True, reads=[R_cb], writes=[R_pb[2]])
                                yield
                                fw.act(EG[:, :, :], v4(4), AF.Exp, reads=[R_pb[4]], writes=[R_EG])
                                yield
                                for d in range(2):
                                    for hh in range(2):
                                        u = d * 2 + hh
                                        c_ = ucol(d, hh)
                                        fw.act(Et[:, u, :], PB[2][:, u * 128:u * 128 + 128], AF.Exp, reads=[R_pb[2], R_ts],
                                               writes=[R_Et[u]], bias=ngc[:, Td[d], c_:c_ + 1])
                                for d in range(2):
                                    c0 = ucol(d, 0)
                                    eb = egc[:, Td[d], c0:c0 + 2].unsqueeze(2).broadcast_to([128, 2, 128])
                                    kb_ = kdec[:, Td[d], c0:c0 + 2].unsqueeze(2).broadcast_to([128, 2, 128])
                                    fw.tt("pool", Keg[:, 2 * d:2 * d + 2, :], K_tok[:, Td[d], :, :], eb, ALU.mult, reads=[R_Kt, R_ts], writes=[R_Keg[d]])
                                    fw.tt("pool", Ktil[:, 2 * d:2 * d + 2, :], K_tok[:, Td[d], :, :], kb_, ALU.mult, reads=[R_Kt, R_ts], writes=[R_Ktil[d]])
                                    fw.tt("pool", qtT[:, 2 * d:2 * d + 2, :], bq[:, :, tsl(d)], EG[:, 2 * d:2 * d + 2, :], ALU.mult,
                                          reads=[R_bq, R_EG], writes=[R_qtT[d]])
                                yield
                                for d in range(2):
                                    for hh in range(2):
                                        u = d * 2 + hh
                                        c_ = ucol(d, hh)
                                        fw.stt(X[:, u, :], Et[:, u, :], beta[:, Td[d], c_:c_ + 1], KKs[:, u, :], ALU.mult, ALU.mult,
                                               reads=[R_Et[u], R_ts, R_KKs], writes=[R_X[u]])
                                fw.tt("dve", qkT[:, :, :], v4(1), Et[:, :, :], ALU.mult, reads=[R_pb[1], R_Et], writes=[R_qkT])
                                yield
                                pbv3 = PB[3][:, :].bitcast(BF16)
                                for u in range(4):
                                    fw.tr(pbv3[:, u * 128:u * 128 + 128], X[:, u, :], ident_b, reads=[R_X[u], R_cb], writes=[R_pb[3]])
                                (Yc, R_Yc), (Zc, R_Zc) = Yb[0], (X, R_X)
                                Pc, R_Pc = Pbb[0]
                                fw.tt("dve", Pc[:, :, :], ident_b4, X[:, :, :], ALU.subtract, reads=[R_cb, R_X], writes=[R_Pc])
                                yield
                                fw.cp("act", Yc[:, :, :], pbv3[:, 0:512].rearrange("p (u n) -> p u n", u=4), reads=[R_pb[3]], writes=[R_Yc])
                                yield
                                for s_ in range(1, 6):
                                    Yn, R_Yn = Yb[s_ % 2]
                                    Zn, R_Zn = Zb[s_ % 2]
                                    Pn, R_Pn = Pbb[s_ % 2]
                                    for u in range(4):
                                        fw.mm(PB[3][:, u * 128:u * 128 + 128], Zc[:, u, :], Yc[:, u, :], reads=[R_Zc, R_Yc], writes=[R_pb[3]])
                                    if s_ <= 4:
                                        for u in range(4):
                                            fw.mm(PB[4][:, u * 128:u * 128 + 128], Yc[:, u, :], Zc[:, u, :], reads=[R_Zc, R_Yc], writes=[R_pb[4]])
                                    yield
                                    fw.cp("act", Yn[:, :, :], v4(3), reads=[R_pb[3]], writes=[R_Yn])
                                    if s_ <= 4:
                                        fw.cp("dve", Zn[:, :, :], v4(4), reads=[R_pb[4]], writes=[R_Zn])
                                    yield
                                    fw.mm(PB[0][:, :], ident_b, Pc[:, :, :].rearrange("p u n -> p (u n)"), start=True, stop=False,
                                          reads=[R_cb, R_Pc], writes=[R_pb[0]])
                                    for u in range(4):
                                        fw.mm(PB[0][:, u * 128:u * 128 + 128], Yn[:, u, :], Pc[:, u, :], start=False, stop=(u == 3),
                                              reads=[R_Yn, R_Pc], writes=[R_pb[0]])
                                    yield
                                    fw.cp("act" if s_ % 2 else "dve", Pn[:, :, :], v4(0), reads=[R_pb[0]], writes=[R_Pn])
                                    (Yc, R_Yc), (Zc, R_Zc), (Pc, R_Pc) = (Yn, R_Yn), (Zn, R_Zn), (Pn, R_Pn)
                                    yield
                                for d in range(2):
                                    for hh in range(2):
                                        u = d * 2 + hh
                                        fw.mm(PB[0][:, u * 128:u * 128 + 128], Pc[:, u, :], V_tok[:, Td[d], hh, :], reads=[R_Pc, R_Vt], writes=[R_pb[0]])
                                        fw.mm(PB[1][:, u * 128:u * 128 + 128], Keg[:, u, :], Pc[:, u, :], reads=[R_Pc, R_Keg[d]], writes=[R_pb[1]])
                                yield
                                for d in range(2):
                                    c0 = ucol(d, 0)
                                    bb_ = beta[:, Td[d], c0:c0 + 2].unsqueeze(2).broadcast_to([128, 2, 128])
                                    fw.tt("dve", upp[:, 2 * d:2 * d + 2, :], v4(0)[:, 2 * d:2 * d + 2, :], bb_, ALU.mult,
                                          reads=[R_pb[0], R_ts], writes=[R_upp[d]])
                                fw.cp("act", wT[:, :, :], v4(1), reads=[R_pb[1]], writes=[R_wT])
                                yield

                            def scan(n):
                                Td = (n, 15 - n)
                                tsl = lambda d: slice(Td[d] * 128, Td[d] * 128 + 128)
                                qkT, R_qkT = qkT2[n % 2]
                                Ktil, R_Ktil = Ktil2[n % 2]
                                upp, R_upp = upp2[n % 2]
                                wT, R_wT = wT2[n % 2]
                                qtT, R_qtT = qtT2[n % 2]
                                for sub in range(2):
                                    info = []
                                    for d in range(2):
                                        cc = sub if d == 0 else 1 - sub
                                        info.append((d, cc, slice(cc * 64, cc * 64 + 64)))
                                    for (d, cc, rows) in info:
                                        for hh in range(2):
                                            u = d * 2 + hh
                                            fw.mm(PB[6 + d][:, hh * 128:hh * 128 + 128], wT[:, u, :], Sb[:, u, :], reads=[R_wT, R_Sb[d]], writes=[R_psv[d]])
                                    yield
                                    for (d, cc, rows) in info:
                                        for hh in range(2):
                                            u = d * 2 + hh
                                            c_ = ucol(d, hh)
                                            fw.stt(vnew2[cc][rows, u, :], PB[6 + d][rows, hh * 128:hh * 128 + 128], nbeta[rows, Td[d], c_:c_ + 1],
                                                   upp[rows, u, :], ALU.mult, ALU.add, reads=[R_psv[d], R_ts, R_upp[d]], writes=[R_vn[d][hh]])
                                    yield
                                    for (d, cc, rows) in info:
                                        for hh in range(2):
                                            u = d * 2 + hh
                                            fw.mm(PB[6 + d][:, 256 + hh * 128:256 + hh * 128 + 128], Ktil[:, u, :], vnew2[cc][:, u, :],
                                                  reads=[R_Ktil[d], R_vn[d][hh]], writes=[R_pss[d]])
                                        for hh in range(2):
                                            u = d * 2 + hh
                                            oc_ = u * 128 + cc * 64
                                            fw.mm(PB[5][:, oc_:oc_ + 64], Sb[:, u, :], qtT[:, u, rows], start=True, stop=False,
                                                  reads=[R_Sb[d], R_qtT[d]], writes=[R_po[d]])
                                            fw.mm(PB[5][:, oc_:oc_ + 64], vnew2[cc][:, u, :], qkT[:, u, rows], start=False, stop=True,
                                                  reads=[R_vn[d][hh], R_qkT], writes=[R_po[d]])
                                    yield
                                    for (d, cc, rows) in info:
                                        for hh in range(2):
                                            u = d * 2 + hh
                                            c_ = ucol(d, hh)
                                            fw.stt(S32[:, u, :], S32[:, u, :], gam[:, Td[d], cc, c_:c_ + 1], PB[6 + d][:, 256 + hh * 128:256 + hh * 128 + 128],
                                                   ALU.mult, ALU.add, reads=[R_S32[d][hh], R_ts, R_pss[d]], writes=[R_S32[d][hh]])
                                    yield
                                    for (d, cc, rows) in info:
                                        fw.cp("pool", Sb[:, 2 * d:2 * d + 2, :], S32[:, 2 * d:2 * d + 2, :], reads=[R_S32[d]], writes=[R_Sb[d]])
                                    yield
                                for d in range(2):
                                    fw.tt("dve", obuf[:, :, tsl(d)], obuf[:, :, tsl(d)], v4(5)[:, 2 * d:2 * d + 2, :], ALU.add,
                                          reads=[R_ob, R_po[d]], writes=[R_ob])
                                yield

                            def run_interleaved(gens, weights):
                                gens = [g for g in gens if g is not None]
                                alive = list(range(len(gens)))
                                while alive:
                                    for gi in list(alive):
                                        for _ in range(weights[gi]):
                                            try:
                                                next(gens[gi])
                                            except StopIteration:
                                                alive.remove(gi)
                                                break

                            run_interleaved([prep(0)], [1])
                            for n in range(16):
                                run_interleaved([prep(n + 1) if n < 15 else None, scan(n)], [3, 1])
                        fw.barrier()
                        tap("b_o", obuf[:, :, :], [R_ob], dst=(tap_d["b_o"][:, 2 * pp:2 * pp + 2, :] if "b_o" in tap_d else None))
                        with contextlib.ExitStack() as s3:
                            oT = fw.sb("b_oT", [128, 2, S], BF16, s3)
                            R_o = [fw.res(), fw.res()]
                            t13_, R_t13_ = fw.sb("b_t13", [128, 512], F32, s3), fw.res()
                            st3 = [(None, None, fw.sb(f"b_sq3{i}", [128, 512], BF16, s3), fw.res(),
                                    fw.sb(f"b_ln3{i}", [128, 512], F32, s3), fw.res(), fw.sb(f"b_rs3{i}", [128, 512], F32, s3), fw.res(),
                                    t13_, R_t13_) for i in range(2)]
                            wz, R_wz = load_w(win_d[l, :, P_B + pp * 1024 + 768:P_B + pp * 1024 + 1024], 256)
                            zsT = fw.sb("b_zsT", [128, 2, S], BF16, s3)
                            R_zsT = fw.res()
                            n = 0
                            for hh in range(2):
                                for tb in range(4):
                                    ts_ = slice(tb * 512, tb * 512 + 512)
                                    pz = n % 2
                                    n += 1
                                    proj_fm(wz, R_wz, hh * 128, 128, tb, PB[pz], R_pb[pz])
                                    fw.act(zsT[:, hh, ts_], PB[pz][:, :], AF.Silu, reads=[R_pb[pz]], writes=[R_zsT])
                            calls3 = [(hh, tb) for hh in range(2) for tb in range(4)]

                            def n3_front(i):
                                hh, tb = calls3[i]
                                ts_ = slice(tb * 512, tb * 512 + 512)
                                zs, R_zs, sq, R_sq, lnv, R_ln, rs, R_rs, t1, R_t1 = st3[i % 2]
                                fw.act(sq[:, :], obuf[:, hh, ts_], AF.Square, reads=[R_ob], writes=[R_sq])
                                fw.mm(PB[2 + i % 2][:, :], ones_b, sq[:, :], reads=[R_sq, R_cb], writes=[R_pb[2 + i % 2]])

                            def n3_back(i):
                                hh, tb = calls3[i]
                                ts_ = slice(tb * 512, tb * 512 + 512)
                                zs, R_zs, sq, R_sq, lnv, R_ln, rs, R_rs, t1, R_t1 = st3[i % 2]
                                pn = 2 + i % 2
                                fw.act(lnv[:, :], PB[pn][:, :], AF.Ln, reads=[R_pb[pn]], writes=[R_ln], scale=1.0 / 128, bias=eps_col)
                                fw.act(rs[:, :], lnv[:, :], AF.Exp, reads=[R_ln], writes=[R_rs], scale=-0.5)
                                fw.stt(t1[:, :], obuf[:, hh, ts_], cfc(f"onb_{l}"), rs[:, :], ALU.mult, ALU.mult,
                                       reads=[R_ob, R_cf, R_rs], writes=[R_t1])
                                fw.tt("dve", oT[:, hh, ts_], t1[:, :], zsT[:, hh, ts_], ALU.mult, reads=[R_t1, R_zsT], writes=[R_o[hh]])

                            n3_front(0)
                            for i in range(len(calls3)):
                                if i + 1 < len(calls3):
                                    n3_front(i + 1)
                                n3_back(i)
                            tap("b_oT", oT[:, :, :], R_o, dst=(tap_d["b_oT"][:, 2 * pp:2 * pp + 2, :] if "b_oT" in tap_d else None))
                            wout_update(l, [(256 + (2 * pp + hh) * 128, (lambda tb, hh=hh: oT[:, hh, tb * 512:tb * 512 + 512]), R_o[hh])
                                            for hh in range(2)], s3)
                        fw.barrier()
            fw.barrier()

        def ffn(l):
            with contextlib.ExitStack() as scr0:
                rmsnorm_fm(l, "n2", scr0)
            fw.barrier()
            with contextlib.ExitStack() as scr:
                NH = 12
                actT = fw.sb("f_act", [128, NH, S], BF16, scr)
                sg = [fw.sb(f"f_sg{i}", [128, 512], F32, scr) for i in range(2)]
                R_sg = [fw.res() for _ in range(2)]
                n = 0
                for (f0, nf) in ((0, 12), (12, 10)):
                    R_a = [[fw.res() for _ in range(4)] for _ in range(nf)]
                    for g in range(nf // 2):
                        gg = f0 // 2 + g
                        wv, R_wv = load_w(wgu_d[l, :, gg * 512:gg * 512 + 512], 512)
                        for fi in range(2):
                            f = g * 2 + fi
                            for tb in range(4):
                                pg = (n % 2) * 2
                                pu = pg + 1
                                proj_fm(wv, R_wv, fi * 256, 128, tb, PB[pg], R_pb[pg])
                                proj_fm(wv, R_wv, fi * 256 + 128, 128, tb, PB[pu], R_pb[pu])
                                si = n % 2
                                fw.act(sg[si][:, :], PB[pg][:, :], AF.Silu, reads=[R_pb[pg]], writes=[R_sg[si]])
                                fw.tt("dve", actT[:, f, tb * 512:tb * 512 + 512], sg[si][:, :], PB[pu][:, :], ALU.mult,
                                      reads=[R_sg[si], R_pb[pu]], writes=[R_a[f][tb]])
                                n += 1
                    for oc in range(8):
                        wv, R_wv = load_w(wdn_d[l, f0 * 128:(f0 + nf) * 128, oc * 128:oc * 128 + 128], 128, kchunks=nf)
                        for tb in range(4):
                            pb = 4 + (oc * 4 + tb) % 4
                            ts_ = slice(tb * 512, tb * 512 + 512)
                            for f in range(nf):
                                fw.mm(PB[pb][:, :], wv[:, f, :], actT[:, f, ts_], start=(f == 0), stop=(f == nf - 1),
                                      reads=[R_wv, R_a[f][tb]], writes=[R_pb[pb]])
                            fw.tt("dve", xT[:, oc, ts_], xT[:, oc, ts_], PB[pb][:, :], ALU.add,
                                  reads=[R_x[oc][tb], R_pb[pb]], writes=[R_x[oc][tb]])
                    fw.barrier()
            fw.barrier()

        eps_t = fw.sb("eps_t", [128, 1], F32)
        R_eps = fw.res()
        fw.op("pool", lambda e: e.memset(eps_t[:, :], EPS), writes=[R_eps])
        eps_col = eps_t[:, 0:1]
        fw.barrier()

        for l in range(nl):
            with contextlib.ExitStack() as scr:
                rmsnorm_fm(l, "n1", scr)
            if l == 0:
                tap("hT", hT[:, :, :], R_h)
            fw.barrier()
            if "skipA" not in taps:
                mixer_A(l)
            if "skipC" not in taps:
                mixer_C(l)
            if "skipB" not in taps:
                mixer_B(l)
            if l == 0:
                tap("xmid", xT[:, :, :], [r for rr in R_x for r in rr])
            ffn(l)

        yv = yT_d.rearrange("(c p) t -> p c t", p=128)
        for c in range(8):
            fw.dma("sp", yv[:, c, :], xT[:, c, :], reads=R_x[c], writes=[R_out], sem=s_st)
        fw.wait_all("sp", [R_out, R_tap])
        print("ninst", fw.ninst)
    return nc


TAP_SHAPES = {
    "hT": ((128, 8, S), BF16), "a_qT": ((128, 2, S), BF16), "a_oT": ((128, 2, S), BF16), "c_qT": ((128, 2, S), BF16),
    "c_oT": ((128, 2, S), BF16), "xmid": ((128, 8, S), F32), "f_act": ((128, NFF, S), BF16),
    "b_g": ((128, 16, 8), F32), "b_beta": ((128, 16, 8), F32), "b_gc": ((128, 16, 8), F32),
    "b_q": ((128, 4, S), BF16), "b_k": ((128, 4, S), BF16), "b_v": ((128, 4, S), BF16), "b_o": ((128, 4, S), F32),
    "b_oT": ((128, 4, S), BF16),
}


_PROG_CACHE = {}


def _prep_inputs(inputs, nl):
    perm_in = _win_perm()
    perm_out = _wout_perm()
    perm_gu = _wgu_perm()
    w_in = np.ascontiguousarray(inputs["w_in"][:nl][:, :, perm_in])
    w_out = np.ascontiguousarray(inputs["w_out"][:nl][:, perm_out, :])
    w_gu = np.ascontiguousarray(inputs["w_gate_up"][:nl][:, :, perm_gu])
    w_dn = np.ascontiguousarray(inputs["w_down"][:nl])
    cf, cb16, ropeA, ropeC, _ = _build_consts(inputs, nl)
    shared = {"w_in": w_in, "w_out": w_out, "w_gu": w_gu, "w_dn": w_dn, "cf": cf, "cb": cb16,
              "ropeA": ropeA, "ropeC": ropeC}
    return shared


def kernel(**inputs):
    inputs = {k: np.asarray(v) for k, v in inputs.items()}
    nl = L_FULL
    shared = _prep_inputs(inputs, nl)
    x = inputs["x"]
    in_maps = []
    for b in range(8):
        m = dict(shared)
        m["xT"] = np.ascontiguousarray(x[b].T)
        in_maps.append(m)
    if nl not in _PROG_CACHE:
        _PROG_CACHE[nl] = build_program(nl)
    res = run_bass_kernel_spmd(_PROG_CACHE[nl], in_maps, core_ids=list(range(8)))
    out = np.stack([np.ascontiguousarray(r["yT"].T) for r in res.results], axis=0)
    return out.astype(np.float32)
```
